# Optimizing a Trainium2 kernel written in Bass

```python
import jax, jax.numpy as jnp
from jax import lax
import numpy as np

D_MODEL = 1024
BATCH = 2
SEQ = 8192
DEPTH = 4

N_MIXERS = 3
NORM_EPS = 1e-6
D_FF = 4 * D_MODEL
A_CHUNK = 128
A_WIDTH = 2 * D_MODEL
A_GROUP_DIM = 128
A_GROUPS = A_WIDTH // A_GROUP_DIM
B_EXPAND = 128
B_HEADS = D_MODEL // B_EXPAND
B_CHUNK = 64
C_INNER = 2 * D_MODEL
C_HEADS = 4
C_DH = C_INNER // C_HEADS
C_CONV = 4
C_QKV_BLOCK = 4
C_CHUNK = 64

kernel_name = "hybrid_gmlp_hgrn2_mlstm_trunk"


def _rmsnorm(x, g):
    xf = x.astype(jnp.float32)
    y = xf * lax.rsqrt(jnp.mean(xf * xf, axis=-1, keepdims=True) + NORM_EPS)
    return (y * g.astype(jnp.float32)).astype(x.dtype)


def _layernorm(x, g, b=None):
    xf = x.astype(jnp.float32)
    mu = jnp.mean(xf, axis=-1, keepdims=True)
    xc = xf - mu
    y = xc * lax.rsqrt(jnp.mean(xc * xc, axis=-1, keepdims=True) + NORM_EPS) * g.astype(jnp.float32)
    if b is not None:
        y = y + b.astype(jnp.float32)
    return y.astype(x.dtype)


def _to_chunks(t, n_heads, chunk):
    b, s, hd = t.shape
    return t.reshape(b, s // chunk, chunk, n_heads, hd // n_heads).transpose(1, 0, 3, 2, 4)


def _from_chunks(t):
    nc, b, h, l, d = t.shape
    return t.transpose(1, 0, 3, 2, 4).reshape(b, nc * l, h * d)


def _sq_relu_mlp(h, w1, w2):
    return jnp.square(jax.nn.relu(h @ w1)) @ w2


def _gmlp_mixer(h, w_in, ln_g, ln_b, ws, bs, w_out):
    b_, s_, _ = h.shape
    uv = jax.nn.gelu(h @ w_in, approximate=False)
    u, v = jnp.split(uv, 2, axis=-1)
    v = _layernorm(v, ln_g, ln_b)
    v = v.reshape(b_, s_ // A_CHUNK, A_CHUNK, A_GROUPS, A_GROUP_DIM)
    causal = jnp.tril(jnp.ones((A_CHUNK, A_CHUNK), dtype=bool))
    w_causal = jnp.where(causal[None], ws, jnp.zeros((), ws.dtype)).astype(v.dtype)
    mixed = jnp.einsum('gts,bcsgd->bctgd', w_causal, v) + jnp.transpose(bs).astype(v.dtype)[None, None, :, :, None]
    y = u * mixed.reshape(b_, s_, A_WIDTH)
    return y @ w_out


def _hgrn2_mixer(h, w_in, lower_bound, norm_g, w_out):
    b_, s_, _ = h.shape
    q, f, i, g = jnp.split(h @ w_in, 4, axis=-1)
    q = jax.nn.silu(q).astype(jnp.float32)
    i = i.astype(jnp.float32)
    logf = jnp.logaddexp(jnp.log(lower_bound), jnp.log1p(-lower_bound) + jax.nn.log_sigmoid(f.astype(jnp.float32)))
    k = -jnp.expm1(logf)
    qc = _to_chunks(q, B_HEADS, B_CHUNK)
    kc = _to_chunks(k, B_HEADS, B_CHUNK)
    ic = _to_chunks(i, B_HEADS, B_CHUNK)
    lfc = _to_chunks(logf, B_HEADS, B_CHUNK)
    causal = jnp.tril(jnp.ones((B_CHUNK, B_CHUNK), dtype=bool))[:, :, None]

    def step(state, xs):
        qb, kb, ib, lfb = xs
        bcum = jnp.cumsum(lfb, axis=2)
        diff = bcum[:, :, :, None, :] - bcum[:, :, None, :, :]
        decay = jnp.exp(jnp.where(causal, diff, -jnp.inf))
        scores = jnp.einsum('bhtd,bhsd,bhtsd->bhts', qb, kb, decay)
        o = jnp.einsum('bhts,bhsv->bhtv', scores, ib) + jnp.einsum('bhtd,bhdv->bhtv', qb * jnp.exp(bcum), state)
        b_last = bcum[:, :, -1:, :]
        new_state = jnp.exp(b_last[:, :, 0, :])[..., None] * state + jnp.einsum('bhsd,bhsv->bhdv', kb * jnp.exp(b_last - bcum), ib)
        return new_state, o

    state0 = jnp.zeros((b_, B_HEADS, B_EXPAND, D_MODEL // B_HEADS), jnp.float32)
    _, oc = lax.scan(step, state0, (qc, kc, ic, lfc))
    o = _from_chunks(oc).reshape(b_, s_, B_HEADS, D_MODEL // B_HEADS)
    o = _rmsnorm(o, norm_g.reshape(B_HEADS, D_MODEL // B_HEADS)).reshape(b_, s_, D_MODEL)
    o = o * jax.nn.silu(g.astype(jnp.float32))
    return o.astype(h.dtype) @ w_out


def _headwise(t, w):
    b_, s_, c = t.shape
    nb, bsz, _ = w.shape
    return jnp.einsum('bsnd,nde->bsne', t.reshape(b_, s_, nb, bsz), w).reshape(b_, s_, c)


def _mlstm_mixer(h, w_in, conv_w, conv_b, wq, wk, wv, w_gate, b_gate, skip, norm_g, w_out):
    b_, s_, _ = h.shape
    xm, z = jnp.split(h @ w_in, 2, axis=-1)
    conv = lax.conv_general_dilated(
        xm, conv_w.astype(xm.dtype)[:, None, :], window_strides=(1,), padding=[(C_CONV - 1, 0)],
        dimension_numbers=('NWC', 'WIO', 'NWC'), feature_group_count=C_INNER) + conv_b
    ca = jax.nn.silu(conv)
    q = _headwise(ca, wq)
    k = _headwise(ca, wk)
    v = _headwise(xm, wv)
    gates = (jnp.concatenate([q, k, v], axis=-1) @ w_gate + b_gate).astype(jnp.float32)
    logi = gates[..., :C_HEADS]
    logf = jax.nn.log_sigmoid(gates[..., C_HEADS:])
    qc = _to_chunks(q.astype(jnp.float32), C_HEADS, C_CHUNK)
    kc = _to_chunks(k.astype(jnp.float32) * (C_DH ** -0.5), C_HEADS, C_CHUNK)
    vc = _to_chunks(v.astype(jnp.float32), C_HEADS, C_CHUNK)
    lic = _to_chunks(logi, C_HEADS, C_CHUNK)[..., 0]
    lfc = _to_chunks(logf, C_HEADS, C_CHUNK)[..., 0]
    causal = jnp.tril(jnp.ones((C_CHUNK, C_CHUNK), dtype=bool))

    def step(carry, xs):
        c_st, n_st, m_st = carry
        qb, kb, vb, li, lf = xs
        a = jnp.cumsum(lf, axis=-1)
        dmat = jnp.where(causal, a[..., :, None] - a[..., None, :] + li[..., None, :], -jnp.inf)
        inter = a + m_st[..., None]
        m_t = jnp.maximum(jnp.max(dmat, axis=-1), inter)
        w_intra = jnp.exp(dmat - m_t[..., None])
        w_inter = jnp.exp(inter - m_t)
        qk = jnp.einsum('bhtd,bhsd->bhts', qb, kb) * w_intra
        num = jnp.einsum('bhts,bhsv->bhtv', qk, vb) + w_inter[..., None] * jnp.einsum('bhtd,bhdv->bhtv', qb, c_st)
        den = jnp.sum(qk, axis=-1) + w_inter * jnp.einsum('bhtd,bhd->bht', qb, n_st)
        hb = num / jnp.maximum(jnp.abs(den), jnp.exp(-m_t))[..., None]
        a_last = a[..., -1]
        g_s = a_last[..., None] - a + li
        m_new = jnp.maximum(a_last + m_st, jnp.max(g_s, axis=-1))
        w_s = jnp.exp(g_s - m_new[..., None])
        dec = jnp.exp(a_last + m_st - m_new)
        c_new = dec[..., None, None] * c_st + jnp.einsum('bhs,bhsd,bhsv->bhdv', w_s, kb, vb)
        n_new = dec[..., None] * n_st + jnp.einsum('bhs,bhsd->bhd', w_s, kb)
        return (c_new, n_new, m_new), hb

    carry0 = (jnp.zeros((b_, C_HEADS, C_DH, C_DH), jnp.float32),
              jnp.zeros((b_, C_HEADS, C_DH), jnp.float32),
              jnp.full((b_, C_HEADS), -jnp.inf, jnp.float32))
    _, hc = lax.scan(step, carry0, (qc, kc, vc, lic, lfc))
    hh = _from_chunks(hc).reshape(b_, s_, C_HEADS, C_DH)
    hh = _layernorm(hh, norm_g.reshape(C_HEADS, C_DH)).reshape(b_, s_, C_INNER)
    hh = (hh + skip.astype(jnp.float32) * ca.astype(jnp.float32)) * jax.nn.silu(z.astype(jnp.float32))
    return hh.astype(h.dtype) @ w_out


def setup_inputs(seed: int = 0) -> dict:
    key = jax.random.key(seed)
    ks = iter(jax.random.split(key, 40))
    nrm = lambda shape, scale: jax.random.normal(next(ks), shape, jnp.float32) * scale
    kinds = [i % N_MIXERS for i in range(DEPTH)]
    n_a, n_b, n_c = kinds.count(0), kinds.count(1), kinds.count(2)
    nb = C_INNER // C_QKV_BLOCK
    b_gate = jnp.concatenate([
        nrm((n_c, C_HEADS), 0.1),
        jnp.broadcast_to(jnp.linspace(3.0, 6.0, C_HEADS, dtype=jnp.float32), (n_c, C_HEADS)) + nrm((n_c, C_HEADS), 0.02)], axis=-1)
    return {
        "x": nrm((BATCH, SEQ, D_MODEL), 1.0),
        "mix_norm_g": 1.0 + nrm((DEPTH, D_MODEL), 0.02),
        "ffn_norm_g": 1.0 + nrm((DEPTH, D_MODEL), 0.02),
        "final_norm_g": 1.0 + nrm((D_MODEL,), 0.02),
        "ffn_w1": nrm((DEPTH, D_MODEL, D_FF), D_MODEL ** -0.5),
        "ffn_w2": nrm((DEPTH, D_FF, D_MODEL), 0.5 * D_FF ** -0.5),
        "hgrn_lb_logits": nrm((DEPTH, D_MODEL), 0.1),
        "a_w_in": nrm((n_a, D_MODEL, 2 * A_WIDTH), D_MODEL ** -0.5),
        "a_ln_g": 1.0 + nrm((n_a, A_WIDTH), 0.02),
        "a_ln_b": nrm((n_a, A_WIDTH), 0.02),
        "a_ws": nrm((n_a, A_GROUPS, A_CHUNK, A_CHUNK), A_CHUNK ** -0.5),
        "a_bs": 1.0 + nrm((n_a, A_GROUPS, A_CHUNK), 0.02),
        "a_w_out": nrm((n_a, A_WIDTH, D_MODEL), A_WIDTH ** -0.5),
        "b_w_in": nrm((n_b, D_MODEL, 4 * D_MODEL), D_MODEL ** -0.5),
        "b_norm_g": 1.0 + nrm((n_b, D_MODEL), 0.02),
        "b_w_out": nrm((n_b, D_MODEL, D_MODEL), D_MODEL ** -0.5),
        "c_w_in": nrm((n_c, D_MODEL, 2 * C_INNER), D_MODEL ** -0.5),
        "c_conv_w": nrm((n_c, C_CONV, C_INNER), C_CONV ** -0.5),
        "c_conv_b": nrm((n_c, C_INNER), 0.02),
        "c_wq": nrm((n_c, nb, C_QKV_BLOCK, C_QKV_BLOCK), C_QKV_BLOCK ** -0.5),
        "c_wk": nrm((n_c, nb, C_QKV_BLOCK, C_QKV_BLOCK), C_QKV_BLOCK ** -0.5),
        "c_wv": nrm((n_c, nb, C_QKV_BLOCK, C_QKV_BLOCK), C_QKV_BLOCK ** -0.5),
        "c_w_gate": nrm((n_c, 3 * C_INNER, 2 * C_HEADS), 0.5 * (3 * C_INNER) ** -0.5),
        "c_b_gate": b_gate,
        "c_skip": 1.0 + nrm((n_c, C_INNER), 0.02),
        "c_norm_g": 1.0 + nrm((n_c, C_INNER), 0.02),
        "c_w_out": nrm((n_c, C_INNER, D_MODEL), C_INNER ** -0.5),
    }


def reference(x, mix_norm_g, ffn_norm_g, final_norm_g, ffn_w1, ffn_w2, hgrn_lb_logits,
              a_w_in, a_ln_g, a_ln_b, a_ws, a_bs, a_w_out,
              b_w_in, b_norm_g, b_w_out,
              c_w_in, c_conv_w, c_conv_b, c_wq, c_wk, c_wv, c_w_gate, c_b_gate, c_skip, c_norm_g, c_w_out):
    lb_sm = jax.nn.softmax(hgrn_lb_logits.astype(jnp.float32), axis=0)
    lower_bounds = jnp.cumsum(lb_sm, axis=0) - lb_sm[0]
    counts = [0, 0, 0]
    h = x
    for layer in range(DEPTH):
        kind = layer % N_MIXERS
        j = counts[kind]
        counts[kind] += 1
        hn = _rmsnorm(h, mix_norm_g[layer])
        if kind == 0:
            y = _gmlp_mixer(hn, a_w_in[j], a_ln_g[j], a_ln_b[j], a_ws[j], a_bs[j], a_w_out[j])
        elif kind == 1:
            y = _hgrn2_mixer(hn, b_w_in[j], lower_bounds[layer], b_norm_g[j], b_w_out[j])
        else:
            y = _mlstm_mixer(hn, c_w_in[j], c_conv_w[j], c_conv_b[j], c_wq[j], c_wk[j], c_wv[j],
                             c_w_gate[j], c_b_gate[j], c_skip[j], c_norm_g[j], c_w_out[j])
        h = h + y
        h = h + _sq_relu_mlp(_rmsnorm(h, ffn_norm_g[layer]), ffn_w1[layer], ffn_w2[layer])
    return _rmsnorm(h, final_norm_g)
```

```python
import numpy as np
from contextlib import ExitStack
import concourse.bass as bass
import concourse.mybir as mybir
from concourse.bass_utils import run_bass_kernel_spmd

F32 = mybir.dt.float32
BF16 = mybir.dt.bfloat16
AF = mybir.ActivationFunctionType
ALU = mybir.AluOpType
ENGS = ("pe", "act", "dve", "pool", "sp")
EPS = 1e-6
TOK = 2048
NT = 4
TW = 512
NCH = 16


class Res:
    __slots__ = ("w", "r", "name", "dsem")

    def __init__(self, name=""):
        self.w = {}
        self.r = {}
        self.name = name
        self.dsem = None


class Prog:
    def __init__(self, nc):
        self.nc = nc
        self.dry = False
        self.streams = {e: [] for e in ENGS}
        self.semnames = []
        self.engsem = {e: self.newsem("p_" + e) for e in ENGS}
        self.cnt = {e: 0 for e in ENGS}
        self.seen = {e: {} for e in ENGS}
        self.dmatot = {}
        self.out_events = {}

    def newsem(self, name):
        name = "%s_%d" % (name, len(self.semnames))
        self.semnames.append(name)
        return len(self.semnames) - 1

    def _waits(self, eng, reads, writes):
        need = {}
        for r in reads:
            for s, v in r.w.items():
                if need.get(s, 0) < v:
                    need[s] = v
        for w in writes:
            for s, v in w.w.items():
                if need.get(s, 0) < v:
                    need[s] = v
            for s, v in w.r.items():
                if need.get(s, 0) < v:
                    need[s] = v
        seen = self.seen[eng]
        own = self.engsem[eng]
        waits = []
        for s, v in need.items():
            if s == own:
                if eng == "pe":
                    continue
                if self.cnt[eng] - v >= 4:
                    continue
            if seen.get(s, 0) < v:
                waits.append((s, v))
                seen[s] = v
        return waits

    def op(self, eng, fn, reads=(), writes=(), inc=True):
        if self.dry:
            return
        waits = self._waits(eng, reads, writes)
        own = self.engsem[eng]
        if inc:
            self.cnt[eng] += 1
            ev = self.cnt[eng]
        else:
            ev = self.cnt[eng] + 1
        self.streams[eng].append((waits, fn, own if inc else None, 1))
        for r in reads:
            if r.r.get(own, 0) < ev:
                r.r[own] = ev
        for w in writes:
            w.w = {own: ev}
            w.r = {}

    def dma(self, queue, pairs, reads=(), writes=(), is_output=False):
        if self.dry:
            return
        tgt = writes[0] if writes else reads[0]
        if tgt.dsem is None:
            tgt.dsem = {}
        if queue not in tgt.dsem:
            tgt.dsem[queue] = self.newsem("d_%s_%s" % (tgt.name, queue))
        sem = tgt.dsem[queue]
        waits = self._waits(queue, reads, writes)
        for i, (o, a) in enumerate(pairs):
            self.dmatot[sem] = self.dmatot.get(sem, 0) + 16
            self.streams[queue].append(
                (waits if i == 0 else [], (lambda e, o=o, a=a: e.dma_start(out=o, in_=a)), sem, 16))
        ev = self.dmatot[sem]
        for r in reads:
            if r.r.get(sem, 0) < ev:
                r.r[sem] = ev
        for w in writes:
            w.w = {sem: ev}
            w.r = {}
        if is_output:
            self.out_events[sem] = ev

    def finish(self, eng="sp"):
        waits = [(s, v) for s, v in self.out_events.items()]
        for e in ENGS:
            if e != eng and self.cnt[e] > 0:
                waits.append((self.engsem[e], self.cnt[e]))
        self.streams[eng].append((waits, None, None, 0))

    def emit(self):
        nc = self.nc
        with ExitStack() as st:
            sems = [st.enter_context(nc.semaphore(n)) for n in self.semnames]
            block = st.enter_context(nc.Block())

            def run(engname, e):
                for waits, fn, isem, amt in self.streams[engname]:
                    for s, v in waits:
                        e.wait_ge(sems[s], v)
                    if fn is None:
                        continue
                    ins = fn(e)
                    if isem is not None:
                        ins.then_inc(sems[isem], amt)

            @block.tensor
            def _(e):
                run("pe", e)

            @block.scalar
            def _(e):
                run("act", e)

            @block.vector
            def _(e):
                run("dve", e)

            @block.gpsimd
            def _(e):
                run("pool", e)

            @block.sync
            def _(e):
                run("sp", e)


class WRing:
    def __init__(self, P, slots):
        self.P = P
        self.slots = slots
        self.res = [Res("ws%d" % i) for i in range(len(slots))]
        self.plan = []
        self.reset()

    def reset(self):
        self.free = list(range(len(self.slots)))
        self.nneed = 0
        self.nissue = 0
        self.slot_of = {}

    def _pump(self):
        while self.free and self.nissue < len(self.plan):
            s = self.free.pop(0)
            src, shape = self.plan[self.nissue]
            view = self._view(s, shape)
            self.P.dma("pool", [(view, src)], writes=[self.res[s]])
            self.slot_of[self.nissue] = s
            self.nissue += 1

    def _view(self, s, shape):
        t = self.slots[s]
        a, b = shape
        return t[:, 0:a * b].rearrange("p (a b) -> p a b", a=a)

    def need(self, src, shape):
        if self.P.dry:
            self.plan.append((src, shape))
            return self._view(0, shape), self.res[0], None
        idx = self.nneed
        self.nneed += 1
        self._pump()
        s = self.slot_of[idx]
        return self._view(s, shape), self.res[s], s

    def release(self, h):
        if self.P.dry:
            return
        self.free.append(h)
        self._pump()


class Builder:
    def __init__(self, stages, nslots=4):
        self.stages = stages
        self.nc = bass.Bass("TRN2", target_bir_lowering=False)
        self.P = Prog(self.nc)
        self.dram = {}
        self.off = 16512
        self.nslots = nslots

    def din(self, name, shape, dt=F32):
        if name not in self.dram:
            self.dram[name] = self.nc.dram_tensor(name, list(shape), dt, kind="ExternalInput").ap()
        return self.dram[name]

    def sb(self, name, shape, dt):
        n = 1
        for s in shape[1:]:
            n *= s
        nbytes = n * (4 if dt == F32 else 2)
        nbytes = (nbytes + 63) // 64 * 64
        self.uid = getattr(self, "uid", 0) + 1
        t = self.nc.alloc_sbuf_tensor_at("%s_%d" % (name, self.uid), list(shape), dt, offset=self.off)
        self.off += nbytes
        assert self.off <= 229344, ("SBUF overflow", name, self.off)
        return t

    def dbg(self, name, ap, shape, res=()):
        import os
        if not os.environ.get("KDBG"):
            return
        if name not in self.dram:
            self.dram[name] = self.nc.dram_tensor(name, list(shape), F32, kind="ExternalOutput").ap()
            self.dbgnames = getattr(self, "dbgnames", []) + [name]
        if self.P.dry:
            return
        self.barrier()
        r = Res("dbg_" + name)
        self.P.dma("pool", [(self.dram[name], ap)], reads=list(res) + [r], is_output=True)

    def bank(self):
        b = self.rot[self.rp % len(self.rot)]
        self.rp += 1
        return b

    def setup(self, st):
        nc, P = self.nc, self.P
        self.hT = self.sb("hT", [128, 8, TOK], F32)
        self.hn = self.sb("hn", [128, 8, TOK], BF16)
        self.RH = [[Res("hT%d_%d" % (c, t)) for t in range(NT)] for c in range(8)]
        self.RN = [[Res("hn%d_%d" % (c, t)) for t in range(NT)] for c in range(8)]
        self.ring = WRing(P, [self.sb("wslot%d" % i, [128, 4096], BF16) for i in range(self.nslots)])
        self.ps = [st.enter_context(nc.psum_tensor("ps%d" % i, [128, 512], F32)) for i in range(8)]
        self.RP = [Res("ps%d" % i) for i in range(8)]
        self.rot = list(range(8))
        self.rp = 0
        self.ones_bf = self.sb("ones_bf", [128, 128], BF16)
        self.ident_bf = self.sb("ident_bf", [128, 128], BF16)
        self.Rconst = Res("const")
        self.gains = self.sb("gains", [128, 9, 8], F32)
        self.Rgains = Res("gains")
        self.arena0 = self.off

    def load_consts(self):
        P = self.P
        P.op("pool", lambda e: e.memset(self.ones_bf[:], 1.0), writes=[self.Rconst])
        identd = self.din("ident", [128, 128])
        P.dma("pool", [(self.ident_bf[:], identd)], writes=[self.Rconst])
        P.dma("sp", [(self.gains[:], self.din("gains", [128, 9, 8]))], writes=[self.Rgains])

    def barrier(self):
        P = self.P
        if P.dry:
            return
        tot = [(P.engsem[e], P.cnt[e]) for e in ENGS if P.cnt[e] > 0]
        for e in ENGS:
            waits = []
            for s, v in tot:
                if P.seen[e].get(s, 0) < v and not (s == P.engsem[e]):
                    waits.append((s, v))
                    P.seen[e][s] = v
            if waits:
                P.streams[e].append((waits, None, None, 0))

    def load_x(self):
        xT = self.din("xT", [1024, TOK])
        v = xT.rearrange("(c p) t -> p c t", p=128)
        for t in range(NT):
            sl = slice(t * TW, (t + 1) * TW)
            self.P.dma("sp", [(self.hT[:, :, sl], v[:, :, sl])], writes=[self.RH[c][t] for c in range(8)])

    def rmsnorm(self, gi, out_final=None, keep=False):
        P = self.P
        base = self.off
        sqb = [self.sb("sqb%d" % i, [128, 8, TW], BF16) for i in range(2)]
        Rsq = [[Res("sq") for c in range(8)] for i in range(2)]
        rs = [self.sb("rs%d" % i, [128, TW], F32) for i in range(2)]
        Rrs = [Res("rs") for i in range(2)]
        rstd = [self.sb("rstd%d" % i, [128, TW], F32) for i in range(2)]
        Rrstd = [Res("rstd") for i in range(2)]
        if out_final is not None:
            ost = [self.sb("ost%d" % i, [128, 8, TW], F32) for i in range(2)]
            Rost = [Res("ost") for i in range(2)]
        hT, hn = self.hT, self.hn
        for t in range(NT):
            sl = slice(t * TW, (t + 1) * TW)
            i = t % 2
            for c in range(8):
                P.op("act", lambda e, c=c, i=i, sl=sl: e.activation(sqb[i][:, c, :], hT[:, c, sl], AF.Square),
                     reads=[self.RH[c][t]], writes=[Rsq[i][c]])
            b = self.bank()
            for c in range(8):
                P.op("pe", lambda e, c=c, i=i, b=b: e.matmul(self.ps[b][:], self.ones_bf[:], sqb[i][:, c, :], start=(c == 0), stop=(c == 7)),
                     reads=[Rsq[i][c], self.Rconst], writes=[self.RP[b]], inc=(c == 7))
            P.op("act", lambda e, i=i, b=b: e.activation(rs[i][:], self.ps[b][:], AF.Ln, bias=self.epsb[:], scale=1.0 / 1024.0),
                 reads=[self.RP[b], self.Rconst], writes=[Rrs[i]])
            P.op("act", lambda e, i=i: e.activation(rstd[i][:], rs[i][:], AF.Exp, scale=-0.5), reads=[Rrs[i]], writes=[Rrstd[i]])
            for c in range(8):
                if out_final is None:
                    P.op("dve", lambda e, c=c, i=i, sl=sl: e.scalar_tensor_tensor(
                        out=hn[:, c, sl], in0=hT[:, c, sl], scalar=self.gains[:, gi, c:c + 1], in1=rstd[i][:],
                        op0=ALU.mult, op1=ALU.mult),
                        reads=[self.RH[c][t], Rrstd[i], self.Rgains], writes=[self.RN[c][t]])
                else:
                    P.op("dve", lambda e, c=c, i=i, sl=sl: e.scalar_tensor_tensor(
                        out=ost[i][:, c, :], in0=hT[:, c, sl], scalar=self.gains[:, gi, c:c + 1], in1=rstd[i][:],
                        op0=ALU.mult, op1=ALU.mult),
                        reads=[self.RH[c][t], Rrstd[i], self.Rgains], writes=[Rost[i]])
            if out_final is not None:
                ov = out_final.rearrange("(c p) t -> p c t", p=128)
                P.dma("sp", [(ov[:, :, sl], ost[i][:])], reads=[Rost[i]], is_output=True)
        if keep:
            return
        self.barrier()
        self.off = base

    def ffn(self, l):
        P = self.P
        base = self.off
        self.rmsnorm(4 + l, keep=True)
        w1d = self.din("w1_%d" % l, [8, 128, 8, 512])
        w2d = self.din("w2_%d" % l, [8, 128, 4, 1024])
        h1 = [self.sb("h1_%d" % i, [128, 4, TW], BF16) for i in range(2)]
        Rh1 = [[Res("h1") for m in range(4)] for i in range(2)]
        rl = [self.sb("rl%d" % i, [128, TW], F32) for i in range(2)]
        Rrl = [Res("rl") for i in range(2)]
        hT, hn = self.hT, self.hn
        n = 0
        for j in range(8):
            w1, r1, s1 = self.ring.need(w1d[j], (8, 512))
            w2, r2, s2 = self.ring.need(w2d[j], (4, 1024))
            for t in range(NT):
                sl = slice(t * TW, (t + 1) * TW)
                i = (j * NT + t) % 2
                for m in range(4):
                    b = self.bank()
                    for k in range(8):
                        P.op("pe", lambda e, b=b, k=k, m=m, sl=sl, w1=w1: e.matmul(
                            self.ps[b][:], w1[:, k, m * 128:(m + 1) * 128], hn[:, k, sl], start=(k == 0), stop=(k == 7)),
                            reads=[r1, self.RN[k][t]], writes=[self.RP[b]], inc=(k == 7))
                    q = n % 2
                    n += 1
                    P.op("act", lambda e, b=b, q=q: e.activation(rl[q][:], self.ps[b][:], AF.Relu),
                         reads=[self.RP[b]], writes=[Rrl[q]])
                    P.op("act", lambda e, q=q, i=i, m=m: e.activation(h1[i][:, m, :], rl[q][:], AF.Square),
                         reads=[Rrl[q]], writes=[Rh1[i][m]])
                for mo in range(8):
                    b = self.bank()
                    for k in range(4):
                        P.op("pe", lambda e, b=b, k=k, mo=mo, i=i, w2=w2: e.matmul(
                            self.ps[b][:], w2[:, k, mo * 128:(mo + 1) * 128], h1[i][:, k, :], start=(k == 0), stop=(k == 3)),
                            reads=[r2, Rh1[i][k]], writes=[self.RP[b]], inc=(k == 3))
                    P.op("dve", lambda e, b=b, mo=mo, sl=sl: e.tensor_tensor(
                        out=hT[:, mo, sl], in0=self.ps[b][:], in1=hT[:, mo, sl], op=ALU.add),
                        reads=[self.RP[b], self.RH[mo][t]], writes=[self.RH[mo][t]])
            self.ring.release(s1)
            self.ring.release(s2)
        self.barrier()
        self.off = base


    def mixer_a(self, l, j):
        P = self.P
        self.rmsnorm(l)
        base = self.off
        hT, hn = self.hT, self.hn
        wud = self.din("a_wu_%d" % j, [4, 128, 8, 512])
        wvd = self.din("a_wv_%d" % j, [4, 128, 8, 512])
        wod = self.din("a_wo_%d" % j, [4, 128, 4, 1024])
        wsd = self.din("a_wsT_%d" % j, [128, 16, 128])
        bsd = self.din("a_bs_%d" % j, [1, 2048])
        lgd = self.din("a_lng_%d" % j, [128, 16])
        lbd = self.din("a_lnb_%d" % j, [128, 16])
        cmd = self.din("cmask", [128, 128])
        if "a_dgv" not in self.dram:
            self.dram["a_dgv"] = self.nc.dram_tensor("a_dgv", [16, 128, 2048], F32)
        dgv = self.dram["a_dgv"]
        wsf = self.sb("wsf", [128, 16, 128], F32)
        wsb = self.sb("wsb", [128, 16, 128], BF16)
        bsb = self.sb("bsb", [128, 16, 128], F32)
        extra = self.sb("extra", [128, 16, 128], F32)
        lng = self.sb("lng", [128, 16], F32)
        lnb = self.sb("lnb", [128, 16], F32)
        cm = self.sb("cm", [128, 128], F32)
        onesf = self.sb("onesf", [128, 128], F32)
        Rw, Rwb, Rbs, Rex, Rln, Rcm = Res("wsf"), Res("wsb"), Res("bsb"), Res("extra"), Res("ln"), Res("cm")
        P.dma("sp", [(wsf[:], wsd)], writes=[Rw])
        P.dma("sp", [(bsb[:].rearrange("p g t -> p (g t)"), bsd.partition_broadcast(128).rearrange("p o n -> p (o n)"))], writes=[Rbs])
        P.dma("sp", [(lng[:], lgd), (lnb[:], lbd)], writes=[Rln])
        P.dma("sp", [(cm[:], cmd)], writes=[Rcm])
        P.op("pool", lambda e: e.memset(onesf[:], 1.0), writes=[Rcm])
        for g in range(16):
            P.op("pool", lambda e, g=g: e.tensor_tensor(out=wsf[:, g, :], in0=wsf[:, g, :], in1=cm[:], op=ALU.mult),
                 reads=[Rcm], writes=[Rw])
        P.op("pool", lambda e: e.tensor_copy(wsb[:], wsf[:]), reads=[Rw], writes=[Rwb])
        for q in range(4):
            b = self.bank()
            P.op("pe", lambda e, b=b, q=q: e.matmul(self.ps[b][:], onesf[:], wsf[:, q * 4:(q + 1) * 4, :].rearrange("p g t -> p (g t)"),
                                                      start=True, stop=True),
                 reads=[Rw, Rcm], writes=[self.RP[b]])
            for gi in range(4):
                g = q * 4 + gi
                P.op("dve", lambda e, b=b, g=g, gi=gi: e.scalar_tensor_tensor(
                    out=extra[:, g, :], in0=self.ps[b][:, gi * 128:(gi + 1) * 128], scalar=lnb[:, g:g + 1], in1=bsb[:, g, :],
                    op0=ALU.mult, op1=ALU.add), reads=[self.RP[b], Rln, Rbs], writes=[Rex])
        stats = self.sb("stats", [128, NCH, 4, 6], F32)
        Rst = Res("stats")
        mv = self.sb("mv", [128, NCH, 2], F32)
        Rmv = Res("mv")
        mean = self.sb("mean", [128, NCH], F32)
        rstd = self.sb("lrstd", [128, NCH], F32)
        NG = 6
        gt = [self.sb("gt%d" % i, [128, TW], F32) for i in range(NG)]
        Rgt = [Res("gt") for i in range(NG)]
        n = 0
        for sl_ in range(4):
            wv, rv, sv = self.ring.need(wvd[sl_], (8, 512))
            for c in range(NCH):
                t = c // 4
                b = self.bank()
                for k in range(8):
                    P.op("pe", lambda e, b=b, k=k, c=c, wv=wv: e.matmul(
                        self.ps[b][:], hn[:, k, c * 128:(c + 1) * 128], wv[:, k, :], start=(k == 0), stop=(k == 7)),
                        reads=[rv, self.RN[k][t]], writes=[self.RP[b]], inc=(k == 7))
                i = n % NG
                n += 1
                P.op("act", lambda e, b=b, i=i: e.activation(gt[i][:], self.ps[b][:], AF.Gelu),
                     reads=[self.RP[b]], writes=[Rgt[i]])
                P.op("dve", lambda e, i=i, c=c, sl_=sl_: e.bn_stats(stats[:, c, sl_, :], gt[i][:]),
                     reads=[Rgt[i]], writes=[Rst])
                P.dma("sp", [(dgv.ap()[c, :, sl_ * 512:(sl_ + 1) * 512], gt[i][:])], reads=[Rgt[i]])
            self.ring.release(sv)
        Rdg = Res("dgv")
        if not P.dry:
            Rdg.w = {sm_: P.dmatot[sm_] for r_ in Rgt if r_.dsem for sm_ in r_.dsem.values()}
        for c in range(NCH):
            P.op("dve", lambda e, c=c: e.bn_aggr(mv[:, c, :], stats[:, c, :, :].rearrange("p a b -> p (a b)")),
                 reads=[Rst], writes=[Rmv])
        P.op("dve", lambda e: e.tensor_copy(mean[:], mv[:, :, 0]), reads=[Rmv], writes=[Rmv])
        P.op("act", lambda e: e.activation(rstd[:], mv[:, :, 1], AF.Sqrt, bias=self.epsb[:], scale=1.0),
             reads=[Rmv, self.Rconst], writes=[Rmv])
        P.op("dve", lambda e: e.reciprocal(rstd[:], rstd[:]), reads=[Rmv], writes=[Rmv])
        vn = [self.sb("vn%d" % i, [128, TW], BF16) for i in range(4)]
        Rvn = [Res("vn") for i in range(4)]
        u = [self.sb("u%d" % i, [128, TW], F32) for i in range(2)]
        Ru = [Res("u") for i in range(2)]
        z = [self.sb("z%d" % i, [128, TW], F32) for i in range(2)]
        Rz = [Res("z") for i in range(2)]
        y = [self.sb("y%d" % i, [128, TW], BF16) for i in range(8)]
        Ry = [Res("y") for i in range(8)]
        nu = 0
        ny = 0
        for p in range(4):
            wu, ru, su = self.ring.need(wud[p], (8, 512))
            wo, ro, so = self.ring.need(wod[p], (4, 1024))
            for t in range(NT):
                sl = slice(t * TW, (t + 1) * TW)
                for c4 in range(4):
                    c = t * 4 + c4
                    i = n % NG
                    n += 1
                    P.dma("sp", [(gt[i][:], dgv.ap()[c, :, p * 512:(p + 1) * 512])], reads=[Rdg], writes=[Rgt[i]])
                    P.op("dve", lambda e, i=i, c=c, c4=c4: e.tensor_scalar(
                        out=vn[c4][:], in0=gt[i][:], scalar1=mean[:, c:c + 1], scalar2=rstd[:, c:c + 1],
                        op0=ALU.subtract, op1=ALU.mult), reads=[Rgt[i], Rmv], writes=[Rvn[c4]])
                yb = (ny % 2) * 4
                ny += 1
                for gi in range(4):
                    g = p * 4 + gi
                    b = self.bank()
                    for k in range(8):
                        P.op("pe", lambda e, b=b, k=k, gi=gi, sl=sl, wu=wu: e.matmul(
                            self.ps[b][:], wu[:, k, gi * 128:(gi + 1) * 128], hn[:, k, sl], start=(k == 0), stop=(k == 7)),
                            reads=[ru, self.RN[k][t]], writes=[self.RP[b]], inc=(k == 7))
                    iu = nu % 2
                    nu += 1
                    P.op("act", lambda e, b=b, iu=iu: e.activation(u[iu][:], self.ps[b][:], AF.Gelu),
                         reads=[self.RP[b]], writes=[Ru[iu]])
                    b2 = self.bank()
                    for c4 in range(4):
                        P.op("pe", lambda e, b2=b2, c4=c4, gi=gi, g=g: e.matmul(
                            self.ps[b2][:, c4 * 128:(c4 + 1) * 128], vn[c4][:, gi * 128:(gi + 1) * 128], wsb[:, g, :],
                            start=True, stop=True), reads=[Rvn[c4], Rwb], writes=[self.RP[b2]], inc=(c4 == 3))
                    for c4 in range(4):
                        P.op("dve", lambda e, b2=b2, c4=c4, g=g, iu=iu: e.scalar_tensor_tensor(
                            out=z[iu][:, c4 * 128:(c4 + 1) * 128], in0=self.ps[b2][:, c4 * 128:(c4 + 1) * 128],
                            scalar=lng[:, g:g + 1], in1=extra[:, g, :], op0=ALU.mult, op1=ALU.add),
                            reads=[self.RP[b2], Rln, Rex], writes=[Rz[iu]])
                    P.op("pool", lambda e, iu=iu, yb=yb, gi=gi: e.tensor_tensor(
                        out=y[yb + gi][:], in0=z[iu][:], in1=u[iu][:], op=ALU.mult),
                        reads=[Rz[iu], Ru[iu]], writes=[Ry[yb + gi]])
                for mo in range(8):
                    b = self.bank()
                    for k in range(4):
                        P.op("pe", lambda e, b=b, k=k, mo=mo, yb=yb, wo=wo: e.matmul(
                            self.ps[b][:], wo[:, k, mo * 128:(mo + 1) * 128], y[yb + k][:], start=(k == 0), stop=(k == 3)),
                            reads=[ro, Ry[yb + k]], writes=[self.RP[b]], inc=(k == 3))
                    P.op("dve", lambda e, b=b, mo=mo, sl=sl: e.tensor_tensor(
                        out=hT[:, mo, sl], in0=self.ps[b][:], in1=hT[:, mo, sl], op=ALU.add),
                        reads=[self.RP[b], self.RH[mo][t]], writes=[self.RH[mo][t]])
            self.ring.release(su)
            self.ring.release(so)
        self.barrier()
        self.off = base


    def mixer_b(self, l):
        P = self.P
        self.rmsnorm(l)
        base = self.off
        hT, hn = self.hT, self.hn
        wind = self.din("b_win", [8, 128, 8, 512])
        woutd = self.din("b_wout", [2, 128, 4, 1024])
        lbld = self.din("b_lbl", [128, 4, 8])
        ngd = self.din("b_ng", [128, 8])
        seld = self.din("sel", [128, 4])
        bmd = self.din("bmask4", [128, 512])
        if "b_ccin" not in self.dram:
            self.dram["b_ccin"] = self.nc.dram_tensor("b_ccin", [128, 1032], F32)
            self.dram["b_ccout"] = self.nc.dram_tensor("b_ccout", [512, 1032], F32)
        cc_in, cc_out = self.dram["b_ccin"], self.dram["b_ccout"]
        if "b_dsg" not in self.dram:
            self.dram["b_dsg"] = self.nc.dram_tensor("b_dsg", [8, 4, 128, TW], F32)
            self.dram["b_dit"] = self.nc.dram_tensor("b_dit", [8, 4, 128, TW], BF16)
        dsg, dit = self.dram["b_dsg"], self.dram["b_dit"]
        Rdb = Res("b_spill")
        mode = [1]
        Rcci, Rcco = Res("ccin"), Res("ccout")
        lg = self.sb("lg", [128, 4, 8], F32)
        ex = self.sb("ex", [128, 4, 8], F32)
        ssum = self.sb("ssum", [128, 8], F32)
        lb = self.sb("lb", [128, 8], F32)
        oml = self.sb("oml", [128, 8], F32)
        noml = self.sb("noml", [128, 8], F32)
        ng = self.sb("ng", [128, 8], F32)
        sel = self.sb("sel", [128, 4], F32)
        bm4 = self.sb("bm4", [128, 512], F32)
        ones5 = self.sb("ones5", [128, 512], F32)
        rm5 = self.sb("rm5", [128, 512], F32)
        Rc = Res("bconst")
        Rlb = Res("lb")
        P.dma("sp", [(lg[:], lbld), (ng[:], ngd), (sel[:], seld), (bm4[:], bmd)], writes=[Rc])
        P.op("pool", lambda e: e.memset(ones5[:], 1.0), writes=[Rc])
        P.op("pool", lambda e: e.memset(rm5[:], 1.0), writes=[Rc])
        P.op("pool", lambda e: e.memset(rm5[:].rearrange("p (c l) -> p c l", l=64)[:, :, 0:1], 0.0), writes=[Rc])
        P.op("act", lambda e: e.activation(ex[:], lg[:], AF.Exp), reads=[Rc], writes=[Rlb])
        P.op("dve", lambda e: e.tensor_tensor(out=ssum[:], in0=ex[:, 0, :], in1=ex[:, 1, :], op=ALU.add), reads=[Rlb], writes=[Rlb])
        P.op("dve", lambda e: e.tensor_tensor(out=ssum[:], in0=ssum[:], in1=ex[:, 2, :], op=ALU.add), reads=[Rlb], writes=[Rlb])
        P.op("dve", lambda e: e.tensor_tensor(out=ssum[:], in0=ssum[:], in1=ex[:, 3, :], op=ALU.add), reads=[Rlb], writes=[Rlb])
        P.op("dve", lambda e: e.reciprocal(ssum[:], ssum[:]), reads=[Rlb], writes=[Rlb])
        P.op("dve", lambda e: e.tensor_tensor(out=lb[:], in0=ex[:, 1, :], in1=ssum[:], op=ALU.mult), reads=[Rlb], writes=[Rlb])
        P.op("dve", lambda e: e.tensor_scalar(out=oml[:], in0=lb[:], scalar1=-1.0, scalar2=1.0, op0=ALU.mult, op1=ALU.add), reads=[Rlb], writes=[Rlb])
        P.op("dve", lambda e: e.tensor_scalar(out=noml[:], in0=lb[:], scalar1=-1.0, scalar2=None, op0=ALU.add), reads=[Rlb], writes=[Rlb])
        xch = self.sb("xch", [128, 1032], F32)
        Rx = Res("xch")
        Rg = Res("gath")
        sst = self.sb("sst", [128, 8, 128], F32)
        Rsst = Res("sst")
        sg = [self.sb("sg%d" % i, [128, TW], F32) for i in range(2)]
        Rsg = [Res("sg") for i in range(2)]
        lft = [self.sb("lft%d" % i, [128, TW], F32) for i in range(2)]
        Rlf = [Res("lft") for i in range(2)]
        et = [self.sb("et%d" % i, [128, TW], F32) for i in range(2)]
        Ret = [Res("et") for i in range(2)]
        itok = self.sb("itok", [128, NCH, 128], BF16)
        Rit = [Res("itok") for t in range(NT)]
        ktok = self.sb("ktok", [128, NCH, 128], BF16)
        Rkt = [Res("ktok") for t in range(NT)]
        pbase = self.off

        def sigm(dst, Rdst, b):
            P.op("act", lambda e: e.activation(dst[:], self.ps[b][:], AF.Exp, scale=-1.0), reads=[self.RP[b]], writes=[Rdst])
            P.op("act", lambda e: e.activation(dst[:], dst[:], AF.Ln, bias=self.oneb[:]), reads=[self.Rconst], writes=[Rdst])
            P.op("act", lambda e: e.activation(dst[:], dst[:], AF.Exp, scale=-1.0), writes=[Rdst])

        def proj_f(w, rw, h, t, i2, kk_out, Rkk):
            sl = slice(t * TW, (t + 1) * TW)
            if mode[0] == 1:
                b = self.bank()
                for k in range(8):
                    P.op("pe", lambda e, b=b, k=k, sl=sl: e.matmul(self.ps[b][:], w[:, k, 128:256], hn[:, k, sl], start=(k == 0), stop=(k == 7)),
                         reads=[rw, self.RN[k][t]], writes=[self.RP[b]], inc=(k == 7))
                sigm(sg[i2], Rsg[i2], b)
                P.dma("sp", [(dsg.ap()[h, t], sg[i2][:])], reads=[Rsg[i2]])
            else:
                P.dma("sp", [(sg[i2][:], dsg.ap()[h, t])], reads=[Rdb], writes=[Rsg[i2]])
            P.op("act", lambda e: e.activation(lft[i2][:], sg[i2][:], AF.Ln, bias=lb[:, h:h + 1], scale=oml[:, h:h + 1]),
                 reads=[Rsg[i2], Rlb], writes=[Rlf[i2]])
            P.op("dve", lambda e: e.tensor_scalar(out=kk_out, in0=sg[i2][:], scalar1=noml[:, h:h + 1], scalar2=oml[:, h:h + 1],
                                                  op0=ALU.mult, op1=ALU.add), reads=[Rsg[i2], Rlb], writes=[Rkk])

        iTb = [self.sb("iTb%d" % i, [128, TW], BF16) for i in range(1)] * 2
        RiT = [Res("iTb")] * 2
        icnt = [0]
        pbase = self.off

        def proj_i(w, rw, t, h=0):
            sl = slice(t * TW, (t + 1) * TW)
            itv = itok[:, t * 4:(t + 1) * 4, :].rearrange("p c v -> p (c v)")
            if mode[0] == 2:
                P.dma("sp", [(itv, dit.ap()[h, t])], reads=[Rdb], writes=[Rit[t]])
                return
            j = icnt[0] % 2
            icnt[0] += 1
            b = self.bank()
            for k in range(8):
                P.op("pe", lambda e, b=b, k=k: e.matmul(self.ps[b][:], w[:, k, 256:384], hn[:, k, sl], start=(k == 0), stop=(k == 7)),
                     reads=[rw, self.RN[k][t]], writes=[self.RP[b]], inc=(k == 7))
            P.op("act", lambda e, b=b: e.copy(iTb[j][:], self.ps[b][:]), reads=[self.RP[b]], writes=[RiT[j]])
            b2 = self.bank()
            psb = self.ps[b2][:].bitcast(BF16)
            for c4 in range(4):
                P.op("pe", lambda e, c4=c4, psb=psb: e.transpose(psb[:, c4 * 128:(c4 + 1) * 128], iTb[j][:, c4 * 128:(c4 + 1) * 128], self.ident_bf[:]),
                     reads=[RiT[j], self.Rconst], writes=[self.RP[b2]], inc=(c4 == 3))
            P.op("act", lambda e, psb=psb: e.copy(itv, psb[:, 0:512]),
                 reads=[self.RP[b2]], writes=[Rit[t]])
            P.dma("sp", [(dit.ap()[h, t], itv)], reads=[Rit[t]])

        def transp(src, Rsrc, t):
            b = self.bank()
            psb = self.ps[b][:].bitcast(BF16)
            for c4 in range(4):
                c = t * 4 + c4
                P.op("pe", lambda e, c=c, c4=c4, psb=psb: e.transpose(psb[:, c4 * 128:(c4 + 1) * 128], src[:, c * 128:(c + 1) * 128], self.ident_bf[:]),
                     reads=[Rsrc, self.Rconst], writes=[self.RP[b]], inc=(c4 == 3))
            P.op("act", lambda e, psb=psb: e.copy(ktok[:, t * 4:(t + 1) * 4, :].rearrange("p c v -> p (c v)"), psb[:, 0:512]),
                 reads=[self.RP[b]], writes=[Rkt[t]])

        BG = self.sb("BG", [128, TOK], F32)
        RBG = [Res("BG") for t in range(NT)]
        KK = self.sb("KK", [128, TOK], F32)
        RKK = [Res("KK") for t in range(NT)]
        kgb = self.sb("kgb", [128, TOK], BF16)
        Rkg = [Res("kgb") for t in range(NT)]
        n = 0
        for h in range(8):
            w, rw, sw = self.ring.need(wind[h], (8, 512))
            for t in range(NT):
                sl = slice(t * TW, (t + 1) * TW)
                i2 = n % 2
                n += 1
                proj_f(w, rw, h, t, i2, KK[:, sl], RKK[t])
                if t == 0:
                    P.op("dve", lambda e, i2=i2, sl=sl: e.tensor_tensor_scan(BG[:, sl], ones5[:], lft[i2][:], 0.0, ALU.mult, ALU.add),
                         reads=[Rlf[i2], Rc], writes=[RBG[t]])
                else:
                    P.op("dve", lambda e, i2=i2, sl=sl, t=t: e.tensor_tensor_scan(
                        BG[:, sl], ones5[:], lft[i2][:], BG[:, t * TW - 1:t * TW], ALU.mult, ALU.add),
                        reads=[Rlf[i2], Rc, RBG[t - 1]], writes=[RBG[t]])
                proj_i(w, rw, t, h)
            for t in range(NT):
                sl = slice(t * TW, (t + 1) * TW)
                i2 = n % 2
                n += 1
                P.op("act", lambda e, i2=i2, sl=sl: e.activation(et[i2][:], BG[:, sl], AF.Exp, bias=BG[:, TOK - 1:TOK], scale=-1.0),
                     reads=[RBG[t], RBG[NT - 1]], writes=[Ret[i2]])
                P.op("pool", lambda e, i2=i2, sl=sl: e.tensor_tensor(out=kgb[:, sl], in0=KK[:, sl], in1=et[i2][:], op=ALU.mult),
                     reads=[RKK[t], Ret[i2]], writes=[Rkg[t]])
                transp(kgb, Rkg[t], t)
            P.op("act", lambda e, h=h: e.activation(xch[:, 1024 + h:1025 + h], BG[:, TOK - 1:TOK], AF.Exp), reads=[RBG[NT - 1]], writes=[Rx])
            b = self.bank()
            for c in range(NCH):
                P.op("pe", lambda e, b=b, c=c: e.matmul(self.ps[b][:, 0:128], ktok[:, c, :], itok[:, c, :], start=(c == 0), stop=(c == NCH - 1)),
                     reads=[Rkt[c // 4], Rit[c // 4]], writes=[self.RP[b]], inc=(c == NCH - 1))
            P.op("dve", lambda e, b=b, h=h: e.tensor_copy(xch[:, h * 128:(h + 1) * 128], self.ps[b][:, 0:128]), reads=[self.RP[b]], writes=[Rx])
            self.ring.release(sw)
        import os
        stop = int(os.environ.get("KB_STOP", "9"))
        if stop <= 1:
            self.barrier()
            self.off = base
            return
        mode[0] = 2
        if not P.dry:
            Rdb.w = {sm_: P.dmatot[sm_] for r_ in (Rsg + Rit) if r_.dsem for sm_ in r_.dsem.values()}
        P.dma("sp", [(cc_in.ap(), xch[:])], reads=[Rx], writes=[Rcci])
        P.op("pool", lambda e: e.collective_compute("AllGather", ALU.bypass, replica_groups=[[0, 1, 2, 3], [4, 5, 6, 7]],
                                                     ins=[cc_in.ap().opt()], outs=[cc_out.ap().opt()]), reads=[Rcci], writes=[Rcco])
        self.barrier()
        self.off = pbase
        gath = self.sb("gath", [128, 4, 1032], F32)
        P.dma("sp", [(gath[:], cc_out.ap().rearrange("(r p) n -> p r n", p=128))], reads=[Rcco], writes=[Rg])
        pa = self.sb("pa", [128, 128], F32)
        pb_ = self.sb("pb", [128, 128], F32)
        Rpa = Res("pa")
        for h in range(8):
            S = lambda i, h=h: gath[:, i, h * 128:(h + 1) * 128]
            D = lambda i, h=h: gath[:, i, 1024 + h:1025 + h]
            sh = sst[:, h, :]
            P.op("dve", lambda e, S=S, D=D: e.scalar_tensor_tensor(out=pa[:], in0=S(0), scalar=D(1), in1=S(1), op0=ALU.mult, op1=ALU.add),
                 reads=[Rg], writes=[Rpa])
            P.op("dve", lambda e, S=S, D=D: e.scalar_tensor_tensor(out=pb_[:], in0=pa[:], scalar=D(2), in1=S(2), op0=ALU.mult, op1=ALU.add),
                 reads=[Rg, Rpa], writes=[Rpa])
            P.op("dve", lambda e, S=S, sh=sh: e.tensor_scalar(out=sh, in0=S(0), scalar1=sel[:, 1:2], scalar2=None, op0=ALU.mult),
                 reads=[Rg, Rc], writes=[Rsst])
            P.op("dve", lambda e, sh=sh: e.scalar_tensor_tensor(out=sh, in0=pa[:], scalar=sel[:, 2:3], in1=sh, op0=ALU.mult, op1=ALU.add),
                 reads=[Rpa, Rc], writes=[Rsst])
            P.op("dve", lambda e, sh=sh: e.scalar_tensor_tensor(out=sh, in0=pb_[:], scalar=sel[:, 3:4], in1=sh, op0=ALU.mult, op1=ALU.add),
                 reads=[Rpa, Rc], writes=[Rsst])
        if stop <= 2:
            self.barrier()
            self.off = base
            return
        self.barrier()
        self.off = pbase
        gs = self.sb("gs", [128, TOK], BF16)
        Rgs = [Res("gs") for t in range(NT)]
        qtb = self.sb("qtb", [128, TOK], BF16)
        Rqt = [Res("qtb") for t in range(NT)]
        ktb = self.sb("ktb", [128, TOK], BF16)
        Rkb = [Res("ktb") for t in range(NT)]
        smb = self.sb("smb", [128, NCH, 128], BF16)
        Rsm = [Res("smb") for t in range(NT)]
        Sb = self.sb("Sb", [128, 33, 128], BF16)
        RSb = [Res("Sb") for c in range(33)]
        Sf = [self.sb("Sf%d" % i, [128, 128], F32) for i in range(2)]
        RSf = [Res("Sf") for i in range(2)]
        Tt = self.sb("Tt", [128, 128], F32)
        RT = Res("Tt")
        ebl = self.sb("ebl", [128, 32], F32)
        Reb = [Res("ebl") for t in range(NT)]
        qs = [self.sb("qs%d" % i, [128, TW], F32) for i in range(2)]
        Rqs = [Res("qs") for i in range(2)]
        kkt = [self.sb("kkt%d" % i, [128, TW], F32) for i in range(1)] * 2
        Rkkt = [Res("kkt")] * 2
        bc = [self.sb("bc%d" % i, [128, TW], F32) for i in range(1)] * 2
        Rbc = [Res("bc")] * 2
        en = [self.sb("en%d" % i, [128, TW], F32) for i in range(1)] * 2
        Ren = [Res("en")] * 2
        sqo = [self.sb("sqo%d" % i, [128, TW], BF16) for i in range(1)] * 2
        Rsq = [Res("sqo")] * 2
        rs = [self.sb("brs%d" % i, [128, TW], F32) for i in range(1)] * 2
        Rrs = [Res("brs")] * 2
        ont = [self.sb("ont%d" % i, [128, TW], F32) for i in range(1)] * 2
        Ron = [Res("ont")] * 2
        yv = [self.sb("yv%d" % i, [128, TW], BF16) for i in range(2)]
        Ryv = [Res("yv") for i in range(2)]
        self.rot = [0, 1, 2, 3]
        hw = {}
        hwo = {}
        state = {"n": n, "cur": 0, "m": 0}

        def S1a(h, t, u):
            if t == 0:
                hw[h] = self.ring.need(wind[h], (8, 512))
                if h % 4 == 0:
                    hwo[h // 4] = self.ring.need(woutd[h // 4], (4, 1024))
            w, rw, sw = hw[h]
            sl = slice(t * TW, (t + 1) * TW)
            i2 = state["n"] % 2
            state["n"] += 1
            bP = [4 + 2 * (u % 2), 5 + 2 * (u % 2)]
            b = self.bank()
            for k in range(8):
                P.op("pe", lambda e, b=b, k=k, sl=sl, w=w: e.matmul(self.ps[b][:], w[:, k, 0:128], hn[:, k, sl], start=(k == 0), stop=(k == 7)),
                     reads=[rw, self.RN[k][t]], writes=[self.RP[b]], inc=(k == 7))
            sigm(qs[i2], Rqs[i2], b)
            P.op("dve", lambda e, b=b, i2=i2: e.tensor_tensor(out=qs[i2][:], in0=self.ps[b][:], in1=qs[i2][:], op=ALU.mult), reads=[self.RP[b]], writes=[Rqs[i2]])
            proj_f(w, rw, h, t, i2, kkt[i2][:], Rkkt[i2])
            b = self.bank()
            for k in range(8):
                P.op("pe", lambda e, b=b, k=k, sl=sl, w=w: e.matmul(self.ps[b][:], w[:, k, 384:512], hn[:, k, sl], start=(k == 0), stop=(k == 7)),
                     reads=[rw, self.RN[k][t]], writes=[self.RP[b]], inc=(k == 7))
            sigm(en[i2], Ren[i2], b)
            P.op("dve", lambda e, b=b, sl=sl, i2=i2: e.tensor_tensor(out=gs[:, sl], in0=self.ps[b][:], in1=en[i2][:], op=ALU.mult), reads=[self.RP[b], Ren[i2]], writes=[Rgs[t]])
            proj_i(w, rw, t, h)
            P.op("dve", lambda e, i2=i2: e.tensor_tensor_scan(bc[i2][:], rm5[:], lft[i2][:], 0.0, ALU.mult, ALU.add),
                 reads=[Rlf[i2], Rc], writes=[Rbc[i2]])
            P.op("act", lambda e, i2=i2: e.activation(et[i2][:], bc[i2][:], AF.Exp), reads=[Rbc[i2]], writes=[Ret[i2]])
            P.op("act", lambda e, i2=i2: e.activation(en[i2][:], bc[i2][:], AF.Exp, scale=-1.0), reads=[Rbc[i2]], writes=[Ren[i2]])
            P.op("act", lambda e, i2=i2, t=t: e.activation(ebl[:, t * 8:(t + 1) * 8], bc[i2][:].rearrange("p (c l) -> p c l", l=64)[:, :, 63], AF.Exp),
                 reads=[Rbc[i2]], writes=[Reb[t]])
            P.op("pool", lambda e, i2=i2, sl=sl: e.tensor_tensor(out=qtb[:, sl], in0=qs[i2][:], in1=et[i2][:], op=ALU.mult),
                 reads=[Rqs[i2], Ret[i2]], writes=[Rqt[t]])
            P.op("pool", lambda e, i2=i2, sl=sl: e.tensor_tensor(out=ktb[:, sl], in0=kkt[i2][:], in1=en[i2][:], op=ALU.mult),
                 reads=[Rkkt[i2], Ren[i2]], writes=[Rkb[t]])
            if t == NT - 1:
                self.ring.release(sw)


        def S1b(h, t, u):
            sl = slice(t * TW, (t + 1) * TW)
            bP = [4 + 2 * (u % 2), 5 + 2 * (u % 2)]
            transp(ktb, Rkb[t], t)
            b = self.bank()
            for c4 in range(4):
                c = t * 4 + c4
                P.op("pe", lambda e, b=b, c=c, c4=c4: e.matmul(self.ps[b][:, c4 * 128:(c4 + 1) * 128], ktb[:, c * 128:(c + 1) * 128],
                                                               qtb[:, c * 128:(c + 1) * 128], start=True, stop=True),
                     reads=[Rkb[t], Rqt[t]], writes=[self.RP[b]], inc=(c4 == 3))
            P.op("dve", lambda e, b=b, t=t: e.tensor_tensor(out=smb[:, t * 4:(t + 1) * 4, :].rearrange("p c v -> p (c v)"), in0=self.ps[b][:],
                                                             in1=bm4[:], op=ALU.mult), reads=[self.RP[b], Rc], writes=[Rsm[t]])
            for c8 in range(8):
                bk = t * 4 + c8 // 2
                r0 = (c8 % 2) * 64
                P.op("pe", lambda e, c8=c8, bk=bk, r0=r0, bP=bP: e.matmul(
                    self.ps[bP[c8 % 2]][:, (c8 // 2) * 128:(c8 // 2 + 1) * 128], ktok[r0:r0 + 64, bk, :], itok[r0:r0 + 64, bk, :],
                    start=True, stop=True), reads=[Rkt[t], Rit[t]], writes=[self.RP[bP[c8 % 2]]], inc=(c8 >= 6))


        def S2(h, t, u):
            wo, ro, so = hwo[h // 4]
            sl = slice(t * TW, (t + 1) * TW)
            i2 = state["m"] % 2
            state["m"] += 1
            bP = [4 + 2 * (u % 2), 5 + 2 * (u % 2)]
            if t == 0:
                P.op("act", lambda e, h=h: e.copy(Sb[:, 0, :], sst[:, h, :]), reads=[Rsst], writes=[RSb[0]])
                state["cur"] = 0
            cur = state["cur"]
            for c8 in range(8):
                c = t * 8 + c8
                pP = self.ps[bP[c8 % 2]][:, (c8 // 2) * 128:(c8 // 2 + 1) * 128]
                RpP = self.RP[bP[c8 % 2]]
                if c == 0:
                    P.op("dve", lambda e, pP=pP, h=h: e.tensor_tensor(out=Sf[0][:], in0=pP, in1=sst[:, h, :], op=ALU.add),
                         reads=[RpP, Rsst], writes=[RSf[0]])
                    cur = 0
                else:
                    P.op("act", lambda e, c=c, cur=cur: e.activation(Sb[:, c, :], Sf[cur][:], AF.Identity, scale=ebl[:, c - 1:c]),
                         reads=[RSf[cur], Reb[(c - 1) // 8]], writes=[RSb[c]])
                    P.op("dve", lambda e, c=c, cur=cur, pP=pP: e.scalar_tensor_tensor(out=Sf[1 - cur][:], in0=Sf[cur][:], scalar=ebl[:, c - 1:c], in1=pP,
                                                                                      op0=ALU.mult, op1=ALU.add),
                         reads=[RSf[cur], Reb[(c - 1) // 8], RpP], writes=[RSf[1 - cur]])
                    cur = 1 - cur
            b = self.bank()
            for c4 in range(4):
                bk = t * 4 + c4
                o0 = c4 * 128
                P.op("pe", lambda e, b=b, bk=bk, o0=o0: e.matmul(self.ps[b][:, o0:o0 + 128], itok[:, bk, :], smb[:, bk, :], start=True, stop=False),
                     reads=[Rit[t], Rsm[t]], writes=[self.RP[b]], inc=False)
                P.op("pe", lambda e, b=b, bk=bk, o0=o0: e.matmul(self.ps[b][:, o0:o0 + 64], Sb[:, 2 * bk, :], qtb[:, bk * 128:bk * 128 + 64], start=False, stop=False),
                     reads=[RSb[2 * bk], Rqt[t]], writes=[self.RP[b]], inc=False)
                P.op("pe", lambda e, b=b, bk=bk, o0=o0: e.matmul(self.ps[b][:, o0 + 64:o0 + 128], Sb[:, 2 * bk + 1, :], qtb[:, bk * 128 + 64:bk * 128 + 128], start=False, stop=True),
                     reads=[RSb[2 * bk + 1], Rqt[t]], writes=[self.RP[b]], inc=(c4 == 3))
            P.op("act", lambda e, b=b, i2=i2: e.activation(sqo[i2][:], self.ps[b][:], AF.Square), reads=[self.RP[b]], writes=[Rsq[i2]])
            b2 = self.bank()
            P.op("pe", lambda e, b2=b2, i2=i2: e.matmul(self.ps[b2][:], self.ones_bf[:], sqo[i2][:], start=True, stop=True),
                 reads=[Rsq[i2], self.Rconst], writes=[self.RP[b2]])
            P.op("act", lambda e, b2=b2, i2=i2: e.activation(rs[i2][:], self.ps[b2][:], AF.Ln, bias=self.epsb[:], scale=1.0 / 128.0),
                 reads=[self.RP[b2], self.Rconst], writes=[Rrs[i2]])
            P.op("act", lambda e, i2=i2: e.activation(rs[i2][:], rs[i2][:], AF.Exp, scale=-0.5), reads=[Rrs[i2]], writes=[Rrs[i2]])
            P.op("dve", lambda e, b=b, i2=i2, h=h: e.scalar_tensor_tensor(out=ont[i2][:], in0=self.ps[b][:], scalar=ng[:, h:h + 1], in1=rs[i2][:],
                                                                           op0=ALU.mult, op1=ALU.mult), reads=[self.RP[b], Rrs[i2], Rc], writes=[Ron[i2]])
            P.op("pool", lambda e, i2=i2, sl=sl: e.tensor_tensor(out=yv[i2][:], in0=ont[i2][:], in1=gs[:, sl], op=ALU.mult),
                 reads=[Ron[i2], Rgs[t]], writes=[Ryv[i2]])
            for mo in range(8):
                b3 = self.bank()
                P.op("pe", lambda e, b3=b3, mo=mo, i2=i2, h=h, wo=wo: e.matmul(self.ps[b3][:], wo[:, h % 4, mo * 128:(mo + 1) * 128], yv[i2][:], start=True, stop=True),
                     reads=[ro, Ryv[i2]], writes=[self.RP[b3]])
                P.op("dve", lambda e, b3=b3, mo=mo, sl=sl: e.tensor_tensor(out=hT[:, mo, sl], in0=self.ps[b3][:], in1=hT[:, mo, sl], op=ALU.add),
                     reads=[self.RP[b3], self.RH[mo][t]], writes=[self.RH[mo][t]])

            state["cur"] = cur
            if t == NT - 1 and h % 4 == 3:
                self.ring.release(so)

        units = [(h, t) for h in range(8) for t in range(NT)]
        S1a(units[0][0], units[0][1], 0)
        for u, (h, t) in enumerate(units):
            if u + 1 < len(units):
                S1a(units[u + 1][0], units[u + 1][1], u + 1)
            S1b(h, t, u)
            S2(h, t, u)
        self.rot = list(range(8))
        self.barrier()
        self.off = base

    def mixer_c(self, l):
        P = self.P
        self.rmsnorm(l)
        base = self.off
        hT, hn = self.hT, self.hn
        nc = self.nc
        wxd = self.din("c_wxm", [4, 128, 8, 512])
        wzd = self.din("c_wz", [4, 128, 8, 512])
        wod = self.din("c_wo", [4, 128, 4, 1024])
        lst = [("c_ccin", [128, 24]), ("c_ccout", [512, 24])]
        for h_ in range(8):
            lst += [("c_ccin2_%d" % h_, [128, 2 * 516]), ("c_ccout2_%d" % h_, [512, 2 * 516])]
        for nm, shp in lst:
            if nm not in self.dram:
                self.dram[nm] = nc.dram_tensor(nm, shp, F32)
        if "c_dxm" not in self.dram:
            self.dram["c_dxm"] = nc.dram_tensor("c_dxm", [16, 128, 4, 3 + TW], BF16)
            self.dram["c_dca"] = nc.dram_tensor("c_dca", [16, 128, 4, TW], BF16)
        dxm, dca = self.dram["c_dxm"], self.dram["c_dca"]
        Rd = [Res("c_spill%d" % i) for i in range(16)]
        cci, cco = self.dram["c_ccin"], self.dram["c_ccout"]
        cci2 = [self.dram["c_ccin2_%d" % h_] for h_ in range(8)]
        cco2 = [self.dram["c_ccout2_%d" % h_] for h_ in range(8)]
        Rcci, Rcco = Res("cci"), Res("cco")
        Rcci2 = [Res("cci2_%d" % h_) for h_ in range(8)]
        Rcco2 = [Res("cco2_%d" % h_) for h_ in range(8)]
        LNS = -0.5 * float(np.log(512.0))
        bdq = self.sb("bdq", [128, 16, 128], BF16)
        bdk = self.sb("bdk", [128, 16, 128], BF16)
        bdv = self.sb("bdv", [128, 16, 128], BF16)
        Gqk = self.sb("Gqk", [128, 16, 8], BF16)
        Gv = self.sb("Gv", [128, 16, 8], BF16)
        cw = self.sb("cw", [128, 16, 4], F32)
        cb = self.sb("cb", [128, 16], F32)
        cng = self.sb("cng", [128, 16], F32)
        csk = self.sb("csk", [128, 16], F32)
        bgb = self.sb("bgb", [128, 8], F32)
        E4 = self.sb("E4", [128, 4, 4], BF16)
        onec = self.sb("onec", [128, 1], BF16)
        U = self.sb("U", [128, 128], F32)
        cmb = self.sb("cmb", [128, 128], F32)
        onesf = self.sb("onesf", [128, 128], F32)
        ones16 = self.sb("ones16", [128, 16], F32)
        lnsb = self.sb("lnsb", [128, 1], F32)
        zer = self.sb("zer", [128, 128], BF16)
        sel = self.sb("csel", [128, 4], F32)
        selp = self.sb("cselp", [128, 4], F32)
        Rk = Res("cconst")
        P.dma("pool", [(bdq[:], self.din("c_bdq", [128, 16, 128])), (bdk[:], self.din("c_bdk", [128, 16, 128])),
                       (bdv[:], self.din("c_bdv", [128, 16, 128])), (E4[:], self.din("c_E4", [128, 4, 4]))], writes=[Rk])
        P.dma("sp", [(cw[:], self.din("c_cw", [128, 16, 4])), (cb[:], self.din("c_cb", [128, 16])), (cng[:], self.din("c_ng", [128, 16])),
                     (csk[:], self.din("c_skip", [128, 16])), (U[:], self.din("cmask", [128, 128])),
                     (bgb[:], self.din("c_bg", [1, 8]).partition_broadcast(128).rearrange("p o n -> p (o n)")),
                     (sel[:], self.din("sel", [128, 4])), (selp[:], self.din("selp", [128, 4]))], writes=[Rk])
        P.op("pool", lambda e: e.memset(onesf[:], 1.0), writes=[Rk])
        P.op("pool", lambda e: e.memset(ones16[:], 1.0), writes=[Rk])
        P.op("pool", lambda e: e.memset(onec[:], 1.0), writes=[Rk])
        P.op("pool", lambda e: e.memset(lnsb[:], LNS), writes=[Rk])
        P.op("pool", lambda e: e.memset(zer[:], 0.0), writes=[Rk])
        gnames = ["GT8", "LI", "LF", "A", "TOT", "INC", "AG", "TSw", "TSu", "TSg", "TSea", "DEC", "WL"]
        GT8 = self.sb("GT8", [128, 16, 8], F32)
        G = {k: self.sb(k, [128, 16, 4], F32) for k in gnames[1:]}
        DTOT = self.sb("DTOT", [128, 4], F32)
        ATOT = self.sb("ATOT", [128, 4], F32)
        Rgt = Res("gates")
        pbase = self.off
        bdT = [self.sb("bdT%d" % i, [128, 16, 128], BF16) for i in range(3)]
        wg = self.sb("wg", [128, 48, 8], BF16)
        Rt = Res("bdT")
        P.dma("pool", [(bdT[0][:], self.din("c_bdqT", [128, 16, 128])), (bdT[1][:], self.din("c_bdkT", [128, 16, 128])),
                       (bdT[2][:], self.din("c_bdvT", [128, 16, 128])), (wg[:], self.din("c_wg", [128, 48, 8]))], writes=[Rt])
        for fc in range(16):
            b = self.bank()
            P.op("pe", lambda e, b=b, fc=fc: e.matmul(self.ps[b][:, 0:8], bdT[0][:, fc, :], wg[:, fc, :], start=True, stop=False), reads=[Rt], writes=[self.RP[b]], inc=False)
            P.op("pe", lambda e, b=b, fc=fc: e.matmul(self.ps[b][:, 0:8], bdT[1][:, fc, :], wg[:, 16 + fc, :], start=False, stop=True), reads=[Rt], writes=[self.RP[b]])
            P.op("act", lambda e, b=b, fc=fc: e.copy(Gqk[:, fc, :], self.ps[b][:, 0:8]), reads=[self.RP[b]], writes=[Rk])
            b = self.bank()
            P.op("pe", lambda e, b=b, fc=fc: e.matmul(self.ps[b][:, 0:8], bdT[2][:, fc, :], wg[:, 32 + fc, :], start=True, stop=True), reads=[Rt], writes=[self.RP[b]])
            P.op("act", lambda e, b=b, fc=fc: e.copy(Gv[:, fc, :], self.ps[b][:, 0:8]), reads=[self.RP[b]], writes=[Rk])
        self.barrier()
        self.off = pbase
        xh = self.sb("xh", [128, 24], F32)
        gh = self.sb("gh", [128, 4, 24], F32)
        hnh = self.sb("hnh", [128, 8, 3], BF16)
        hacc = self.sb("hacc", [128, 24], F32)
        Rxh, Rgh, Rhnh = Res("xh"), Res("gh"), Res("hnh")
        P.op("dve", lambda e: e.tensor_copy(xh[:].rearrange("p (k t) -> p k t", t=3), hn[:, :, TOK - 3:TOK]),
             reads=[self.RN[k][NT - 1] for k in range(8)], writes=[Rxh])
        P.dma("sp", [(cci.ap(), xh[:])], reads=[Rxh], writes=[Rcci])
        P.op("pool", lambda e: e.collective_compute("AllGather", ALU.bypass, replica_groups=[[0, 1, 2, 3], [4, 5, 6, 7]],
                                                     ins=[cci.ap().opt()], outs=[cco.ap().opt()]), reads=[Rcci], writes=[Rcco])
        P.dma("sp", [(gh[:], cco.ap().rearrange("(r p) n -> p r n", p=128))], reads=[Rcco], writes=[Rgh])
        P.op("dve", lambda e: e.tensor_scalar(out=hacc[:], in0=gh[:, 0, :], scalar1=selp[:, 0:1], scalar2=None, op0=ALU.mult), reads=[Rgh, Rk], writes=[Rhnh])
        for r in range(1, 4):
            P.op("dve", lambda e, r=r: e.scalar_tensor_tensor(out=hacc[:], in0=gh[:, r, :], scalar=selp[:, r:r + 1], in1=hacc[:], op0=ALU.mult, op1=ALU.add),
                 reads=[Rgh, Rk], writes=[Rhnh])
        P.op("dve", lambda e: e.tensor_copy(hnh[:].rearrange("p k t -> p (k t)"), hacc[:]), reads=[Rhnh], writes=[Rhnh])
        xmb = [self.sb("xmb%d" % i, [128, 3 + TW], BF16) for i in range(4)]
        Rxm = [Res("xmb") for i in range(4)]
        cat = [self.sb("cat%d" % i, [128, TW], BF16) for i in range(4)]
        Rca = [Res("cat") for i in range(4)]
        Dj = [self.sb("Dj%d" % i, [128, 4, 128], BF16) for i in range(4)]
        RDj = [Res("Dj") for i in range(4)]

        def make_D(fc, i):
            for j in range(4):
                P.op("dve", lambda e, j=j: e.tensor_scalar(out=Dj[i][:, j, :], in0=self.ident_bf[:], scalar1=cw[:, fc, j:j + 1], scalar2=None, op0=ALU.mult),
                     reads=[Rk, self.Rconst], writes=[RDj[i]])

        def front(w, rw, fc, fcl, t, i):
            sl = slice(t * TW, (t + 1) * TW)
            if t == 0:
                b = self.bank()
                for k in range(8):
                    P.op("pe", lambda e, b=b, k=k: e.matmul(self.ps[b][:, 0:3], w[:, k, fcl * 128:(fcl + 1) * 128], hnh[:, k, :], start=(k == 0), stop=(k == 7)),
                         reads=[rw, Rhnh], writes=[self.RP[b]], inc=(k == 7))
                P.op("act", lambda e, b=b: e.copy(xmb[i][:, 0:3], self.ps[b][:, 0:3]), reads=[self.RP[b]], writes=[Rxm[i]])
            else:
                P.op("act", lambda e: e.copy(xmb[i][:, 0:3], xmb[i][:, TW:TW + 3]), reads=[Rxm[i]], writes=[Rxm[i]])
            b = self.bank()
            for k in range(8):
                P.op("pe", lambda e, b=b, k=k: e.matmul(self.ps[b][:], w[:, k, fcl * 128:(fcl + 1) * 128], hn[:, k, sl], start=(k == 0), stop=(k == 7)),
                     reads=[rw, self.RN[k][t]], writes=[self.RP[b]], inc=(k == 7))
            P.op("act", lambda e, b=b: e.copy(xmb[i][:, 3:3 + TW], self.ps[b][:]), reads=[self.RP[b]], writes=[Rxm[i]])
            b2 = self.bank()
            for j in range(4):
                P.op("pe", lambda e, b2=b2, j=j: e.matmul(self.ps[b2][:], Dj[i][:, j, :], xmb[i][:, j:j + TW], start=(j == 0), stop=(j == 3)),
                     reads=[RDj[i], Rxm[i]], writes=[self.RP[b2]], inc=(j == 3))
            P.op("act", lambda e, b2=b2: e.activation(cat[i][:], self.ps[b2][:], AF.Silu, bias=cb[:, fc:fc + 1]), reads=[self.RP[b2], Rk], writes=[Rca[i]])

        def fload(fc, fcl, t):
            P.dma("sp", [(xmb[fcl][:], dxm.ap()[fc, :, t, :]), (cat[fcl][:], dca.ap()[fc, :, t, :])], reads=[Rd[fc]], writes=[Rxm[fcl], Rca[fcl]])

        self.barrier()
        self.rot = [0, 1, 2, 3]
        gtmp0 = self.off
        g8T = self.sb("g8T", [8, TOK], F32)
        identf = self.sb("identf", [128, 128], F32)
        Rg8 = Res("g8T")
        Ridf = Res("identf")
        P.dma("sp", [(identf[:], self.din("ident", [128, 128]))], writes=[Ridf])
        for hd in range(4):
            w, rw, sw = self.ring.need(wxd[hd], (8, 512))
            for fcl in range(4):
                fc = hd * 4 + fcl
                make_D(fc, fcl)
                for t in range(NT):
                    front(w, rw, fc, fcl, t, fcl)
                    P.dma("sp", [(dxm.ap()[fc, :, t, :], xmb[fcl][:]), (dca.ap()[fc, :, t, :], cat[fcl][:])], reads=[Rxm[fcl], Rca[fcl]], writes=[Rd[fc]])
                    gb = 4 + t
                    P.op("pe", lambda e, gb=gb, fc=fc, fcl=fcl: e.matmul(self.ps[gb][0:8, :], Gqk[:, fc, :], cat[fcl][:], start=(fc == 0), stop=False),
                         reads=[Rca[fcl], Rk], writes=[self.RP[gb]], inc=False)
                    P.op("pe", lambda e, gb=gb, fc=fc, fcl=fcl: e.matmul(self.ps[gb][0:8, :], Gv[:, fc, :], xmb[fcl][:, 3:3 + TW], start=False, stop=(fc == 15)),
                         reads=[Rxm[fcl], Rk], writes=[self.RP[gb]])
            self.ring.release(sw)
        for t in range(NT):
            P.op("act", lambda e, t=t: e.copy(g8T[:, t * TW:(t + 1) * TW], self.ps[4 + t][0:8, :]), reads=[self.RP[4 + t]], writes=[Rg8])
        bt = self.bank()
        for c in range(16):
            P.op("pe", lambda e, c=c: e.transpose(self.ps[bt][:, c * 8:(c + 1) * 8], g8T[:, c * 128:(c + 1) * 128], identf[0:8, 0:8]),
                 reads=[Rg8, Ridf], writes=[self.RP[bt]], inc=(c == 15))
        P.op("act", lambda e: e.copy(GT8[:].rearrange("p c g -> p (c g)"), self.ps[bt][:, 0:128]), reads=[self.RP[bt]], writes=[Rgt])
        self.barrier()
        self.off = gtmp0
        self.rot = list(range(8))
        for c in range(16):
            P.op("dve", lambda e, c=c: e.tensor_tensor(out=GT8[:, c, :], in0=GT8[:, c, :], in1=bgb[:], op=ALU.add), reads=[Rk], writes=[Rgt])
        gw = lambda k: G[k][:].rearrange("p c h -> p (c h)")
        P.op("dve", lambda e: e.tensor_copy(G["LI"][:], GT8[:, :, 0:4]), writes=[Rgt])
        P.op("act", lambda e: e.activation(G["LF"][:], GT8[:, :, 4:8], AF.Sigmoid), writes=[Rgt])
        P.op("act", lambda e: e.activation(gw("LF"), gw("LF"), AF.Ln), writes=[Rgt])
        b = self.bank()
        P.op("pe", lambda e, b=b: e.matmul(self.ps[b][:, 0:64], U[:], gw("LF"), start=True, stop=True), reads=[Rgt, Rk], writes=[self.RP[b]])
        P.op("act", lambda e, b=b: e.copy(gw("A"), self.ps[b][:, 0:64]), reads=[self.RP[b]], writes=[Rgt])
        b = self.bank()
        P.op("pe", lambda e, b=b: e.matmul(self.ps[b][:, 0:64], onesf[:], gw("LF"), start=True, stop=True), reads=[Rgt, Rk], writes=[self.RP[b]])
        P.op("act", lambda e, b=b: e.copy(gw("TOT"), self.ps[b][:, 0:64]), reads=[self.RP[b]], writes=[Rgt])
        for hd in range(4):
            P.op("dve", lambda e, hd=hd: e.tensor_tensor_scan(G["INC"][:, :, hd], ones16[:], G["TOT"][:, :, hd], 0.0, ALU.mult, ALU.add), reads=[Rk], writes=[Rgt])
        P.op("dve", lambda e: e.tensor_copy(ATOT[:], G["INC"][:, 15, :]), writes=[Rgt])
        P.op("dve", lambda e: e.tensor_tensor(out=gw("AG"), in0=gw("A"), in1=gw("INC"), op=ALU.add), writes=[Rgt])
        P.op("dve", lambda e: e.tensor_tensor(out=gw("AG"), in0=gw("AG"), in1=gw("TOT"), op=ALU.subtract), writes=[Rgt])
        P.op("dve", lambda e: e.tensor_tensor(out=gw("WL"), in0=gw("LI"), in1=gw("A"), op=ALU.subtract), writes=[Rgt])
        P.op("act", lambda e: e.activation(gw("TSw"), gw("WL"), AF.Exp, bias=lnsb[:]), reads=[Rk], writes=[Rgt])
        P.op("dve", lambda e: e.tensor_tensor(out=gw("WL"), in0=gw("WL"), in1=gw("TOT"), op=ALU.add), writes=[Rgt])
        P.op("act", lambda e: e.activation(gw("TSu"), gw("WL"), AF.Exp, bias=lnsb[:]), reads=[Rk], writes=[Rgt])
        P.op("dve", lambda e: e.tensor_tensor(out=gw("WL"), in0=gw("LI"), in1=gw("AG"), op=ALU.subtract), writes=[Rgt])
        for hd in range(4):
            P.op("dve", lambda e, hd=hd: e.tensor_scalar(out=G["WL"][:, :, hd], in0=G["WL"][:, :, hd], scalar1=ATOT[:, hd:hd + 1], scalar2=None, op0=ALU.add), writes=[Rgt])
        P.op("act", lambda e: e.activation(gw("TSg"), gw("WL"), AF.Exp, bias=lnsb[:]), reads=[Rk], writes=[Rgt])
        P.op("act", lambda e: e.activation(gw("TSea"), gw("A"), AF.Exp), writes=[Rgt])
        P.op("act", lambda e: e.activation(gw("DEC"), gw("TOT"), AF.Exp), writes=[Rgt])
        P.op("act", lambda e: e.activation(DTOT[:], ATOT[:], AF.Exp), writes=[Rgt])
        import os
        self.dbg("d_GT8", GT8[:].rearrange("p c g -> p (c g)"), [128, 128])
        for k_ in ["LI", "LF", "A", "TOT", "INC", "AG", "TSw", "TSu", "TSg", "TSea", "DEC"]:
            self.dbg("d_" + k_, G[k_][:].rearrange("p c h -> p (c h)"), [128, 64])
        if int(os.environ.get("KC_STOP", "9")) <= 1:
            self.barrier()
            self.off = base
            return
        khat = [self.sb("khat%d" % i, [128, TW], BF16) for i in range(4)]
        Rkh = [Res("khat") for i in range(4)]
        vtok = [self.sb("vtok%d" % i, [128, TW], BF16) for i in range(4)]
        Rvt = [Res("vtok") for i in range(4)]

        def kv_tok(hd, fcl, t, TS):
            fc = hd * 4 + fcl
            b = self.bank()
            for c4 in range(4):
                P.op("pe", lambda e, b=b, c4=c4: e.matmul(self.ps[b][:, c4 * 128:(c4 + 1) * 128], cat[fcl][:, c4 * 128:(c4 + 1) * 128], bdk[:, fc, :], start=True, stop=True),
                     reads=[Rca[fcl], Rk], writes=[self.RP[b]], inc=(c4 == 3))
            for c4 in range(4):
                c = t * 4 + c4
                P.op("dve", lambda e, b=b, c4=c4, c=c: e.tensor_scalar(out=khat[c4][:, fcl * 128:(fcl + 1) * 128], in0=self.ps[b][:, c4 * 128:(c4 + 1) * 128],
                                                                       scalar1=TS[:, c, hd:hd + 1], scalar2=None, op0=ALU.mult), reads=[self.RP[b], Rgt], writes=[Rkh[c4]])
            b = self.bank()
            for c4 in range(4):
                P.op("pe", lambda e, b=b, c4=c4: e.matmul(self.ps[b][:, c4 * 128:(c4 + 1) * 128], xmb[fcl][:, 3 + c4 * 128:3 + (c4 + 1) * 128], bdv[:, fc, :], start=True, stop=True),
                     reads=[Rxm[fcl], Rk], writes=[self.RP[b]], inc=(c4 == 3))
            for c4 in range(4):
                P.op("act", lambda e, b=b, c4=c4: e.copy(vtok[c4][:, fcl * 128:(fcl + 1) * 128], self.ps[b][:, c4 * 128:(c4 + 1) * 128]), reads=[self.RP[b]], writes=[Rvt[c4]])

        KSUB = int(os.environ.get("KC_SUB", "9"))
        xcbase = self.off
        xc = [self.sb("xc%d" % i, [128, 516], F32) for i in range(2)]
        Rxc = [Res("xc") for i in range(2)]
        nx = 0
        for hd in range(4):
            self.rot = [0, 1, 2]
            pc = [3, 4, 5, 6]
            pn = 7
            for t in range(NT):
                for fcl in range(4):
                    fload(hd * 4 + fcl, fcl, t)
                    kv_tok(hd, fcl, t, G["TSg"])
                for c4 in (range(4) if KSUB >= 2 else []):
                    c = t * 4 + c4
                    for dc in range(4):
                        P.op("pe", lambda e, c4=c4, dc=dc, c=c: e.matmul(self.ps[pc[dc]][:], khat[c4][:, dc * 128:(dc + 1) * 128], vtok[c4][:], start=(c == 0), stop=(c == 15)),
                             reads=[Rkh[c4], Rvt[c4]], writes=[self.RP[pc[dc]]], inc=False)
                    for dc in range(4):
                        P.op("pe", lambda e, c4=c4, dc=dc, c=c: e.matmul(self.ps[pn][:, 0:4], khat[c4][:, dc * 128:(dc + 1) * 128], E4[:, dc, :], start=(c == 0 and dc == 0), stop=(c == 15 and dc == 3)),
                             reads=[Rkh[c4], Rk], writes=[self.RP[pn]], inc=(dc == 3))
            for dc in (range(4) if KSUB >= 3 else []):
                i = nx % 2
                nx += 1
                P.op("dve", lambda e, dc=dc, i=i: e.tensor_copy(xc[i][:, 0:512], self.ps[pc[dc]][:]), reads=[self.RP[pc[dc]]], writes=[Rxc[i]])
                P.op("dve", lambda e, dc=dc, i=i: e.tensor_copy(xc[i][:, 512:513], self.ps[pn][:, dc:dc + 1]), reads=[self.RP[pn]], writes=[Rxc[i]])
                P.op("dve", lambda e, i=i, hd=hd: e.tensor_copy(xc[i][:, 513:514], DTOT[:, hd:hd + 1]), reads=[Rgt], writes=[Rxc[i]])
                P.op("dve", lambda e, i=i: e.tensor_copy(xc[i][:, 514:516], DTOT[:, 0:2]), reads=[Rgt], writes=[Rxc[i]])
                q2 = hd * 2 + dc // 2
                P.dma("sp", [(cci2[q2].ap()[:, (dc % 2) * 516:(dc % 2 + 1) * 516], xc[i][:])], reads=[Rxc[i]], writes=[Rcci2[q2]])
            for q2 in ([hd * 2, hd * 2 + 1] if KSUB >= 4 else []):
                P.op("pool", lambda e, q2=q2: e.collective_compute("AllGather", ALU.bypass, replica_groups=[[0, 1, 2, 3], [4, 5, 6, 7]],
                                                                    ins=[cci2[q2].ap().opt()], outs=[cco2[q2].ap().opt()]), reads=[Rcci2[q2]], writes=[Rcco2[q2]])
        self.rot = list(range(8))
        import os
        if int(os.environ.get("KC_STOP", "9")) <= 2:
            self.barrier()
            self.off = base
            return
        self.barrier()
        self.off = xcbase
        qt = [self.sb("qt%d" % i, [128, TW], BF16) for i in range(4)]
        Rq = [Res("qt") for i in range(4)]
        kt = [self.sb("kt%d" % i, [128, TW], BF16) for i in range(4)]
        Rkt = [Res("kt") for i in range(4)]
        Cf = [self.sb("Cf%d" % i, [128, 516], F32) for i in range(4)]
        RCf = [Res("Cf") for i in range(4)]
        Cb = [self.sb("Cb%d" % i, [128, 516], BF16) for i in range(4)]
        RCb = [Res("Cb") for i in range(4)]
        al0 = self.off
        stg = [self.sb("stg%d" % i, [128, 516], F32) for i in range(4)]
        Rstg = Res("stg")
        pa = self.sb("cpa", [128, 516], F32)
        pb_ = self.sb("cpb", [128, 516], F32)
        Rpa = Res("cpa")
        al1 = self.off
        self.off = al0
        sm = self.sb("sm", [128, 128], BF16)
        Rsm = Res("sm")
        dnm = self.sb("dnm", [128, 4], F32)
        Rdn = Res("dnm")
        hbv = self.sb("hbv", [128, TW], F32)
        Rhb = Res("hbv")
        st6 = self.sb("st6", [128, 6], F32)
        mvv = self.sb("mvv", [128, 2], F32)
        hnr = self.sb("hnr", [128, TW], BF16)
        Rhn = Res("hnr")
        hhT = self.sb("hhT", [128, 4, TW], BF16)
        Rhh = [Res("hhT") for i in range(4)]
        zs = self.sb("zs", [128, TW], F32)
        Rzs = Res("zs")
        t1 = self.sb("t1", [128, TW], F32)
        Rt1 = Res("t1")
        yv = [self.sb("cyv%d" % i, [128, TW], BF16) for i in range(4)]
        Ryv = [Res("cyv") for i in range(4)]
        self.off = max(self.off, al1)
        self.rot = [0, 1, 2, 3]
        for hd in range(4):
            covs = [cco2[hd * 2 + i_].ap().rearrange("(r p) n -> p r n", p=128) for i_ in range(2)]
            wz, rz, sz = self.ring.need(wzd[hd], (8, 512))
            wo, ro, so = self.ring.need(wod[hd], (4, 1024))
            self.barrier()
            for dc in range(4):
                cov = covs[dc // 2]
                P.dma("sp", [(stg[r][:], cov[:, r, (dc % 2) * 516:(dc % 2 + 1) * 516]) for r in range(4)], reads=[Rcco2[hd * 2 + dc // 2]], writes=[Rstg])
                P.op("dve", lambda e: e.scalar_tensor_tensor(out=pa[:], in0=stg[0][:], scalar=stg[1][:, 513:514], in1=stg[1][:], op0=ALU.mult, op1=ALU.add), reads=[Rstg], writes=[Rpa])
                P.op("dve", lambda e: e.scalar_tensor_tensor(out=pb_[:], in0=pa[:], scalar=stg[2][:, 513:514], in1=stg[2][:], op0=ALU.mult, op1=ALU.add), reads=[Rstg, Rpa], writes=[Rpa])
                P.op("dve", lambda e, dc=dc: e.tensor_scalar(out=Cf[dc][:], in0=stg[0][:], scalar1=sel[:, 1:2], scalar2=None, op0=ALU.mult), reads=[Rstg, Rk], writes=[RCf[dc]])
                P.op("dve", lambda e, dc=dc: e.scalar_tensor_tensor(out=Cf[dc][:], in0=pa[:], scalar=sel[:, 2:3], in1=Cf[dc][:], op0=ALU.mult, op1=ALU.add), reads=[Rpa, Rk], writes=[RCf[dc]])
                P.op("dve", lambda e, dc=dc: e.scalar_tensor_tensor(out=Cf[dc][:], in0=pb_[:], scalar=sel[:, 3:4], in1=Cf[dc][:], op0=ALU.mult, op1=ALU.add), reads=[Rpa, Rk], writes=[RCf[dc]])
                P.op("act", lambda e, dc=dc: e.copy(Cb[dc][:], Cf[dc][:]), reads=[RCf[dc]], writes=[RCb[dc]])
            self.barrier()
            def S1c(t):
                sl = slice(t * TW, (t + 1) * TW)
                for fcl in range(4):
                    fc = hd * 4 + fcl
                    fload(fc, fcl, t)
                    b = self.bank()
                    P.op("pe", lambda e, b=b, fc=fc, fcl=fcl: e.matmul(self.ps[b][:], bdq[:, fc, :], cat[fcl][:], start=True, stop=True), reads=[Rca[fcl], Rk], writes=[self.RP[b]])
                    P.op("act", lambda e, b=b, fcl=fcl: e.copy(qt[fcl][:], self.ps[b][:]), reads=[self.RP[b]], writes=[Rq[fcl]])
                    b = self.bank()
                    P.op("pe", lambda e, b=b, fc=fc, fcl=fcl: e.matmul(self.ps[b][:], bdk[:, fc, :], cat[fcl][:], start=True, stop=True), reads=[Rca[fcl], Rk], writes=[self.RP[b]])
                    P.op("act", lambda e, b=b, fcl=fcl: e.copy(kt[fcl][:], self.ps[b][:]), reads=[self.RP[b]], writes=[Rkt[fcl]])
                    kv_tok(hd, fcl, t, G["TSu"])

            def CHc(t):
                sl = slice(t * TW, (t + 1) * TW)
                pend = [None]
                for c4 in range(5):
                    if c4 < 4:
                        c = t * 4 + c4
                        cs = slice(c4 * 128, (c4 + 1) * 128)
                        b = self.bank()
                        for fcl in range(4):
                            P.op("pe", lambda e, b=b, fcl=fcl, cs=cs: e.matmul(self.ps[b][:, 0:128], kt[fcl][:, cs], qt[fcl][:, cs], start=(fcl == 0), stop=(fcl == 3)),
                                 reads=[Rkt[fcl], Rq[fcl]], writes=[self.RP[b]], inc=(fcl == 3))
                        P.op("dve", lambda e, b=b, c=c, hd=hd: e.scalar_tensor_tensor(out=sm[:], in0=self.ps[b][:, 0:128], scalar=G["TSw"][:, c, hd:hd + 1], in1=U[:], op0=ALU.mult, op1=ALU.mult),
                             reads=[self.RP[b], Rgt, Rk], writes=[Rsm])
                        b2 = 6 + (c % 2)
                        P.op("pe", lambda e, b2=b2, c4=c4: e.matmul(self.ps[b2][:], sm[:], vtok[c4][:], start=True, stop=False), reads=[Rsm, Rvt[c4]], writes=[self.RP[b2]], inc=False)
                        for fcl in range(4):
                            P.op("pe", lambda e, b2=b2, fcl=fcl, cs=cs: e.matmul(self.ps[b2][:], qt[fcl][:, cs], Cb[fcl][:, 0:512], start=False, stop=(fcl == 3)),
                                 reads=[Rq[fcl], RCb[fcl]], writes=[self.RP[b2]], inc=(fcl == 3))
                        b3 = 4 + (c % 2)
                        P.op("pe", lambda e, b3=b3: e.matmul(self.ps[b3][:, 0:1], sm[:], onec[:], start=True, stop=False), reads=[Rsm, Rk], writes=[self.RP[b3]], inc=False)
                        for fcl in range(4):
                            P.op("pe", lambda e, b3=b3, fcl=fcl, cs=cs: e.matmul(self.ps[b3][:, 0:1], qt[fcl][:, cs], Cb[fcl][:, 512:513], start=False, stop=(fcl == 3)),
                                 reads=[Rq[fcl], RCb[fcl]], writes=[self.RP[b3]], inc=(fcl == 3))
                        pass
                    if pend[0] is not None:
                        pend[0]()
                        pend[0] = None
                    hops = []
                    if c4 >= 1:
                        pc4 = c4 - 1
                        pc_ = t * 4 + pc4
                        pcs = slice(pc4 * 128, (pc4 + 1) * 128)
                        eb2 = 6 + (pc_ % 2)
                        eb3 = 4 + (pc_ % 2)
                        ea = G["TSea"][:, pc_, hd:hd + 1]
                        la = G["A"][:, pc_, hd:hd + 1]
                        hops.append(lambda eb3=eb3, ea=ea: P.op("act", lambda e: e.activation(dnm[:, 0:1], self.ps[eb3][:, 0:1], AF.Square, scale=ea),
                                                                reads=[self.RP[eb3], Rgt], writes=[Rdn]))
                        hops.append(lambda: P.op("pool", lambda e: e.tensor_scalar(out=dnm[:, 1:2], in0=dnm[:, 0:1], scalar1=1.0, scalar2=None, op0=ALU.max), reads=[Rdn], writes=[Rdn]))
                        hops.append(lambda: P.op("act", lambda e: e.activation(dnm[:, 2:3], dnm[:, 1:2], AF.Ln), reads=[Rdn], writes=[Rdn]))
                        hops.append(lambda la=la: P.op("act", lambda e: e.activation(dnm[:, 3:4], dnm[:, 2:3], AF.Exp, scale=-0.5, bias=la), reads=[Rdn, Rgt], writes=[Rdn]))
                        hops.append(lambda eb2=eb2: P.op("act", lambda e: e.activation(hbv[:], self.ps[eb2][:], AF.Identity, scale=dnm[:, 3:4]), reads=[self.RP[eb2], Rdn], writes=[Rhb]))
                        def _h():
                            P.op("dve", lambda e: e.bn_stats(st6[:], hbv[:]), reads=[Rhb], writes=[Rdn])
                        _h._dve = True
                        hops.append(_h)
                        hops.append(lambda: P.op("dve", lambda e: e.bn_aggr(mvv[:], st6[:]), reads=[Rdn], writes=[Rdn]))
                        hops.append(lambda: P.op("act", lambda e: e.activation(mvv[:, 1:2], mvv[:, 1:2], AF.Ln, bias=self.epsb[:]), reads=[Rdn, self.Rconst], writes=[Rdn]))
                        hops.append(lambda: P.op("act", lambda e: e.activation(mvv[:, 1:2], mvv[:, 1:2], AF.Exp, scale=-0.5), reads=[Rdn], writes=[Rdn]))
                        hops.append(lambda: P.op("dve", lambda e: e.tensor_scalar(out=hnr[:], in0=hbv[:], scalar1=mvv[:, 0:1], scalar2=mvv[:, 1:2], op0=ALU.subtract, op1=ALU.mult),
                                                 reads=[Rhb, Rdn], writes=[Rhn]))

                        def tr_hop(pcs=pcs):
                            b4 = self.bank()
                            psb = self.ps[b4][:].bitcast(BF16)
                            for fcl in range(4):
                                P.op("pe", lambda e, fcl=fcl, psb=psb: e.transpose(psb[:, fcl * 128:(fcl + 1) * 128], hnr[:, fcl * 128:(fcl + 1) * 128], self.ident_bf[:]),
                                     reads=[Rhn, self.Rconst], writes=[self.RP[b4]], inc=(fcl == 3))
                            for fcl in range(4):
                                P.op("act", lambda e, fcl=fcl, psb=psb, pcs=pcs: e.copy(hhT[:, fcl, pcs], psb[:, fcl * 128:(fcl + 1) * 128]), reads=[self.RP[b4]], writes=[Rhh[fcl]])
                        pend[0] = tr_hop
                    if c4 < 4:
                        dec = G["DEC"][:, c, hd:hd + 1]
                        for dc in range(4):
                            b5 = self.bank()
                            P.op("pe", lambda e, b5=b5, dc=dc, c4=c4: e.matmul(self.ps[b5][:], khat[c4][:, dc * 128:(dc + 1) * 128], vtok[c4][:], start=True, stop=True),
                                 reads=[Rkh[c4], Rvt[c4]], writes=[self.RP[b5]])
                            b6 = self.bank()
                            P.op("pe", lambda e, b6=b6, dc=dc, c4=c4: e.matmul(self.ps[b6][:, 0:1], khat[c4][:, dc * 128:(dc + 1) * 128], onec[:], start=True, stop=True),
                                 reads=[Rkh[c4], Rk], writes=[self.RP[b6]])
                            P.op("dve", lambda e, b5=b5, dc=dc, dec=dec: e.scalar_tensor_tensor(out=Cf[dc][:, 0:512], in0=Cf[dc][:, 0:512], scalar=dec, in1=self.ps[b5][:], op0=ALU.mult, op1=ALU.add),
                                 reads=[self.RP[b5], Rgt], writes=[RCf[dc]])
                            P.op("dve", lambda e, b6=b6, dc=dc, dec=dec: e.scalar_tensor_tensor(out=Cf[dc][:, 512:513], in0=Cf[dc][:, 512:513], scalar=dec, in1=self.ps[b6][:, 0:1], op0=ALU.mult, op1=ALU.add),
                                 reads=[self.RP[b6], Rgt], writes=[RCf[dc]])
                            P.op("act", lambda e, dc=dc: e.copy(Cb[dc][:, 0:513], Cf[dc][:, 0:513]), reads=[RCf[dc]], writes=[RCb[dc]])
                            for _ in range(3):
                                if hops and not getattr(hops[0], "_dve", False):
                                    hops.pop(0)()
                    while hops:
                        hops.pop(0)()
                if pend[0] is not None:
                    pend[0]()
                    pend[0] = None
            def OUTc(t):
                sl = slice(t * TW, (t + 1) * TW)
                for fcl in range(4):
                    fc = hd * 4 + fcl
                    b = self.bank()
                    for k in range(8):
                        P.op("pe", lambda e, b=b, k=k, fcl=fcl, sl=sl, wz=wz: e.matmul(self.ps[b][:], wz[:, k, fcl * 128:(fcl + 1) * 128], hn[:, k, sl], start=(k == 0), stop=(k == 7)),
                             reads=[rz, self.RN[k][t]], writes=[self.RP[b]], inc=(k == 7))
                    P.op("act", lambda e, b=b: e.activation(zs[:], self.ps[b][:], AF.Silu), reads=[self.RP[b]], writes=[Rzs])
                    tb, Rtb = (t1, Rt1) if fcl % 2 == 0 else (hbv, Rhb)
                    P.dma("pool", [(tb[:], dca.ap()[fc, :, t, :])], reads=[Rd[fc]], writes=[Rtb])
                    P.op("dve", lambda e, tb=tb, fc=fc: e.tensor_scalar(out=tb[:], in0=tb[:], scalar1=csk[:, fc:fc + 1], scalar2=None, op0=ALU.mult),
                         reads=[Rk], writes=[Rtb])
                    P.op("dve", lambda e, tb=tb, fcl=fcl, fc=fc: e.scalar_tensor_tensor(out=tb[:], in0=hhT[:, fcl, :], scalar=cng[:, fc:fc + 1], in1=tb[:], op0=ALU.mult, op1=ALU.add),
                         reads=[Rhh[fcl], Rk], writes=[Rtb])
                    P.op("pool", lambda e, tb=tb, fcl=fcl: e.tensor_tensor(out=yv[fcl][:], in0=tb[:], in1=zs[:], op=ALU.mult), reads=[Rtb, Rzs], writes=[Ryv[fcl]])
                for mo in range(8):
                    b = self.bank()
                    for k in range(4):
                        P.op("pe", lambda e, b=b, k=k, mo=mo, wo=wo: e.matmul(self.ps[b][:], wo[:, k, mo * 128:(mo + 1) * 128], yv[k][:], start=(k == 0), stop=(k == 3)),
                             reads=[ro, Ryv[k]], writes=[self.RP[b]], inc=(k == 3))
                    P.op("dve", lambda e, b=b, mo=mo, sl=sl: e.tensor_tensor(out=hT[:, mo, sl], in0=self.ps[b][:], in1=hT[:, mo, sl], op=ALU.add),
                         reads=[self.RP[b], self.RH[mo][t]], writes=[self.RH[mo][t]])
            S1c(0)
            for t in range(NT):
                CHc(t)
                if t + 1 < NT:
                    S1c(t + 1)
                OUTc(t)
            self.ring.release(sz)
            self.ring.release(so)
        self.rot = list(range(8))
        self.barrier()
        self.off = base

    def dout(self):
        if "yT" not in self.dram:
            self.dram["yT"] = self.nc.dram_tensor("yT", [1024, TOK], F32, kind="ExternalOutput").ap()
        return self.dram["yT"]

    def final(self):
        self.rmsnorm(8, out_final=self.dout())

    def store_h(self):
        yT = self.dout()
        ov = yT.rearrange("(c p) t -> p c t", p=128)
        for t in range(NT):
            sl = slice(t * TW, (t + 1) * TW)
            self.P.dma("sp", [(ov[:, :, sl], self.hT[:, :, sl])], reads=[self.RH[c][t] for c in range(8)], is_output=True)

    def run_stages(self):
        for s in self.stages:
            if s[0] == "F":
                self.ffn(int(s[1:]))
            elif s[0] == "A":
                self.mixer_a(int(s[1]), int(s[2]))
            elif s[0] == "B":
                self.mixer_b(int(s[1]))
            elif s[0] == "C":
                self.mixer_c(int(s[1]))
            elif s == "N":
                self.final()
            elif s == "S":
                self.store_h()
            else:
                raise ValueError(s)

    def build(self):
        nc, P = self.nc, self.P
        with ExitStack() as st:
            self.setup(st)
            self.epsb = self.sb("epsb", [128, 1], F32)
            self.oneb = self.sb("oneb", [128, 1], F32)
            self.arena0 = self.off
            P.dry = True
            self.run_stages()
            P.dry = False
            self.off = self.arena0
            self.rp = 0
            self.ring.reset()
            self.load_consts()
            P.op("pool", lambda e: e.memset(self.epsb[:], EPS), writes=[self.Rconst])
            P.op("pool", lambda e: e.memset(self.oneb[:], 1.0), writes=[self.Rconst])
            self.load_x()
            self.run_stages()
            P.finish("sp")
            P.emit()
        return nc


def _common_inputs(inp):
    f = lambda a: np.ascontiguousarray(np.asarray(a, dtype=np.float32))
    d = {}
    d["ident"] = np.eye(128, dtype=np.float32)
    g = np.concatenate([inp["mix_norm_g"], inp["ffn_norm_g"], inp["final_norm_g"][None]], axis=0)
    d["gains"] = f(g.reshape(9, 8, 128).transpose(2, 0, 1))
    d["cmask"] = np.triu(np.ones((128, 128), np.float32))
    for j in range(2):
        wi = inp["a_w_in"][j]
        d["a_wu_%d" % j] = f(wi[:, :2048].reshape(8, 128, 4, 512).transpose(2, 1, 0, 3))
        d["a_wv_%d" % j] = f(wi[:, 2048:].reshape(8, 128, 4, 512).transpose(2, 1, 0, 3))
        d["a_wo_%d" % j] = f(inp["a_w_out"][j].reshape(4, 4, 128, 1024).transpose(0, 2, 1, 3))
        d["a_wsT_%d" % j] = f(inp["a_ws"][j].transpose(2, 0, 1))
        d["a_bs_%d" % j] = f(inp["a_bs"][j].reshape(1, 2048))
        d["a_lng_%d" % j] = f(inp["a_ln_g"][j].reshape(16, 128).T)
        d["a_lnb_%d" % j] = f(inp["a_ln_b"][j].reshape(16, 128).T)
    bw = inp["b_w_in"][0]
    d["b_win"] = f(bw.reshape(8, 128, 4, 8, 128).transpose(3, 1, 0, 2, 4).reshape(8, 128, 8, 512))
    d["b_wout"] = f(inp["b_w_out"][0].reshape(2, 4, 128, 1024).transpose(0, 2, 1, 3))
    d["b_lbl"] = f(inp["hgrn_lb_logits"].reshape(4, 8, 128).transpose(2, 0, 1))
    d["b_ng"] = f(inp["b_norm_g"][0].reshape(8, 128).T)
    ii = np.arange(128)
    bm = ((ii[:, None] <= ii[None, :]) & ((ii[:, None] // 64) == (ii[None, :] // 64))).astype(np.float32)
    d["bmask4"] = f(np.tile(bm, (1, 4)))
    cwi = inp["c_w_in"][0]
    d["c_wxm"] = f(cwi[:, :2048].reshape(8, 128, 4, 512).transpose(2, 1, 0, 3))
    d["c_wz"] = f(cwi[:, 2048:].reshape(8, 128, 4, 512).transpose(2, 1, 0, 3))
    d["c_wo"] = f(inp["c_w_out"][0].reshape(4, 4, 128, 1024).transpose(0, 2, 1, 3))
    d["c_cw"] = f(inp["c_conv_w"][0].reshape(4, 16, 128).transpose(2, 1, 0))
    d["c_cb"] = f(inp["c_conv_b"][0].reshape(16, 128).T)
    d["c_ng"] = f(inp["c_norm_g"][0].reshape(16, 128).T)
    d["c_skip"] = f(inp["c_skip"][0].reshape(16, 128).T)
    d["c_bg"] = f(inp["c_b_gate"][0].reshape(1, 8))
    d["c_wg"] = f(inp["c_w_gate"][0].reshape(48, 128, 8).transpose(1, 0, 2))
    e4 = np.zeros((128, 4, 4), np.float32)
    for j in range(4):
        e4[:, j, j] = 1.0
    d["c_E4"] = e4
    for nm, key in [("q", "c_wq"), ("k", "c_wk"), ("v", "c_wv")]:
        wb = inp[key][0]
        bd = np.zeros((16, 32, 4, 32, 4), np.float32)
        wr = wb.reshape(16, 32, 4, 4)
        for n_ in range(32):
            bd[:, n_, :, n_, :] = wr[:, n_]
        bd = bd.reshape(16, 128, 128)
        d["c_bd%s" % nm] = f(bd.transpose(1, 0, 2))
        d["c_bd%sT" % nm] = f(bd.transpose(2, 0, 1))
    for l in range(4):
        w1 = inp["ffn_w1"][l]
        d["w1_%d" % l] = f(w1.reshape(8, 128, 8, 512).transpose(2, 1, 0, 3))
        w2 = inp["ffn_w2"][l]
        d["w2_%d" % l] = f(w2.reshape(8, 4, 128, 1024).transpose(0, 2, 1, 3))
    return d


def _run(stages, inp, hin):
    bld = Builder(stages)
    nc = bld.build()
    com = _common_inputs(inp)
    used = set(bld.dram.keys()) - set(getattr(bld, 'dbgnames', [])) - {'yT', 'b_ccin', 'b_ccout', 'c_ccin', 'c_ccout', 'c_ccin2', 'c_ccout2', 'c_dxm', 'c_dca', 'a_dgv', 'b_dsg', 'b_dit'} - {'c_ccin2_%d' % i for i in range(8)} - {'c_ccout2_%d' % i for i in range(8)}
    in_maps = []
    for c in range(8):
        b, sg = divmod(c, 4)
        m = {k: v for k, v in com.items() if k in used}
        m["xT"] = np.ascontiguousarray(hin[b, sg * TOK:(sg + 1) * TOK, :].T)
        if "sel" in used:
            selv = np.zeros((128, 4), np.float32)
            selv[:, sg] = 1.0
            m["sel"] = selv
        if "selp" in used:
            selv = np.zeros((128, 4), np.float32)
            if sg > 0:
                selv[:, sg - 1] = 1.0
            m["selp"] = selv
        in_maps.append(m)
    import os
    if os.environ.get("KTRACE"):
        res = run_bass_kernel_spmd(nc, in_maps, core_ids=list(range(8)), trace=True)
        print("EXEC_NS", res.exec_time_ns)
    else:
        res = run_bass_kernel_spmd(nc, in_maps, core_ids=list(range(8)))
    _run.dbg = [{n: res.results[c][n] for n in getattr(bld, "dbgnames", [])} for c in range(8)]
    out = np.empty((2, 8192, 1024), np.float32)
    for c in range(8):
        b, sg = divmod(c, 4)
        out[b, sg * TOK:(sg + 1) * TOK, :] = res.results[c]["yT"].T
    return out


def kernel(**inputs):
    inp = {k: np.asarray(v) for k, v in inputs.items()}
    return _run(["A00", "F0", "B1", "F1", "C2", "F2", "A31", "F3", "N"], inp, inp["x"])
```

```python
import numpy as np
from contextlib import ExitStack
import concourse.bass as bass
import concourse.mybir as mybir
from concourse.bass_utils import run_bass_kernel_spmd

F32 = mybir.dt.float32
BF16 = mybir.dt.bfloat16
AF = mybir.ActivationFunctionType
ALU = mybir.AluOpType
ENGS = ("pe", "act", "dve", "pool", "sp")
EPS = 1e-6
TOK = 2048
NT = 4
TW = 512
NCH = 16


class Res:
    __slots__ = ("w", "r", "name", "dsem")

    def __init__(self, name=""):
        self.w = {}
        self.r = {}
        self.name = name
        self.dsem = None


class Prog:
    def __init__(self, nc):
        self.nc = nc
        self.dry = False
        self.streams = {e: [] for e in ENGS}
        self.semnames = []
        self.engsem = {e: self.newsem("p_" + e) for e in ENGS}
        self.cnt = {e: 0 for e in ENGS}
        self.seen = {e: {} for e in ENGS}
        self.dmatot = {}
        self.out_events = {}

    def newsem(self, name):
        name = "%s_%d" % (name, len(self.semnames))
        self.semnames.append(name)
        return len(self.semnames) - 1

    def _waits(self, eng, reads, writes):
        need = {}
        for r in reads:
            for s, v in r.w.items():
                if need.get(s, 0) < v:
                    need[s] = v
        for w in writes:
            for s, v in w.w.items():
                if need.get(s, 0) < v:
                    need[s] = v
            for s, v in w.r.items():
                if need.get(s, 0) < v:
                    need[s] = v
        seen = self.seen[eng]
        own = self.engsem[eng]
        waits = []
        for s, v in need.items():
            if s == own:
                if eng == "pe":
                    continue
                if self.cnt[eng] - v >= 4:
                    continue
            if seen.get(s, 0) < v:
                waits.append((s, v))
                seen[s] = v
        return waits

    def op(self, eng, fn, reads=(), writes=(), inc=True):
        if self.dry:
            return
        waits = self._waits(eng, reads, writes)
        own = self.engsem[eng]
        if inc:
            self.cnt[eng] += 1
            ev = self.cnt[eng]
        else:
            ev = self.cnt[eng] + 1
        self.streams[eng].append((waits, fn, own if inc else None, 1))
        for r in reads:
            if r.r.get(own, 0) < ev:
                r.r[own] = ev
        for w in writes:
            w.w = {own: ev}
            w.r = {}

    def dma(self, queue, pairs, reads=(), writes=(), is_output=False):
        if self.dry:
            return
        tgt = writes[0] if writes else reads[0]
        if tgt.dsem is None:
            tgt.dsem = {}
        if queue not in tgt.dsem:
            tgt.dsem[queue] = self.newsem("d_%s_%s" % (tgt.name, queue))
        sem = tgt.dsem[queue]
        waits = self._waits(queue, reads, writes)
        for i, (o, a) in enumerate(pairs):
            self.dmatot[sem] = self.dmatot.get(sem, 0) + 16
            self.streams[queue].append(
                (waits if i == 0 else [], (lambda e, o=o, a=a: e.dma_start(out=o, in_=a)), sem, 16))
        ev = self.dmatot[sem]
        for r in reads:
            if r.r.get(sem, 0) < ev:
                r.r[sem] = ev
        for w in writes:
            w.w = {sem: ev}
            w.r = {}
        if is_output:
            self.out_events[sem] = ev

    def finish(self, eng="sp"):
        waits = [(s, v) for s, v in self.out_events.items()]
        for e in ENGS:
            if e != eng and self.cnt[e] > 0:
                waits.append((self.engsem[e], self.cnt[e]))
        self.streams[eng].append((waits, None, None, 0))

    def emit(self):
        nc = self.nc
        with ExitStack() as st:
            sems = [st.enter_context(nc.semaphore(n)) for n in self.semnames]
            block = st.enter_context(nc.Block())

            def run(engname, e):
                for waits, fn, isem, amt in self.streams[engname]:
                    for s, v in waits:
                        e.wait_ge(sems[s], v)
                    if fn is None:
                        continue
                    ins = fn(e)
                    if isem is not None:
                        ins.then_inc(sems[isem], amt)

            @block.tensor
            def _(e):
                run("pe", e)

            @block.scalar
            def _(e):
                run("act", e)

            @block.vector
            def _(e):
                run("dve", e)

            @block.gpsimd
            def _(e):
                run("pool", e)

            @block.sync
            def _(e):
                run("sp", e)


class WRing:
    def __init__(self, P, slots):
        self.P = P
        self.slots = slots
        self.res = [Res("ws%d" % i) for i in range(len(slots))]
        self.plan = []
        self.reset()

    def reset(self):
        self.free = list(range(len(self.slots)))
        self.nneed = 0
        self.nissue = 0
        self.slot_of = {}

    def _pump(self):
        while self.free and self.nissue < len(self.plan):
            s = self.free.pop(0)
            src, shape = self.plan[self.nissue]
            view = self._view(s, shape)
            self.P.dma("pool", [(view, src)], writes=[self.res[s]])
            self.slot_of[self.nissue] = s
            self.nissue += 1

    def _view(self, s, shape):
        t = self.slots[s]
        a, b = shape
        return t[:, 0:a * b].rearrange("p (a b) -> p a b", a=a)

    def need(self, src, shape):
        if self.P.dry:
            self.plan.append((src, shape))
            return self._view(0, shape), self.res[0], None
        idx = self.nneed
        self.nneed += 1
        self._pump()
        s = self.slot_of[idx]
        return self._view(s, shape), self.res[s], s

    def release(self, h):
        if self.P.dry:
            return
        self.free.append(h)
        self._pump()


class Builder:
    def __init__(self, stages, nslots=4):
        self.stages = stages
        self.nc = bass.Bass("TRN2", target_bir_lowering=False)
        self.P = Prog(self.nc)
        self.dram = {}
        self.off = 16512
        self.nslots = nslots

    def din(self, name, shape, dt=F32):
        if name not in self.dram:
            self.dram[name] = self.nc.dram_tensor(name, list(shape), dt, kind="ExternalInput").ap()
        return self.dram[name]

    def sb(self, name, shape, dt):
        n = 1
        for s in shape[1:]:
            n *= s
        nbytes = n * (4 if dt == F32 else 2)
        nbytes = (nbytes + 63) // 64 * 64
        self.uid = getattr(self, "uid", 0) + 1
        t = self.nc.alloc_sbuf_tensor_at("%s_%d" % (name, self.uid), list(shape), dt, offset=self.off)
        self.off += nbytes
        assert self.off <= 229344, ("SBUF overflow", name, self.off)
        return t

    def dbg(self, name, ap, shape, res=()):
        import os
        if not os.environ.get("KDBG"):
            return
        if name not in self.dram:
            self.dram[name] = self.nc.dram_tensor(name, list(shape), F32, kind="ExternalOutput").ap()
            self.dbgnames = getattr(self, "dbgnames", []) + [name]
        if self.P.dry:
            return
        self.barrier()
        r = Res("dbg_" + name)
        self.P.dma("pool", [(self.dram[name], ap)], reads=list(res) + [r], is_output=True)

    def bank(self):
        b = self.rot[self.rp % len(self.rot)]
        self.rp += 1
        return b

    def setup(self, st):
        nc, P = self.nc, self.P
        self.hT = self.sb("hT", [128, 8, TOK], F32)
        self.hn = self.sb("hn", [128, 8, TOK], BF16)
        self.RH = [[Res("hT%d_%d" % (c, t)) for t in range(NT)] for c in range(8)]
        self.RN = [[Res("hn%d_%d" % (c, t)) for t in range(NT)] for c in range(8)]
        self.ring = WRing(P, [self.sb("wslot%d" % i, [128, 4096], BF16) for i in range(self.nslots)])
        self.ps = [st.enter_context(nc.psum_tensor("ps%d" % i, [128, 512], F32)) for i in range(8)]
        self.RP = [Res("ps%d" % i) for i in range(8)]
        self.rot = list(range(8))
        self.rp = 0
        self.ones_bf = self.sb("ones_bf", [128, 128], BF16)
        self.ident_bf = self.sb("ident_bf", [128, 128], BF16)
        self.Rconst = Res("const")
        self.gains = self.sb("gains", [128, 9, 8], F32)
        self.Rgains = Res("gains")
        self.arena0 = self.off

    def load_consts(self):
        P = self.P
        P.op("pool", lambda e: e.memset(self.ones_bf[:], 1.0), writes=[self.Rconst])
        identd = self.din("ident", [128, 128])
        P.dma("pool", [(self.ident_bf[:], identd)], writes=[self.Rconst])
        P.dma("sp", [(self.gains[:], self.din("gains", [128, 9, 8]))], writes=[self.Rgains])

    def barrier(self):
        P = self.P
        if P.dry:
            return
        tot = [(P.engsem[e], P.cnt[e]) for e in ENGS if P.cnt[e] > 0]
        for e in ENGS:
            waits = []
            for s, v in tot:
                if P.seen[e].get(s, 0) < v and not (s == P.engsem[e]):
                    waits.append((s, v))
                    P.seen[e][s] = v
            if waits:
                P.streams[e].append((waits, None, None, 0))

    def load_x(self):
        xT = self.din("xT", [1024, TOK])
        v = xT.rearrange("(c p) t -> p c t", p=128)
        for t in range(NT):
            sl = slice(t * TW, (t + 1) * TW)
            self.P.dma("sp", [(self.hT[:, :, sl], v[:, :, sl])], writes=[self.RH[c][t] for c in range(8)])

    def rmsnorm(self, gi, out_final=None, keep=False):
        P = self.P
        base = self.off
        sqb = [self.sb("sqb%d" % i, [128, 8, TW], BF16) for i in range(2)]
        Rsq = [[Res("sq") for c in range(8)] for i in range(2)]
        rs = [self.sb("rs%d" % i, [128, TW], F32) for i in range(2)]
        Rrs = [Res("rs") for i in range(2)]
        rstd = [self.sb("rstd%d" % i, [128, TW], F32) for i in range(2)]
        Rrstd = [Res("rstd") for i in range(2)]
        if out_final is not None:
            ost = [self.sb("ost%d" % i, [128, 8, TW], F32) for i in range(2)]
            Rost = [Res("ost") for i in range(2)]
        hT, hn = self.hT, self.hn
        for t in range(NT):
            sl = slice(t * TW, (t + 1) * TW)
            i = t % 2
            for c in range(8):
                P.op("act", lambda e, c=c, i=i, sl=sl: e.activation(sqb[i][:, c, :], hT[:, c, sl], AF.Square),
                     reads=[self.RH[c][t]], writes=[Rsq[i][c]])
            b = self.bank()
            for c in range(8):
                P.op("pe", lambda e, c=c, i=i, b=b: e.matmul(self.ps[b][:], self.ones_bf[:], sqb[i][:, c, :], start=(c == 0), stop=(c == 7)),
                     reads=[Rsq[i][c], self.Rconst], writes=[self.RP[b]], inc=(c == 7))
            P.op("act", lambda e, i=i, b=b: e.activation(rs[i][:], self.ps[b][:], AF.Ln, bias=self.epsb[:], scale=1.0 / 1024.0),
                 reads=[self.RP[b], self.Rconst], writes=[Rrs[i]])
            P.op("act", lambda e, i=i: e.activation(rstd[i][:], rs[i][:], AF.Exp, scale=-0.5), reads=[Rrs[i]], writes=[Rrstd[i]])
            for c in range(8):
                if out_final is None:
                    P.op("dve", lambda e, c=c, i=i, sl=sl: e.scalar_tensor_tensor(
                        out=hn[:, c, sl], in0=hT[:, c, sl], scalar=self.gains[:, gi, c:c + 1], in1=rstd[i][:],
                        op0=ALU.mult, op1=ALU.mult),
                        reads=[self.RH[c][t], Rrstd[i], self.Rgains], writes=[self.RN[c][t]])
                else:
                    P.op("dve", lambda e, c=c, i=i, sl=sl: e.scalar_tensor_tensor(
                        out=ost[i][:, c, :], in0=hT[:, c, sl], scalar=self.gains[:, gi, c:c + 1], in1=rstd[i][:],
                        op0=ALU.mult, op1=ALU.mult),
                        reads=[self.RH[c][t], Rrstd[i], self.Rgains], writes=[Rost[i]])
            if out_final is not None:
                ov = out_final.rearrange("(c p) t -> p c t", p=128)
                P.dma("sp", [(ov[:, :, sl], ost[i][:])], reads=[Rost[i]], is_output=True)
        if keep:
            return
        self.barrier()
        self.off = base

    def ffn(self, l):
        P = self.P
        base = self.off
        self.rmsnorm(4 + l, keep=True)
        w1d = self.din("w1_%d" % l, [8, 128, 8, 512])
        w2d = self.din("w2_%d" % l, [8, 128, 4, 1024])
        h1 = [self.sb("h1_%d" % i, [128, 4, TW], BF16) for i in range(2)]
        Rh1 = [[Res("h1") for m in range(4)] for i in range(2)]
        rl = [self.sb("rl%d" % i, [128, TW], F32) for i in range(2)]
        Rrl = [Res("rl") for i in range(2)]
        hT, hn = self.hT, self.hn
        n = 0
        for j in range(8):
            w1, r1, s1 = self.ring.need(w1d[j], (8, 512))
            w2, r2, s2 = self.ring.need(w2d[j], (4, 1024))
            for t in range(NT):
                sl = slice(t * TW, (t + 1) * TW)
                i = (j * NT + t) % 2
                for m in range(4):
                    b = self.bank()
                    for k in range(8):
                        P.op("pe", lambda e, b=b, k=k, m=m, sl=sl, w1=w1: e.matmul(
                            self.ps[b][:], w1[:, k, m * 128:(m + 1) * 128], hn[:, k, sl], start=(k == 0), stop=(k == 7)),
                            reads=[r1, self.RN[k][t]], writes=[self.RP[b]], inc=(k == 7))
                    q = n % 2
                    n += 1
                    P.op("act", lambda e, b=b, q=q: e.activation(rl[q][:], self.ps[b][:], AF.Relu),
                         reads=[self.RP[b]], writes=[Rrl[q]])
                    P.op("act", lambda e, q=q, i=i, m=m: e.activation(h1[i][:, m, :], rl[q][:], AF.Square),
                         reads=[Rrl[q]], writes=[Rh1[i][m]])
                for mo in range(8):
                    b = self.bank()
                    for k in range(4):
                        P.op("pe", lambda e, b=b, k=k, mo=mo, i=i, w2=w2: e.matmul(
                            self.ps[b][:], w2[:, k, mo * 128:(mo + 1) * 128], h1[i][:, k, :], start=(k == 0), stop=(k == 3)),
                            reads=[r2, Rh1[i][k]], writes=[self.RP[b]], inc=(k == 3))
                    P.op("dve", lambda e, b=b, mo=mo, sl=sl: e.tensor_tensor(
                        out=hT[:, mo, sl], in0=self.ps[b][:], in1=hT[:, mo, sl], op=ALU.add),
                        reads=[self.RP[b], self.RH[mo][t]], writes=[self.RH[mo][t]])
            self.ring.release(s1)
            self.ring.release(s2)
        self.barrier()
        self.off = base


    def mixer_a(self, l, j):
        P = self.P
        self.rmsnorm(l)
        base = self.off
        hT, hn = self.hT, self.hn
        wud = self.din("a_wu_%d" % j, [4, 128, 8, 512])
        wvd = self.din("a_wv_%d" % j, [4, 128, 8, 512])
        wod = self.din("a_wo_%d" % j, [4, 128, 4, 1024])
        wsd = self.din("a_wsT_%d" % j, [128, 16, 128])
        bsd = self.din("a_bs_%d" % j, [1, 2048])
        lgd = self.din("a_lng_%d" % j, [128, 16])
        lbd = self.din("a_lnb_%d" % j, [128, 16])
        cmd = self.din("cmask", [128, 128])
        if "a_dgv" not in self.dram:
            self.dram["a_dgv"] = self.nc.dram_tensor("a_dgv", [16, 128, 2048], F32)
        dgv = self.dram["a_dgv"]
        wsf = self.sb("wsf", [128, 16, 128], F32)
        wsb = self.sb("wsb", [128, 16, 128], BF16)
        bsb = self.sb("bsb", [128, 16, 128], F32)
        extra = self.sb("extra", [128, 16, 128], F32)
        lng = self.sb("lng", [128, 16], F32)
        lnb = self.sb("lnb", [128, 16], F32)
        cm = self.sb("cm", [128, 128], F32)
        onesf = self.sb("onesf", [128, 128], F32)
        Rw, Rwb, Rbs, Rex, Rln, Rcm = Res("wsf"), Res("wsb"), Res("bsb"), Res("extra"), Res("ln"), Res("cm")
        P.dma("sp", [(wsf[:], wsd)], writes=[Rw])
        P.dma("sp", [(bsb[:].rearrange("p g t -> p (g t)"), bsd.partition_broadcast(128).rearrange("p o n -> p (o n)"))], writes=[Rbs])
        P.dma("sp", [(lng[:], lgd), (lnb[:], lbd)], writes=[Rln])
        P.dma("sp", [(cm[:], cmd)], writes=[Rcm])
        P.op("pool", lambda e: e.memset(onesf[:], 1.0), writes=[Rcm])
        for g in range(16):
            P.op("pool", lambda e, g=g: e.tensor_tensor(out=wsf[:, g, :], in0=wsf[:, g, :], in1=cm[:], op=ALU.mult),
                 reads=[Rcm], writes=[Rw])
        P.op("pool", lambda e: e.tensor_copy(wsb[:], wsf[:]), reads=[Rw], writes=[Rwb])
        for q in range(4):
            b = self.bank()
            P.op("pe", lambda e, b=b, q=q: e.matmul(self.ps[b][:], onesf[:], wsf[:, q * 4:(q + 1) * 4, :].rearrange("p g t -> p (g t)"),
                                                      start=True, stop=True),
                 reads=[Rw, Rcm], writes=[self.RP[b]])
            for gi in range(4):
                g = q * 4 + gi
                P.op("dve", lambda e, b=b, g=g, gi=gi: e.scalar_tensor_tensor(
                    out=extra[:, g, :], in0=self.ps[b][:, gi * 128:(gi + 1) * 128], scalar=lnb[:, g:g + 1], in1=bsb[:, g, :],
                    op0=ALU.mult, op1=ALU.add), reads=[self.RP[b], Rln, Rbs], writes=[Rex])
        stats = self.sb("stats", [128, NCH, 4, 6], F32)
        Rst = Res("stats")
        mv = self.sb("mv", [128, NCH, 2], F32)
        Rmv = Res("mv")
        mean = self.sb("mean", [128, NCH], F32)
        rstd = self.sb("lrstd", [128, NCH], F32)
        NG = 6
        gt = [self.sb("gt%d" % i, [128, TW], F32) for i in range(NG)]
        Rgt = [Res("gt") for i in range(NG)]
        n = 0
        for sl_ in range(4):
            wv, rv, sv = self.ring.need(wvd[sl_], (8, 512))
            for c in range(NCH):
                t = c // 4
                b = self.bank()
                for k in range(8):
                    P.op("pe", lambda e, b=b, k=k, c=c, wv=wv: e.matmul(
                        self.ps[b][:], hn[:, k, c * 128:(c + 1) * 128], wv[:, k, :], start=(k == 0), stop=(k == 7)),
                        reads=[rv, self.RN[k][t]], writes=[self.RP[b]], inc=(k == 7))
                i = n % NG
                n += 1
                P.op("act", lambda e, b=b, i=i: e.activation(gt[i][:], self.ps[b][:], AF.Gelu),
                     reads=[self.RP[b]], writes=[Rgt[i]])
                P.op("dve", lambda e, i=i, c=c, sl_=sl_: e.bn_stats(stats[:, c, sl_, :], gt[i][:]),
                     reads=[Rgt[i]], writes=[Rst])
                P.dma("sp", [(dgv.ap()[c, :, sl_ * 512:(sl_ + 1) * 512], gt[i][:])], reads=[Rgt[i]])
            self.ring.release(sv)
        Rdg = Res("dgv")
        if not P.dry:
            Rdg.w = {sm_: P.dmatot[sm_] for r_ in Rgt if r_.dsem for sm_ in r_.dsem.values()}
        for c in range(NCH):
            P.op("dve", lambda e, c=c: e.bn_aggr(mv[:, c, :], stats[:, c, :, :].rearrange("p a b -> p (a b)")),
                 reads=[Rst], writes=[Rmv])
        P.op("dve", lambda e: e.tensor_copy(mean[:], mv[:, :, 0]), reads=[Rmv], writes=[Rmv])
        P.op("act", lambda e: e.activation(rstd[:], mv[:, :, 1], AF.Sqrt, bias=self.epsb[:], scale=1.0),
             reads=[Rmv, self.Rconst], writes=[Rmv])
        P.op("dve", lambda e: e.reciprocal(rstd[:], rstd[:]), reads=[Rmv], writes=[Rmv])
        vn = [self.sb("vn%d" % i, [128, TW], BF16) for i in range(4)]
        Rvn = [Res("vn") for i in range(4)]
        u = [self.sb("u%d" % i, [128, TW], F32) for i in range(2)]
        Ru = [Res("u") for i in range(2)]
        z = [self.sb("z%d" % i, [128, TW], F32) for i in range(2)]
        Rz = [Res("z") for i in range(2)]
        y = [self.sb("y%d" % i, [128, TW], BF16) for i in range(8)]
        Ry = [Res("y") for i in range(8)]
        nu = 0
        ny = 0
        for p in range(4):
            wu, ru, su = self.ring.need(wud[p], (8, 512))
            wo, ro, so = self.ring.need(wod[p], (4, 1024))
            for t in range(NT):
                sl = slice(t * TW, (t + 1) * TW)
                for c4 in range(4):
                    c = t * 4 + c4
                    i = n % NG
                    n += 1
                    P.dma("sp", [(gt[i][:], dgv.ap()[c, :, p * 512:(p + 1) * 512])], reads=[Rdg], writes=[Rgt[i]])
                    P.op("dve", lambda e, i=i, c=c, c4=c4: e.tensor_scalar(
                        out=vn[c4][:], in0=gt[i][:], scalar1=mean[:, c:c + 1], scalar2=rstd[:, c:c + 1],
                        op0=ALU.subtract, op1=ALU.mult), reads=[Rgt[i], Rmv], writes=[Rvn[c4]])
                yb = (ny % 2) * 4
                ny += 1
                for gi in range(4):
                    g = p * 4 + gi
                    b = self.bank()
                    for k in range(8):
                        P.op("pe", lambda e, b=b, k=k, gi=gi, sl=sl, wu=wu: e.matmul(
                            self.ps[b][:], wu[:, k, gi * 128:(gi + 1) * 128], hn[:, k, sl], start=(k == 0), stop=(k == 7)),
                            reads=[ru, self.RN[k][t]], writes=[self.RP[b]], inc=(k == 7))
                    iu = nu % 2
                    nu += 1
                    P.op("act", lambda e, b=b, iu=iu: e.activation(u[iu][:], self.ps[b][:], AF.Gelu),
                         reads=[self.RP[b]], writes=[Ru[iu]])
                    b2 = self.bank()
                    for c4 in range(4):
                        P.op("pe", lambda e, b2=b2, c4=c4, gi=gi, g=g: e.matmul(
                            self.ps[b2][:, c4 * 128:(c4 + 1) * 128], vn[c4][:, gi * 128:(gi + 1) * 128], wsb[:, g, :],
                            start=True, stop=True), reads=[Rvn[c4], Rwb], writes=[self.RP[b2]], inc=(c4 == 3))
                    for c4 in range(4):
                        P.op("dve", lambda e, b2=b2, c4=c4, g=g, iu=iu: e.scalar_tensor_tensor(
                            out=z[iu][:, c4 * 128:(c4 + 1) * 128], in0=self.ps[b2][:, c4 * 128:(c4 + 1) * 128],
                            scalar=lng[:, g:g + 1], in1=extra[:, g, :], op0=ALU.mult, op1=ALU.add),
                            reads=[self.RP[b2], Rln, Rex], writes=[Rz[iu]])
                    P.op("pool", lambda e, iu=iu, yb=yb, gi=gi: e.tensor_tensor(
                        out=y[yb + gi][:], in0=z[iu][:], in1=u[iu][:], op=ALU.mult),
                        reads=[Rz[iu], Ru[iu]], writes=[Ry[yb + gi]])
                for mo in range(8):
                    b = self.bank()
                    for k in range(4):
                        P.op("pe", lambda e, b=b, k=k, mo=mo, yb=yb, wo=wo: e.matmul(
                            self.ps[b][:], wo[:, k, mo * 128:(mo + 1) * 128], y[yb + k][:], start=(k == 0), stop=(k == 3)),
                            reads=[ro, Ry[yb + k]], writes=[self.RP[b]], inc=(k == 3))
                    P.op("dve", lambda e, b=b, mo=mo, sl=sl: e.tensor_tensor(
                        out=hT[:, mo, sl], in0=self.ps[b][:], in1=hT[:, mo, sl], op=ALU.add),
                        reads=[self.RP[b], self.RH[mo][t]], writes=[self.RH[mo][t]])
            self.ring.release(su)
            self.ring.release(so)
        self.barrier()
        self.off = base


    def mixer_b(self, l):
        P = self.P
        self.rmsnorm(l)
        base = self.off
        hT, hn = self.hT, self.hn
        wind = self.din("b_win", [8, 128, 8, 512])
        woutd = self.din("b_wout", [2, 128, 4, 1024])
        lbld = self.din("b_lbl", [128, 4, 8])
        ngd = self.din("b_ng", [128, 8])
        seld = self.din("sel", [128, 4])
        bmd = self.din("bmask4", [128, 512])
        if "b_ccin" not in self.dram:
            self.dram["b_ccin"] = self.nc.dram_tensor("b_ccin", [128, 1032], F32)
            self.dram["b_ccout"] = self.nc.dram_tensor("b_ccout", [512, 1032], F32)
        cc_in, cc_out = self.dram["b_ccin"], self.dram["b_ccout"]
        if "b_dsg" not in self.dram:
            self.dram["b_dsg"] = self.nc.dram_tensor("b_dsg", [8, 4, 128, TW], F32)
            self.dram["b_dit"] = self.nc.dram_tensor("b_dit", [8, 4, 128, TW], BF16)
        dsg, dit = self.dram["b_dsg"], self.dram["b_dit"]
        Rdb = Res("b_spill")
        mode = [1]
        Rcci, Rcco = Res("ccin"), Res("ccout")
        lg = self.sb("lg", [128, 4, 8], F32)
        ex = self.sb("ex", [128, 4, 8], F32)
        ssum = self.sb("ssum", [128, 8], F32)
        lb = self.sb("lb", [128, 8], F32)
        oml = self.sb("oml", [128, 8], F32)
        noml = self.sb("noml", [128, 8], F32)
        ng = self.sb("ng", [128, 8], F32)
        sel = self.sb("sel", [128, 4], F32)
        bm4 = self.sb("bm4", [128, 512], F32)
        ones5 = self.sb("ones5", [128, 512], F32)
        rm5 = self.sb("rm5", [128, 512], F32)
        Rc = Res("bconst")
        Rlb = Res("lb")
        P.dma("sp", [(lg[:], lbld), (ng[:], ngd), (sel[:], seld), (bm4[:], bmd)], writes=[Rc])
        P.op("pool", lambda e: e.memset(ones5[:], 1.0), writes=[Rc])
        P.op("pool", lambda e: e.memset(rm5[:], 1.0), writes=[Rc])
        P.op("pool", lambda e: e.memset(rm5[:].rearrange("p (c l) -> p c l", l=64)[:, :, 0:1], 0.0), writes=[Rc])
        P.op("act", lambda e: e.activation(ex[:], lg[:], AF.Exp), reads=[Rc], writes=[Rlb])
        P.op("dve", lambda e: e.tensor_tensor(out=ssum[:], in0=ex[:, 0, :], in1=ex[:, 1, :], op=ALU.add), reads=[Rlb], writes=[Rlb])
        P.op("dve", lambda e: e.tensor_tensor(out=ssum[:], in0=ssum[:], in1=ex[:, 2, :], op=ALU.add), reads=[Rlb], writes=[Rlb])
        P.op("dve", lambda e: e.tensor_tensor(out=ssum[:], in0=ssum[:], in1=ex[:, 3, :], op=ALU.add), reads=[Rlb], writes=[Rlb])
        P.op("dve", lambda e: e.reciprocal(ssum[:], ssum[:]), reads=[Rlb], writes=[Rlb])
        P.op("dve", lambda e: e.tensor_tensor(out=lb[:], in0=ex[:, 1, :], in1=ssum[:], op=ALU.mult), reads=[Rlb], writes=[Rlb])
        P.op("dve", lambda e: e.tensor_scalar(out=oml[:], in0=lb[:], scalar1=-1.0, scalar2=1.0, op0=ALU.mult, op1=ALU.add), reads=[Rlb], writes=[Rlb])
        P.op("dve", lambda e: e.tensor_scalar(out=noml[:], in0=lb[:], scalar1=-1.0, scalar2=None, op0=ALU.add), reads=[Rlb], writes=[Rlb])
        xch = self.sb("xch", [128, 1032], F32)
        Rx = Res("xch")
        Rg = Res("gath")
        sst = self.sb("sst", [128, 8, 128], F32)
        Rsst = Res("sst")
        sg = [self.sb("sg%d" % i, [128, TW], F32) for i in range(2)]
        Rsg = [Res("sg") for i in range(2)]
        lft = [self.sb("lft%d" % i, [128, TW], F32) for i in range(2)]
        Rlf = [Res("lft") for i in range(2)]
        et = [self.sb("et%d" % i, [128, TW], F32) for i in range(2)]
        Ret = [Res("et") for i in range(2)]
        itok = self.sb("itok", [128, NCH, 128], BF16)
        Rit = [Res("itok") for t in range(NT)]
        ktok = self.sb("ktok", [128, NCH, 128], BF16)
        Rkt = [Res("ktok") for t in range(NT)]
        pbase = self.off

        def sigm(dst, Rdst, b):
            P.op("act", lambda e: e.activation(dst[:], self.ps[b][:], AF.Exp, scale=-1.0), reads=[self.RP[b]], writes=[Rdst])
            P.op("act", lambda e: e.activation(dst[:], dst[:], AF.Ln, bias=self.oneb[:]), reads=[self.Rconst], writes=[Rdst])
            P.op("act", lambda e: e.activation(dst[:], dst[:], AF.Exp, scale=-1.0), writes=[Rdst])

        def proj_f(w, rw, h, t, i2, kk_out, Rkk):
            sl = slice(t * TW, (t + 1) * TW)
            if mode[0] == 1:
                b = self.bank()
                for k in range(8):
                    P.op("pe", lambda e, b=b, k=k, sl=sl: e.matmul(self.ps[b][:], w[:, k, 128:256], hn[:, k, sl], start=(k == 0), stop=(k == 7)),
                         reads=[rw, self.RN[k][t]], writes=[self.RP[b]], inc=(k == 7))
                sigm(sg[i2], Rsg[i2], b)
                P.dma("sp", [(dsg.ap()[h, t], sg[i2][:])], reads=[Rsg[i2]])
            else:
                P.dma("sp", [(sg[i2][:], dsg.ap()[h, t])], reads=[Rdb], writes=[Rsg[i2]])
            P.op("act", lambda e: e.activation(lft[i2][:], sg[i2][:], AF.Ln, bias=lb[:, h:h + 1], scale=oml[:, h:h + 1]),
                 reads=[Rsg[i2], Rlb], writes=[Rlf[i2]])
            P.op("dve", lambda e: e.tensor_scalar(out=kk_out, in0=sg[i2][:], scalar1=noml[:, h:h + 1], scalar2=oml[:, h:h + 1],
                                                  op0=ALU.mult, op1=ALU.add), reads=[Rsg[i2], Rlb], writes=[Rkk])

        iTb = [self.sb("iTb%d" % i, [128, TW], BF16) for i in range(1)] * 2
        RiT = [Res("iTb")] * 2
        icnt = [0]
        pbase = self.off

        def proj_i(w, rw, t, h=0):
            sl = slice(t * TW, (t + 1) * TW)
            itv = itok[:, t * 4:(t + 1) * 4, :].rearrange("p c v -> p (c v)")
            if mode[0] == 2:
                P.dma("sp", [(itv, dit.ap()[h, t])], reads=[Rdb], writes=[Rit[t]])
                return
            j = icnt[0] % 2
            icnt[0] += 1
            b = self.bank()
            for k in range(8):
                P.op("pe", lambda e, b=b, k=k: e.matmul(self.ps[b][:], w[:, k, 256:384], hn[:, k, sl], start=(k == 0), stop=(k == 7)),
                     reads=[rw, self.RN[k][t]], writes=[self.RP[b]], inc=(k == 7))
            P.op("act", lambda e, b=b: e.copy(iTb[j][:], self.ps[b][:]), reads=[self.RP[b]], writes=[RiT[j]])
            b2 = self.bank()
            psb = self.ps[b2][:].bitcast(BF16)
            for c4 in range(4):
                P.op("pe", lambda e, c4=c4, psb=psb: e.transpose(psb[:, c4 * 128:(c4 + 1) * 128], iTb[j][:, c4 * 128:(c4 + 1) * 128], self.ident_bf[:]),
                     reads=[RiT[j], self.Rconst], writes=[self.RP[b2]], inc=(c4 == 3))
            P.op("act", lambda e, psb=psb: e.copy(itv, psb[:, 0:512]),
                 reads=[self.RP[b2]], writes=[Rit[t]])
            P.dma("sp", [(dit.ap()[h, t], itv)], reads=[Rit[t]])

        def transp(src, Rsrc, t):
            b = self.bank()
            psb = self.ps[b][:].bitcast(BF16)
            for c4 in range(4):
                c = t * 4 + c4
                P.op("pe", lambda e, c=c, c4=c4, psb=psb: e.transpose(psb[:, c4 * 128:(c4 + 1) * 128], src[:, c * 128:(c + 1) * 128], self.ident_bf[:]),
                     reads=[Rsrc, self.Rconst], writes=[self.RP[b]], inc=(c4 == 3))
            P.op("act", lambda e, psb=psb: e.copy(ktok[:, t * 4:(t + 1) * 4, :].rearrange("p c v -> p (c v)"), psb[:, 0:512]),
                 reads=[self.RP[b]], writes=[Rkt[t]])

        BG = self.sb("BG", [128, TOK], F32)
        RBG = [Res("BG") for t in range(NT)]
        KK = self.sb("KK", [128, TOK], F32)
        RKK = [Res("KK") for t in range(NT)]
        kgb = self.sb("kgb", [128, TOK], BF16)
        Rkg = [Res("kgb") for t in range(NT)]
        n = 0
        for h in range(8):
            w, rw, sw = self.ring.need(wind[h], (8, 512))
            for t in range(NT):
                sl = slice(t * TW, (t + 1) * TW)
                i2 = n % 2
                n += 1
                proj_f(w, rw, h, t, i2, KK[:, sl], RKK[t])
                if t == 0:
                    P.op("dve", lambda e, i2=i2, sl=sl: e.tensor_tensor_scan(BG[:, sl], ones5[:], lft[i2][:], 0.0, ALU.mult, ALU.add),
                         reads=[Rlf[i2], Rc], writes=[RBG[t]])
                else:
                    P.op("dve", lambda e, i2=i2, sl=sl, t=t: e.tensor_tensor_scan(
                        BG[:, sl], ones5[:], lft[i2][:], BG[:, t * TW - 1:t * TW], ALU.mult, ALU.add),
                        reads=[Rlf[i2], Rc, RBG[t - 1]], writes=[RBG[t]])
                proj_i(w, rw, t, h)
            for t in range(NT):
                sl = slice(t * TW, (t + 1) * TW)
                i2 = n % 2
                n += 1
                P.op("act", lambda e, i2=i2, sl=sl: e.activation(et[i2][:], BG[:, sl], AF.Exp, bias=BG[:, TOK - 1:TOK], scale=-1.0),
                     reads=[RBG[t], RBG[NT - 1]], writes=[Ret[i2]])
                P.op("pool", lambda e, i2=i2, sl=sl: e.tensor_tensor(out=kgb[:, sl], in0=KK[:, sl], in1=et[i2][:], op=ALU.mult),
                     reads=[RKK[t], Ret[i2]], writes=[Rkg[t]])
                transp(kgb, Rkg[t], t)
            P.op("act", lambda e, h=h: e.activation(xch[:, 1024 + h:1025 + h], BG[:, TOK - 1:TOK], AF.Exp), reads=[RBG[NT - 1]], writes=[Rx])
            b = self.bank()
            for c in range(NCH):
                P.op("pe", lambda e, b=b, c=c: e.matmul(self.ps[b][:, 0:128], ktok[:, c, :], itok[:, c, :], start=(c == 0), stop=(c == NCH - 1)),
                     reads=[Rkt[c // 4], Rit[c // 4]], writes=[self.RP[b]], inc=(c == NCH - 1))
            P.op("dve", lambda e, b=b, h=h: e.tensor_copy(xch[:, h * 128:(h + 1) * 128], self.ps[b][:, 0:128]), reads=[self.RP[b]], writes=[Rx])
            self.ring.release(sw)
        import os
        stop = int(os.environ.get("KB_STOP", "9"))
        if stop <= 1:
            self.barrier()
            self.off = base
            return
        mode[0] = 2
        if not P.dry:
            Rdb.w = {sm_: P.dmatot[sm_] for r_ in (Rsg + Rit) if r_.dsem for sm_ in r_.dsem.values()}
        P.dma("sp", [(cc_in.ap(), xch[:])], reads=[Rx], writes=[Rcci])
        P.op("pool", lambda e: e.collective_compute("AllGather", ALU.bypass, replica_groups=[[0, 1, 2, 3], [4, 5, 6, 7]],
                                                     ins=[cc_in.ap().opt()], outs=[cc_out.ap().opt()]), reads=[Rcci], writes=[Rcco])
        self.barrier()
        self.off = pbase
        gath = self.sb("gath", [128, 4, 1032], F32)
        P.dma("sp", [(gath[:], cc_out.ap().rearrange("(r p) n -> p r n", p=128))], reads=[Rcco], writes=[Rg])
        pa = self.sb("pa", [128, 128], F32)
        pb_ = self.sb("pb", [128, 128], F32)
        Rpa = Res("pa")
        for h in range(8):
            S = lambda i, h=h: gath[:, i, h * 128:(h + 1) * 128]
            D = lambda i, h=h: gath[:, i, 1024 + h:1025 + h]
            sh = sst[:, h, :]
            P.op("dve", lambda e, S=S, D=D: e.scalar_tensor_tensor(out=pa[:], in0=S(0), scalar=D(1), in1=S(1), op0=ALU.mult, op1=ALU.add),
                 reads=[Rg], writes=[Rpa])
            P.op("dve", lambda e, S=S, D=D: e.scalar_tensor_tensor(out=pb_[:], in0=pa[:], scalar=D(2), in1=S(2), op0=ALU.mult, op1=ALU.add),
                 reads=[Rg, Rpa], writes=[Rpa])
            P.op("dve", lambda e, S=S, sh=sh: e.tensor_scalar(out=sh, in0=S(0), scalar1=sel[:, 1:2], scalar2=None, op0=ALU.mult),
                 reads=[Rg, Rc], writes=[Rsst])
            P.op("dve", lambda e, sh=sh: e.scalar_tensor_tensor(out=sh, in0=pa[:], scalar=sel[:, 2:3], in1=sh, op0=ALU.mult, op1=ALU.add),
                 reads=[Rpa, Rc], writes=[Rsst])
            P.op("dve", lambda e, sh=sh: e.scalar_tensor_tensor(out=sh, in0=pb_[:], scalar=sel[:, 3:4], in1=sh, op0=ALU.mult, op1=ALU.add),
                 reads=[Rpa, Rc], writes=[Rsst])
        if stop <= 2:
            self.barrier()
            self.off = base
            return
        self.barrier()
        self.off = pbase
        gs = self.sb("gs", [128, TOK], BF16)
        Rgs = [Res("gs") for t in range(NT)]
        qtb = self.sb("qtb", [128, TOK], BF16)
        Rqt = [Res("qtb") for t in range(NT)]
        ktb = self.sb("ktb", [128, TOK], BF16)
        Rkb = [Res("ktb") for t in range(NT)]
        smb = self.sb("smb", [128, NCH, 128], BF16)
        Rsm = [Res("smb") for t in range(NT)]
        Sb = self.sb("Sb", [128, 33, 128], BF16)
        RSb = [Res("Sb") for c in range(33)]
        Sf = [self.sb("Sf%d" % i, [128, 128], F32) for i in range(2)]
        RSf = [Res("Sf") for i in range(2)]
        Tt = self.sb("Tt", [128, 128], F32)
        RT = Res("Tt")
        ebl = self.sb("ebl", [128, 32], F32)
        Reb = [Res("ebl") for t in range(NT)]
        qs = [self.sb("qs%d" % i, [128, TW], F32) for i in range(2)]
        Rqs = [Res("qs") for i in range(2)]
        kkt = [self.sb("kkt%d" % i, [128, TW], F32) for i in range(1)] * 2
        Rkkt = [Res("kkt")] * 2
        bc = [self.sb("bc%d" % i, [128, TW], F32) for i in range(1)] * 2
        Rbc = [Res("bc")] * 2
        en = [self.sb("en%d" % i, [128, TW], F32) for i in range(1)] * 2
        Ren = [Res("en")] * 2
        sqo = [self.sb("sqo%d" % i, [128, TW], BF16) for i in range(1)] * 2
        Rsq = [Res("sqo")] * 2
        rs = [self.sb("brs%d" % i, [128, TW], F32) for i in range(1)] * 2
        Rrs = [Res("brs")] * 2
        ont = [self.sb("ont%d" % i, [128, TW], F32) for i in range(1)] * 2
        Ron = [Res("ont")] * 2
        yv = [self.sb("yv%d" % i, [128, TW], BF16) for i in range(2)]
        Ryv = [Res("yv") for i in range(2)]
        self.rot = [0, 1, 2, 3]
        hw = {}
        hwo = {}
        state = {"n": n, "cur": 0, "m": 0}

        def S1a(h, t, u):
            if t == 0:
                hw[h] = self.ring.need(wind[h], (8, 512))
                if h % 4 == 0:
                    hwo[h // 4] = self.ring.need(woutd[h // 4], (4, 1024))
            w, rw, sw = hw[h]
            sl = slice(t * TW, (t + 1) * TW)
            i2 = state["n"] % 2
            state["n"] += 1
            bP = [4 + 2 * (u % 2), 5 + 2 * (u % 2)]
            b = self.bank()
            for k in range(8):
                P.op("pe", lambda e, b=b, k=k, sl=sl, w=w: e.matmul(self.ps[b][:], w[:, k, 0:128], hn[:, k, sl], start=(k == 0), stop=(k == 7)),
                     reads=[rw, self.RN[k][t]], writes=[self.RP[b]], inc=(k == 7))
            sigm(qs[i2], Rqs[i2], b)
            P.op("dve", lambda e, b=b, i2=i2: e.tensor_tensor(out=qs[i2][:], in0=self.ps[b][:], in1=qs[i2][:], op=ALU.mult), reads=[self.RP[b]], writes=[Rqs[i2]])
            proj_f(w, rw, h, t, i2, kkt[i2][:], Rkkt[i2])
            b = self.bank()
            for k in range(8):
                P.op("pe", lambda e, b=b, k=k, sl=sl, w=w: e.matmul(self.ps[b][:], w[:, k, 384:512], hn[:, k, sl], start=(k == 0), stop=(k == 7)),
                     reads=[rw, self.RN[k][t]], writes=[self.RP[b]], inc=(k == 7))
            sigm(en[i2], Ren[i2], b)
            P.op("dve", lambda e, b=b, sl=sl, i2=i2: e.tensor_tensor(out=gs[:, sl], in0=self.ps[b][:], in1=en[i2][:], op=ALU.mult), reads=[self.RP[b], Ren[i2]], writes=[Rgs[t]])
            proj_i(w, rw, t, h)
            P.op("dve", lambda e, i2=i2: e.tensor_tensor_scan(bc[i2][:], rm5[:], lft[i2][:], 0.0, ALU.mult, ALU.add),
                 reads=[Rlf[i2], Rc], writes=[Rbc[i2]])
            P.op("act", lambda e, i2=i2: e.activation(et[i2][:], bc[i2][:], AF.Exp), reads=[Rbc[i2]], writes=[Ret[i2]])
            P.op("act", lambda e, i2=i2: e.activation(en[i2][:], bc[i2][:], AF.Exp, scale=-1.0), reads=[Rbc[i2]], writes=[Ren[i2]])
            P.op("act", lambda e, i2=i2, t=t: e.activation(ebl[:, t * 8:(t + 1) * 8], bc[i2][:].rearrange("p (c l) -> p c l", l=64)[:, :, 63], AF.Exp),
                 reads=[Rbc[i2]], writes=[Reb[t]])
            P.op("pool", lambda e, i2=i2, sl=sl: e.tensor_tensor(out=qtb[:, sl], in0=qs[i2][:], in1=et[i2][:], op=ALU.mult),
                 reads=[Rqs[i2], Ret[i2]], writes=[Rqt[t]])
            P.op("pool", lambda e, i2=i2, sl=sl: e.tensor_tensor(out=ktb[:, sl], in0=kkt[i2][:], in1=en[i2][:], op=ALU.mult),
                 reads=[Rkkt[i2], Ren[i2]], writes=[Rkb[t]])
            if t == NT - 1:
                self.ring.release(sw)


        def S1b(h, t, u):
            sl = slice(t * TW, (t + 1) * TW)
            bP = [4 + 2 * (u % 2), 5 + 2 * (u % 2)]
            transp(ktb, Rkb[t], t)
            b = self.bank()
            for c4 in range(4):
                c = t * 4 + c4
                P.op("pe", lambda e, b=b, c=c, c4=c4: e.matmul(self.ps[b][:, c4 * 128:(c4 + 1) * 128], ktb[:, c * 128:(c + 1) * 128],
                                                               qtb[:, c * 128:(c + 1) * 128], start=True, stop=True),
                     reads=[Rkb[t], Rqt[t]], writes=[self.RP[b]], inc=(c4 == 3))
            P.op("dve", lambda e, b=b, t=t: e.tensor_tensor(out=smb[:, t * 4:(t + 1) * 4, :].rearrange("p c v -> p (c v)"), in0=self.ps[b][:],
                                                             in1=bm4[:], op=ALU.mult), reads=[self.RP[b], Rc], writes=[Rsm[t]])
            for c8 in range(8):
                bk = t * 4 + c8 // 2
                r0 = (c8 % 2) * 64
                P.op("pe", lambda e, c8=c8, bk=bk, r0=r0, bP=bP: e.matmul(
                    self.ps[bP[c8 % 2]][:, (c8 // 2) * 128:(c8 // 2 + 1) * 128], ktok[r0:r0 + 64, bk, :], itok[r0:r0 + 64, bk, :],
                    start=True, stop=True), reads=[Rkt[t], Rit[t]], writes=[self.RP[bP[c8 % 2]]], inc=(c8 >= 6))


        def S2(h, t, u):
            wo, ro, so = hwo[h // 4]
            sl = slice(t * TW, (t + 1) * TW)
            i2 = state["m"] % 2
            state["m"] += 1
            bP = [4 + 2 * (u % 2), 5 + 2 * (u % 2)]
            if t == 0:
                P.op("act", lambda e, h=h: e.copy(Sb[:, 0, :], sst[:, h, :]), reads=[Rsst], writes=[RSb[0]])
                state["cur"] = 0
            cur = state["cur"]
            for c8 in range(8):
                c = t * 8 + c8
                pP = self.ps[bP[c8 % 2]][:, (c8 // 2) * 128:(c8 // 2 + 1) * 128]
                RpP = self.RP[bP[c8 % 2]]
                if c == 0:
                    P.op("dve", lambda e, pP=pP, h=h: e.tensor_tensor(out=Sf[0][:], in0=pP, in1=sst[:, h, :], op=ALU.add),
                         reads=[RpP, Rsst], writes=[RSf[0]])
                    cur = 0
                else:
                    P.op("act", lambda e, c=c, cur=cur: e.activation(Sb[:, c, :], Sf[cur][:], AF.Identity, scale=ebl[:, c - 1:c]),
                         reads=[RSf[cur], Reb[(c - 1) // 8]], writes=[RSb[c]])
                    P.op("dve", lambda e, c=c, cur=cur, pP=pP: e.scalar_tensor_tensor(out=Sf[1 - cur][:], in0=Sf[cur][:], scalar=ebl[:, c - 1:c], in1=pP,
                                                                                      op0=ALU.mult, op1=ALU.add),
                         reads=[RSf[cur], Reb[(c - 1) // 8], RpP], writes=[RSf[1 - cur]])
                    cur = 1 - cur
            b = self.bank()
            for c4 in range(4):
                bk = t * 4 + c4
                o0 = c4 * 128
                P.op("pe", lambda e, b=b, bk=bk, o0=o0: e.matmul(self.ps[b][:, o0:o0 + 128], itok[:, bk, :], smb[:, bk, :], start=True, stop=False),
                     reads=[Rit[t], Rsm[t]], writes=[self.RP[b]], inc=False)
                P.op("pe", lambda e, b=b, bk=bk, o0=o0: e.matmul(self.ps[b][:, o0:o0 + 64], Sb[:, 2 * bk, :], qtb[:, bk * 128:bk * 128 + 64], start=False, stop=False),
                     reads=[RSb[2 * bk], Rqt[t]], writes=[self.RP[b]], inc=False)
                P.op("pe", lambda e, b=b, bk=bk, o0=o0: e.matmul(self.ps[b][:, o0 + 64:o0 + 128], Sb[:, 2 * bk + 1, :], qtb[:, bk * 128 + 64:bk * 128 + 128], start=False, stop=True),
                     reads=[RSb[2 * bk + 1], Rqt[t]], writes=[self.RP[b]], inc=(c4 == 3))
            P.op("act", lambda e, b=b, i2=i2: e.activation(sqo[i2][:], self.ps[b][:], AF.Square), reads=[self.RP[b]], writes=[Rsq[i2]])
            b2 = self.bank()
            P.op("pe", lambda e, b2=b2, i2=i2: e.matmul(self.ps[b2][:], self.ones_bf[:], sqo[i2][:], start=True, stop=True),
                 reads=[Rsq[i2], self.Rconst], writes=[self.RP[b2]])
            P.op("act", lambda e, b2=b2, i2=i2: e.activation(rs[i2][:], self.ps[b2][:], AF.Ln, bias=self.epsb[:], scale=1.0 / 128.0),
                 reads=[self.RP[b2], self.Rconst], writes=[Rrs[i2]])
            P.op("act", lambda e, i2=i2: e.activation(rs[i2][:], rs[i2][:], AF.Exp, scale=-0.5), reads=[Rrs[i2]], writes=[Rrs[i2]])
            P.op("dve", lambda e, b=b, i2=i2, h=h: e.scalar_tensor_tensor(out=ont[i2][:], in0=self.ps[b][:], scalar=ng[:, h:h + 1], in1=rs[i2][:],
                                                                           op0=ALU.mult, op1=ALU.mult), reads=[self.RP[b], Rrs[i2], Rc], writes=[Ron[i2]])
            P.op("pool", lambda e, i2=i2, sl=sl: e.tensor_tensor(out=yv[i2][:], in0=ont[i2][:], in1=gs[:, sl], op=ALU.mult),
                 reads=[Ron[i2], Rgs[t]], writes=[Ryv[i2]])
            def out_proj(i2=i2, h=h, t=t, sl=sl, wo=wo, ro=ro, so=so):
                for mo in range(8):
                    b3 = self.bank()
                    P.op("pe", lambda e, b3=b3, mo=mo: e.matmul(self.ps[b3][:], wo[:, h % 4, mo * 128:(mo + 1) * 128], yv[i2][:], start=True, stop=True),
                         reads=[ro, Ryv[i2]], writes=[self.RP[b3]])
                    P.op("dve", lambda e, b3=b3, mo=mo: e.tensor_tensor(out=hT[:, mo, sl], in0=self.ps[b3][:], in1=hT[:, mo, sl], op=ALU.add),
                         reads=[self.RP[b3], self.RH[mo][t]], writes=[self.RH[mo][t]])
                if t == NT - 1 and h % 4 == 3:
                    self.ring.release(so)
            state["cur"] = cur
            return out_proj

        units = [(h, t) for h in range(8) for t in range(NT)]
        S1a(units[0][0], units[0][1], 0)
        pend_out = None
        for u, (h, t) in enumerate(units):
            if u + 1 < len(units):
                S1a(units[u + 1][0], units[u + 1][1], u + 1)
            if pend_out is not None:
                pend_out()
            S1b(h, t, u)
            pend_out = S2(h, t, u)
        pend_out()
        self.rot = list(range(8))
        self.barrier()
        self.off = base

    def mixer_c(self, l):
        P = self.P
        self.rmsnorm(l)
        base = self.off
        hT, hn = self.hT, self.hn
        nc = self.nc
        wxd = self.din("c_wxm", [4, 128, 8, 512])
        wzd = self.din("c_wz", [4, 128, 8, 512])
        wod = self.din("c_wo", [4, 128, 4, 1024])
        lst = [("c_ccin", [128, 24]), ("c_ccout", [512, 24])]
        for h_ in range(8):
            lst += [("c_ccin2_%d" % h_, [128, 2 * 516]), ("c_ccout2_%d" % h_, [512, 2 * 516])]
        for nm, shp in lst:
            if nm not in self.dram:
                self.dram[nm] = nc.dram_tensor(nm, shp, F32)
        if "c_dxm" not in self.dram:
            self.dram["c_dxm"] = nc.dram_tensor("c_dxm", [16, 128, 4, 3 + TW], BF16)
            self.dram["c_dca"] = nc.dram_tensor("c_dca", [16, 128, 4, TW], BF16)
        dxm, dca = self.dram["c_dxm"], self.dram["c_dca"]
        Rd = [Res("c_spill%d" % i) for i in range(16)]
        cci, cco = self.dram["c_ccin"], self.dram["c_ccout"]
        cci2 = [self.dram["c_ccin2_%d" % h_] for h_ in range(8)]
        cco2 = [self.dram["c_ccout2_%d" % h_] for h_ in range(8)]
        Rcci, Rcco = Res("cci"), Res("cco")
        Rcci2 = [Res("cci2_%d" % h_) for h_ in range(8)]
        Rcco2 = [Res("cco2_%d" % h_) for h_ in range(8)]
        LNS = -0.5 * float(np.log(512.0))
        bdq = self.sb("bdq", [128, 16, 128], BF16)
        bdk = self.sb("bdk", [128, 16, 128], BF16)
        bdv = self.sb("bdv", [128, 16, 128], BF16)
        Gqk = self.sb("Gqk", [128, 16, 8], BF16)
        Gv = self.sb("Gv", [128, 16, 8], BF16)
        cw = self.sb("cw", [128, 16, 4], F32)
        cb = self.sb("cb", [128, 16], F32)
        cng = self.sb("cng", [128, 16], F32)
        csk = self.sb("csk", [128, 16], F32)
        bgb = self.sb("bgb", [128, 8], F32)
        E4 = self.sb("E4", [128, 4, 4], BF16)
        onec = self.sb("onec", [128, 1], BF16)
        U = self.sb("U", [128, 128], F32)
        cmb = self.sb("cmb", [128, 128], F32)
        onesf = self.sb("onesf", [128, 128], F32)
        ones16 = self.sb("ones16", [128, 16], F32)
        lnsb = self.sb("lnsb", [128, 1], F32)
        zer = self.sb("zer", [128, 128], BF16)
        sel = self.sb("csel", [128, 4], F32)
        selp = self.sb("cselp", [128, 4], F32)
        Rk = Res("cconst")
        P.dma("pool", [(bdq[:], self.din("c_bdq", [128, 16, 128])), (bdk[:], self.din("c_bdk", [128, 16, 128])),
                       (bdv[:], self.din("c_bdv", [128, 16, 128])), (E4[:], self.din("c_E4", [128, 4, 4]))], writes=[Rk])
        P.dma("sp", [(cw[:], self.din("c_cw", [128, 16, 4])), (cb[:], self.din("c_cb", [128, 16])), (cng[:], self.din("c_ng", [128, 16])),
                     (csk[:], self.din("c_skip", [128, 16])), (U[:], self.din("cmask", [128, 128])),
                     (bgb[:], self.din("c_bg", [1, 8]).partition_broadcast(128).rearrange("p o n -> p (o n)")),
                     (sel[:], self.din("sel", [128, 4])), (selp[:], self.din("selp", [128, 4]))], writes=[Rk])
        P.op("pool", lambda e: e.memset(onesf[:], 1.0), writes=[Rk])
        P.op("pool", lambda e: e.memset(ones16[:], 1.0), writes=[Rk])
        P.op("pool", lambda e: e.memset(onec[:], 1.0), writes=[Rk])
        P.op("pool", lambda e: e.memset(lnsb[:], LNS), writes=[Rk])
        P.op("pool", lambda e: e.memset(zer[:], 0.0), writes=[Rk])
        gnames = ["GT8", "LI", "LF", "A", "TOT", "INC", "AG", "TSw", "TSu", "TSg", "TSea", "DEC", "WL"]
        GT8 = self.sb("GT8", [128, 16, 8], F32)
        G = {k: self.sb(k, [128, 16, 4], F32) for k in gnames[1:]}
        DTOT = self.sb("DTOT", [128, 4], F32)
        ATOT = self.sb("ATOT", [128, 4], F32)
        Rgt = Res("gates")
        pbase = self.off
        bdT = [self.sb("bdT%d" % i, [128, 16, 128], BF16) for i in range(3)]
        wg = self.sb("wg", [128, 48, 8], BF16)
        Rt = Res("bdT")
        P.dma("pool", [(bdT[0][:], self.din("c_bdqT", [128, 16, 128])), (bdT[1][:], self.din("c_bdkT", [128, 16, 128])),
                       (bdT[2][:], self.din("c_bdvT", [128, 16, 128])), (wg[:], self.din("c_wg", [128, 48, 8]))], writes=[Rt])
        for fc in range(16):
            b = self.bank()
            P.op("pe", lambda e, b=b, fc=fc: e.matmul(self.ps[b][:, 0:8], bdT[0][:, fc, :], wg[:, fc, :], start=True, stop=False), reads=[Rt], writes=[self.RP[b]], inc=False)
            P.op("pe", lambda e, b=b, fc=fc: e.matmul(self.ps[b][:, 0:8], bdT[1][:, fc, :], wg[:, 16 + fc, :], start=False, stop=True), reads=[Rt], writes=[self.RP[b]])
            P.op("act", lambda e, b=b, fc=fc: e.copy(Gqk[:, fc, :], self.ps[b][:, 0:8]), reads=[self.RP[b]], writes=[Rk])
            b = self.bank()
            P.op("pe", lambda e, b=b, fc=fc: e.matmul(self.ps[b][:, 0:8], bdT[2][:, fc, :], wg[:, 32 + fc, :], start=True, stop=True), reads=[Rt], writes=[self.RP[b]])
            P.op("act", lambda e, b=b, fc=fc: e.copy(Gv[:, fc, :], self.ps[b][:, 0:8]), reads=[self.RP[b]], writes=[Rk])
        self.barrier()
        self.off = pbase
        xh = self.sb("xh", [128, 24], F32)
        gh = self.sb("gh", [128, 4, 24], F32)
        hnh = self.sb("hnh", [128, 8, 3], BF16)
        hacc = self.sb("hacc", [128, 24], F32)
        Rxh, Rgh, Rhnh = Res("xh"), Res("gh"), Res("hnh")
        P.op("dve", lambda e: e.tensor_copy(xh[:].rearrange("p (k t) -> p k t", t=3), hn[:, :, TOK - 3:TOK]),
             reads=[self.RN[k][NT - 1] for k in range(8)], writes=[Rxh])
        P.dma("sp", [(cci.ap(), xh[:])], reads=[Rxh], writes=[Rcci])
        P.op("pool", lambda e: e.collective_compute("AllGather", ALU.bypass, replica_groups=[[0, 1, 2, 3], [4, 5, 6, 7]],
                                                     ins=[cci.ap().opt()], outs=[cco.ap().opt()]), reads=[Rcci], writes=[Rcco])
        P.dma("sp", [(gh[:], cco.ap().rearrange("(r p) n -> p r n", p=128))], reads=[Rcco], writes=[Rgh])
        P.op("dve", lambda e: e.tensor_scalar(out=hacc[:], in0=gh[:, 0, :], scalar1=selp[:, 0:1], scalar2=None, op0=ALU.mult), reads=[Rgh, Rk], writes=[Rhnh])
        for r in range(1, 4):
            P.op("dve", lambda e, r=r: e.scalar_tensor_tensor(out=hacc[:], in0=gh[:, r, :], scalar=selp[:, r:r + 1], in1=hacc[:], op0=ALU.mult, op1=ALU.add),
                 reads=[Rgh, Rk], writes=[Rhnh])
        P.op("dve", lambda e: e.tensor_copy(hnh[:].rearrange("p k t -> p (k t)"), hacc[:]), reads=[Rhnh], writes=[Rhnh])
        xmb = [self.sb("xmb%d" % i, [128, 3 + TW], BF16) for i in range(4)]
        Rxm = [Res("xmb") for i in range(4)]
        cat = [self.sb("cat%d" % i, [128, TW], BF16) for i in range(4)]
        Rca = [Res("cat") for i in range(4)]
        Dj = [self.sb("Dj%d" % i, [128, 4, 128], BF16) for i in range(4)]
        RDj = [Res("Dj") for i in range(4)]

        def make_D(fc, i):
            for j in range(4):
                P.op("dve", lambda e, j=j: e.tensor_scalar(out=Dj[i][:, j, :], in0=self.ident_bf[:], scalar1=cw[:, fc, j:j + 1], scalar2=None, op0=ALU.mult),
                     reads=[Rk, self.Rconst], writes=[RDj[i]])

        def front(w, rw, fc, fcl, t, i):
            sl = slice(t * TW, (t + 1) * TW)
            if t == 0:
                b = self.bank()
                for k in range(8):
                    P.op("pe", lambda e, b=b, k=k: e.matmul(self.ps[b][:, 0:3], w[:, k, fcl * 128:(fcl + 1) * 128], hnh[:, k, :], start=(k == 0), stop=(k == 7)),
                         reads=[rw, Rhnh], writes=[self.RP[b]], inc=(k == 7))
                P.op("act", lambda e, b=b: e.copy(xmb[i][:, 0:3], self.ps[b][:, 0:3]), reads=[self.RP[b]], writes=[Rxm[i]])
            else:
                P.op("act", lambda e: e.copy(xmb[i][:, 0:3], xmb[i][:, TW:TW + 3]), reads=[Rxm[i]], writes=[Rxm[i]])
            b = self.bank()
            for k in range(8):
                P.op("pe", lambda e, b=b, k=k: e.matmul(self.ps[b][:], w[:, k, fcl * 128:(fcl + 1) * 128], hn[:, k, sl], start=(k == 0), stop=(k == 7)),
                     reads=[rw, self.RN[k][t]], writes=[self.RP[b]], inc=(k == 7))
            P.op("act", lambda e, b=b: e.copy(xmb[i][:, 3:3 + TW], self.ps[b][:]), reads=[self.RP[b]], writes=[Rxm[i]])
            b2 = self.bank()
            for j in range(4):
                P.op("pe", lambda e, b2=b2, j=j: e.matmul(self.ps[b2][:], Dj[i][:, j, :], xmb[i][:, j:j + TW], start=(j == 0), stop=(j == 3)),
                     reads=[RDj[i], Rxm[i]], writes=[self.RP[b2]], inc=(j == 3))
            P.op("act", lambda e, b2=b2: e.activation(cat[i][:], self.ps[b2][:], AF.Silu, bias=cb[:, fc:fc + 1]), reads=[self.RP[b2], Rk], writes=[Rca[i]])

        def fload(fc, fcl, t):
            P.dma("sp", [(xmb[fcl][:], dxm.ap()[fc, :, t, :]), (cat[fcl][:], dca.ap()[fc, :, t, :])], reads=[Rd[fc]], writes=[Rxm[fcl], Rca[fcl]])

        self.barrier()
        self.rot = [0, 1, 2, 3]
        gtmp0 = self.off
        g8T = self.sb("g8T", [8, TOK], F32)
        identf = self.sb("identf", [128, 128], F32)
        Rg8 = Res("g8T")
        Ridf = Res("identf")
        P.dma("sp", [(identf[:], self.din("ident", [128, 128]))], writes=[Ridf])
        for hd in range(4):
            w, rw, sw = self.ring.need(wxd[hd], (8, 512))
            for fcl in range(4):
                fc = hd * 4 + fcl
                make_D(fc, fcl)
                for t in range(NT):
                    front(w, rw, fc, fcl, t, fcl)
                    P.dma("sp", [(dxm.ap()[fc, :, t, :], xmb[fcl][:]), (dca.ap()[fc, :, t, :], cat[fcl][:])], reads=[Rxm[fcl], Rca[fcl]], writes=[Rd[fc]])
                    gb = 4 + t
                    P.op("pe", lambda e, gb=gb, fc=fc, fcl=fcl: e.matmul(self.ps[gb][0:8, :], Gqk[:, fc, :], cat[fcl][:], start=(fc == 0), stop=False),
                         reads=[Rca[fcl], Rk], writes=[self.RP[gb]], inc=False)
                    P.op("pe", lambda e, gb=gb, fc=fc, fcl=fcl: e.matmul(self.ps[gb][0:8, :], Gv[:, fc, :], xmb[fcl][:, 3:3 + TW], start=False, stop=(fc == 15)),
                         reads=[Rxm[fcl], Rk], writes=[self.RP[gb]])
            self.ring.release(sw)
        for t in range(NT):
            P.op("act", lambda e, t=t: e.copy(g8T[:, t * TW:(t + 1) * TW], self.ps[4 + t][0:8, :]), reads=[self.RP[4 + t]], writes=[Rg8])
        bt = self.bank()
        for c in range(16):
            P.op("pe", lambda e, c=c: e.transpose(self.ps[bt][:, c * 8:(c + 1) * 8], g8T[:, c * 128:(c + 1) * 128], identf[0:8, 0:8]),
                 reads=[Rg8, Ridf], writes=[self.RP[bt]], inc=(c == 15))
        P.op("act", lambda e: e.copy(GT8[:].rearrange("p c g -> p (c g)"), self.ps[bt][:, 0:128]), reads=[self.RP[bt]], writes=[Rgt])
        self.barrier()
        self.off = gtmp0
        self.rot = list(range(8))
        for c in range(16):
            P.op("dve", lambda e, c=c: e.tensor_tensor(out=GT8[:, c, :], in0=GT8[:, c, :], in1=bgb[:], op=ALU.add), reads=[Rk], writes=[Rgt])
        gw = lambda k: G[k][:].rearrange("p c h -> p (c h)")
        P.op("dve", lambda e: e.tensor_copy(G["LI"][:], GT8[:, :, 0:4]), writes=[Rgt])
        P.op("act", lambda e: e.activation(G["LF"][:], GT8[:, :, 4:8], AF.Sigmoid), writes=[Rgt])
        P.op("act", lambda e: e.activation(gw("LF"), gw("LF"), AF.Ln), writes=[Rgt])
        b = self.bank()
        P.op("pe", lambda e, b=b: e.matmul(self.ps[b][:, 0:64], U[:], gw("LF"), start=True, stop=True), reads=[Rgt, Rk], writes=[self.RP[b]])
        P.op("act", lambda e, b=b: e.copy(gw("A"), self.ps[b][:, 0:64]), reads=[self.RP[b]], writes=[Rgt])
        b = self.bank()
        P.op("pe", lambda e, b=b: e.matmul(self.ps[b][:, 0:64], onesf[:], gw("LF"), start=True, stop=True), reads=[Rgt, Rk], writes=[self.RP[b]])
        P.op("act", lambda e, b=b: e.copy(gw("TOT"), self.ps[b][:, 0:64]), reads=[self.RP[b]], writes=[Rgt])
        for hd in range(4):
            P.op("dve", lambda e, hd=hd: e.tensor_tensor_scan(G["INC"][:, :, hd], ones16[:], G["TOT"][:, :, hd], 0.0, ALU.mult, ALU.add), reads=[Rk], writes=[Rgt])
        P.op("dve", lambda e: e.tensor_copy(ATOT[:], G["INC"][:, 15, :]), writes=[Rgt])
        P.op("dve", lambda e: e.tensor_tensor(out=gw("AG"), in0=gw("A"), in1=gw("INC"), op=ALU.add), writes=[Rgt])
        P.op("dve", lambda e: e.tensor_tensor(out=gw("AG"), in0=gw("AG"), in1=gw("TOT"), op=ALU.subtract), writes=[Rgt])
        P.op("dve", lambda e: e.tensor_tensor(out=gw("WL"), in0=gw("LI"), in1=gw("A"), op=ALU.subtract), writes=[Rgt])
        P.op("act", lambda e: e.activation(gw("TSw"), gw("WL"), AF.Exp, bias=lnsb[:]), reads=[Rk], writes=[Rgt])
        P.op("dve", lambda e: e.tensor_tensor(out=gw("WL"), in0=gw("WL"), in1=gw("TOT"), op=ALU.add), writes=[Rgt])
        P.op("act", lambda e: e.activation(gw("TSu"), gw("WL"), AF.Exp, bias=lnsb[:]), reads=[Rk], writes=[Rgt])
        P.op("dve", lambda e: e.tensor_tensor(out=gw("WL"), in0=gw("LI"), in1=gw("AG"), op=ALU.subtract), writes=[Rgt])
        for hd in range(4):
            P.op("dve", lambda e, hd=hd: e.tensor_scalar(out=G["WL"][:, :, hd], in0=G["WL"][:, :, hd], scalar1=ATOT[:, hd:hd + 1], scalar2=None, op0=ALU.add), writes=[Rgt])
        P.op("act", lambda e: e.activation(gw("TSg"), gw("WL"), AF.Exp, bias=lnsb[:]), reads=[Rk], writes=[Rgt])
        P.op("act", lambda e: e.activation(gw("TSea"), gw("A"), AF.Exp), writes=[Rgt])
        P.op("act", lambda e: e.activation(gw("DEC"), gw("TOT"), AF.Exp), writes=[Rgt])
        P.op("act", lambda e: e.activation(DTOT[:], ATOT[:], AF.Exp), writes=[Rgt])
        import os
        self.dbg("d_GT8", GT8[:].rearrange("p c g -> p (c g)"), [128, 128])
        for k_ in ["LI", "LF", "A", "TOT", "INC", "AG", "TSw", "TSu", "TSg", "TSea", "DEC"]:
            self.dbg("d_" + k_, G[k_][:].rearrange("p c h -> p (c h)"), [128, 64])
        if int(os.environ.get("KC_STOP", "9")) <= 1:
            self.barrier()
            self.off = base
            return
        khat = [self.sb("khat%d" % i, [128, TW], BF16) for i in range(4)]
        Rkh = [Res("khat") for i in range(4)]
        vtok = [self.sb("vtok%d" % i, [128, TW], BF16) for i in range(4)]
        Rvt = [Res("vtok") for i in range(4)]

        def kv_tok(hd, fcl, t, TS):
            fc = hd * 4 + fcl
            b = self.bank()
            for c4 in range(4):
                P.op("pe", lambda e, b=b, c4=c4: e.matmul(self.ps[b][:, c4 * 128:(c4 + 1) * 128], cat[fcl][:, c4 * 128:(c4 + 1) * 128], bdk[:, fc, :], start=True, stop=True),
                     reads=[Rca[fcl], Rk], writes=[self.RP[b]], inc=(c4 == 3))
            for c4 in range(4):
                c = t * 4 + c4
                P.op("dve", lambda e, b=b, c4=c4, c=c: e.tensor_scalar(out=khat[c4][:, fcl * 128:(fcl + 1) * 128], in0=self.ps[b][:, c4 * 128:(c4 + 1) * 128],
                                                                       scalar1=TS[:, c, hd:hd + 1], scalar2=None, op0=ALU.mult), reads=[self.RP[b], Rgt], writes=[Rkh[c4]])
            b = self.bank()
            for c4 in range(4):
                P.op("pe", lambda e, b=b, c4=c4: e.matmul(self.ps[b][:, c4 * 128:(c4 + 1) * 128], xmb[fcl][:, 3 + c4 * 128:3 + (c4 + 1) * 128], bdv[:, fc, :], start=True, stop=True),
                     reads=[Rxm[fcl], Rk], writes=[self.RP[b]], inc=(c4 == 3))
            for c4 in range(4):
                P.op("act", lambda e, b=b, c4=c4: e.copy(vtok[c4][:, fcl * 128:(fcl + 1) * 128], self.ps[b][:, c4 * 128:(c4 + 1) * 128]), reads=[self.RP[b]], writes=[Rvt[c4]])

        KSUB = int(os.environ.get("KC_SUB", "9"))
        xcbase = self.off
        xc = [self.sb("xc%d" % i, [128, 516], F32) for i in range(2)]
        Rxc = [Res("xc") for i in range(2)]
        nx = 0
        for hd in range(4):
            self.rot = [0, 1, 2]
            pc = [3, 4, 5, 6]
            pn = 7
            for t in range(NT):
                for fcl in range(4):
                    fload(hd * 4 + fcl, fcl, t)
                    kv_tok(hd, fcl, t, G["TSg"])
                for c4 in (range(4) if KSUB >= 2 else []):
                    c = t * 4 + c4
                    for dc in range(4):
                        P.op("pe", lambda e, c4=c4, dc=dc, c=c: e.matmul(self.ps[pc[dc]][:], khat[c4][:, dc * 128:(dc + 1) * 128], vtok[c4][:], start=(c == 0), stop=(c == 15)),
                             reads=[Rkh[c4], Rvt[c4]], writes=[self.RP[pc[dc]]], inc=False)
                    for dc in range(4):
                        P.op("pe", lambda e, c4=c4, dc=dc, c=c: e.matmul(self.ps[pn][:, 0:4], khat[c4][:, dc * 128:(dc + 1) * 128], E4[:, dc, :], start=(c == 0 and dc == 0), stop=(c == 15 and dc == 3)),
                             reads=[Rkh[c4], Rk], writes=[self.RP[pn]], inc=(dc == 3))
            for dc in (range(4) if KSUB >= 3 else []):
                i = nx % 2
                nx += 1
                P.op("dve", lambda e, dc=dc, i=i: e.tensor_copy(xc[i][:, 0:512], self.ps[pc[dc]][:]), reads=[self.RP[pc[dc]]], writes=[Rxc[i]])
                P.op("dve", lambda e, dc=dc, i=i: e.tensor_copy(xc[i][:, 512:513], self.ps[pn][:, dc:dc + 1]), reads=[self.RP[pn]], writes=[Rxc[i]])
                P.op("dve", lambda e, i=i, hd=hd: e.tensor_copy(xc[i][:, 513:514], DTOT[:, hd:hd + 1]), reads=[Rgt], writes=[Rxc[i]])
                P.op("dve", lambda e, i=i: e.tensor_copy(xc[i][:, 514:516], DTOT[:, 0:2]), reads=[Rgt], writes=[Rxc[i]])
                q2 = hd * 2 + dc // 2
                P.dma("sp", [(cci2[q2].ap()[:, (dc % 2) * 516:(dc % 2 + 1) * 516], xc[i][:])], reads=[Rxc[i]], writes=[Rcci2[q2]])
            for q2 in ([hd * 2, hd * 2 + 1] if KSUB >= 4 else []):
                P.op("pool", lambda e, q2=q2: e.collective_compute("AllGather", ALU.bypass, replica_groups=[[0, 1, 2, 3], [4, 5, 6, 7]],
                                                                    ins=[cci2[q2].ap().opt()], outs=[cco2[q2].ap().opt()]), reads=[Rcci2[q2]], writes=[Rcco2[q2]])
        self.rot = list(range(8))
        import os
        if int(os.environ.get("KC_STOP", "9")) <= 2:
            self.barrier()
            self.off = base
            return
        self.barrier()
        self.off = xcbase
        qt = [self.sb("qt%d" % i, [128, TW], BF16) for i in range(4)]
        Rq = [Res("qt") for i in range(4)]
        kt = [self.sb("kt%d" % i, [128, TW], BF16) for i in range(4)]
        Rkt = [Res("kt") for i in range(4)]
        Cf = [self.sb("Cf%d" % i, [128, 516], F32) for i in range(4)]
        RCf = [Res("Cf") for i in range(4)]
        Cb = [self.sb("Cb%d" % i, [128, 516], BF16) for i in range(4)]
        RCb = [Res("Cb") for i in range(4)]
        al0 = self.off
        stg = [self.sb("stg%d" % i, [128, 516], F32) for i in range(4)]
        Rstg = Res("stg")
        pa = self.sb("cpa", [128, 516], F32)
        pb_ = self.sb("cpb", [128, 516], F32)
        Rpa = Res("cpa")
        al1 = self.off
        self.off = al0
        sm = self.sb("sm", [128, 128], BF16)
        Rsm = Res("sm")
        dnm = self.sb("dnm", [128, 4], F32)
        Rdn = Res("dnm")
        hbv = self.sb("hbv", [128, TW], F32)
        Rhb = Res("hbv")
        st6 = self.sb("st6", [128, 6], F32)
        mvv = self.sb("mvv", [128, 2], F32)
        hnr = self.sb("hnr", [128, TW], BF16)
        Rhn = Res("hnr")
        hhT = self.sb("hhT", [128, 4, TW], BF16)
        Rhh = [Res("hhT") for i in range(4)]
        zs = self.sb("zs", [128, TW], F32)
        Rzs = Res("zs")
        t1 = self.sb("t1", [128, TW], F32)
        Rt1 = Res("t1")
        yv = [self.sb("cyv%d" % i, [128, TW], BF16) for i in range(4)]
        Ryv = [Res("cyv") for i in range(4)]
        self.off = max(self.off, al1)
        self.rot = [0, 1, 2, 3]
        for hd in range(4):
            covs = [cco2[hd * 2 + i_].ap().rearrange("(r p) n -> p r n", p=128) for i_ in range(2)]
            wz, rz, sz = self.ring.need(wzd[hd], (8, 512))
            wo, ro, so = self.ring.need(wod[hd], (4, 1024))
            self.barrier()
            for dc in range(4):
                cov = covs[dc // 2]
                P.dma("sp", [(stg[r][:], cov[:, r, (dc % 2) * 516:(dc % 2 + 1) * 516]) for r in range(4)], reads=[Rcco2[hd * 2 + dc // 2]], writes=[Rstg])
                P.op("dve", lambda e: e.scalar_tensor_tensor(out=pa[:], in0=stg[0][:], scalar=stg[1][:, 513:514], in1=stg[1][:], op0=ALU.mult, op1=ALU.add), reads=[Rstg], writes=[Rpa])
                P.op("dve", lambda e: e.scalar_tensor_tensor(out=pb_[:], in0=pa[:], scalar=stg[2][:, 513:514], in1=stg[2][:], op0=ALU.mult, op1=ALU.add), reads=[Rstg, Rpa], writes=[Rpa])
                P.op("dve", lambda e, dc=dc: e.tensor_scalar(out=Cf[dc][:], in0=stg[0][:], scalar1=sel[:, 1:2], scalar2=None, op0=ALU.mult), reads=[Rstg, Rk], writes=[RCf[dc]])
                P.op("dve", lambda e, dc=dc: e.scalar_tensor_tensor(out=Cf[dc][:], in0=pa[:], scalar=sel[:, 2:3], in1=Cf[dc][:], op0=ALU.mult, op1=ALU.add), reads=[Rpa, Rk], writes=[RCf[dc]])
                P.op("dve", lambda e, dc=dc: e.scalar_tensor_tensor(out=Cf[dc][:], in0=pb_[:], scalar=sel[:, 3:4], in1=Cf[dc][:], op0=ALU.mult, op1=ALU.add), reads=[Rpa, Rk], writes=[RCf[dc]])
                P.op("act", lambda e, dc=dc: e.copy(Cb[dc][:], Cf[dc][:]), reads=[RCf[dc]], writes=[RCb[dc]])
            self.barrier()
            def S1c(t):
                sl = slice(t * TW, (t + 1) * TW)
                for fcl in range(4):
                    fc = hd * 4 + fcl
                    fload(fc, fcl, t)
                    b = self.bank()
                    P.op("pe", lambda e, b=b, fc=fc, fcl=fcl: e.matmul(self.ps[b][:], bdq[:, fc, :], cat[fcl][:], start=True, stop=True), reads=[Rca[fcl], Rk], writes=[self.RP[b]])
                    P.op("act", lambda e, b=b, fcl=fcl: e.copy(qt[fcl][:], self.ps[b][:]), reads=[self.RP[b]], writes=[Rq[fcl]])
                    b = self.bank()
                    P.op("pe", lambda e, b=b, fc=fc, fcl=fcl: e.matmul(self.ps[b][:], bdk[:, fc, :], cat[fcl][:], start=True, stop=True), reads=[Rca[fcl], Rk], writes=[self.RP[b]])
                    P.op("act", lambda e, b=b, fcl=fcl: e.copy(kt[fcl][:], self.ps[b][:]), reads=[self.RP[b]], writes=[Rkt[fcl]])
                    kv_tok(hd, fcl, t, G["TSu"])

            def CHc(t):
                sl = slice(t * TW, (t + 1) * TW)
                pend = [None]
                for c4 in range(5):
                    if c4 < 4:
                        c = t * 4 + c4
                        cs = slice(c4 * 128, (c4 + 1) * 128)
                        b = self.bank()
                        for fcl in range(4):
                            P.op("pe", lambda e, b=b, fcl=fcl, cs=cs: e.matmul(self.ps[b][:, 0:128], kt[fcl][:, cs], qt[fcl][:, cs], start=(fcl == 0), stop=(fcl == 3)),
                                 reads=[Rkt[fcl], Rq[fcl]], writes=[self.RP[b]], inc=(fcl == 3))
                        P.op("dve", lambda e, b=b, c=c, hd=hd: e.scalar_tensor_tensor(out=sm[:], in0=self.ps[b][:, 0:128], scalar=G["TSw"][:, c, hd:hd + 1], in1=U[:], op0=ALU.mult, op1=ALU.mult),
                             reads=[self.RP[b], Rgt, Rk], writes=[Rsm])
                        b2 = 6 + (c % 2)
                        P.op("pe", lambda e, b2=b2, c4=c4: e.matmul(self.ps[b2][:], sm[:], vtok[c4][:], start=True, stop=False), reads=[Rsm, Rvt[c4]], writes=[self.RP[b2]], inc=False)
                        for fcl in range(4):
                            P.op("pe", lambda e, b2=b2, fcl=fcl, cs=cs: e.matmul(self.ps[b2][:], qt[fcl][:, cs], Cb[fcl][:, 0:512], start=False, stop=(fcl == 3)),
                                 reads=[Rq[fcl], RCb[fcl]], writes=[self.RP[b2]], inc=(fcl == 3))
                        b3 = 4 + (c % 2)
                        P.op("pe", lambda e, b3=b3: e.matmul(self.ps[b3][:, 0:1], sm[:], onec[:], start=True, stop=False), reads=[Rsm, Rk], writes=[self.RP[b3]], inc=False)
                        for fcl in range(4):
                            P.op("pe", lambda e, b3=b3, fcl=fcl, cs=cs: e.matmul(self.ps[b3][:, 0:1], qt[fcl][:, cs], Cb[fcl][:, 512:513], start=False, stop=(fcl == 3)),
                                 reads=[Rq[fcl], RCb[fcl]], writes=[self.RP[b3]], inc=(fcl == 3))
                        pass
                    if pend[0] is not None:
                        pend[0]()
                        pend[0] = None
                    hops = []
                    if c4 >= 1:
                        pc4 = c4 - 1
                        pc_ = t * 4 + pc4
                        pcs = slice(pc4 * 128, (pc4 + 1) * 128)
                        eb2 = 6 + (pc_ % 2)
                        eb3 = 4 + (pc_ % 2)
                        ea = G["TSea"][:, pc_, hd:hd + 1]
                        la = G["A"][:, pc_, hd:hd + 1]
                        hops.append(lambda eb3=eb3, ea=ea: P.op("act", lambda e: e.activation(dnm[:, 0:1], self.ps[eb3][:, 0:1], AF.Square, scale=ea),
                                                                reads=[self.RP[eb3], Rgt], writes=[Rdn]))
                        hops.append(lambda: P.op("pool", lambda e: e.tensor_scalar(out=dnm[:, 1:2], in0=dnm[:, 0:1], scalar1=1.0, scalar2=None, op0=ALU.max), reads=[Rdn], writes=[Rdn]))
                        hops.append(lambda: P.op("act", lambda e: e.activation(dnm[:, 2:3], dnm[:, 1:2], AF.Ln), reads=[Rdn], writes=[Rdn]))
                        hops.append(lambda la=la: P.op("act", lambda e: e.activation(dnm[:, 3:4], dnm[:, 2:3], AF.Exp, scale=-0.5, bias=la), reads=[Rdn, Rgt], writes=[Rdn]))
                        hops.append(lambda eb2=eb2: P.op("act", lambda e: e.activation(hbv[:], self.ps[eb2][:], AF.Identity, scale=dnm[:, 3:4]), reads=[self.RP[eb2], Rdn], writes=[Rhb]))
                        def _h():
                            P.op("dve", lambda e: e.bn_stats(st6[:], hbv[:]), reads=[Rhb], writes=[Rdn])
                        _h._dve = True
                        hops.append(_h)
                        hops.append(lambda: P.op("dve", lambda e: e.bn_aggr(mvv[:], st6[:]), reads=[Rdn], writes=[Rdn]))
                        hops.append(lambda: P.op("act", lambda e: e.activation(mvv[:, 1:2], mvv[:, 1:2], AF.Ln, bias=self.epsb[:]), reads=[Rdn, self.Rconst], writes=[Rdn]))
                        hops.append(lambda: P.op("act", lambda e: e.activation(mvv[:, 1:2], mvv[:, 1:2], AF.Exp, scale=-0.5), reads=[Rdn], writes=[Rdn]))
                        hops.append(lambda: P.op("dve", lambda e: e.tensor_scalar(out=hnr[:], in0=hbv[:], scalar1=mvv[:, 0:1], scalar2=mvv[:, 1:2], op0=ALU.subtract, op1=ALU.mult),
                                                 reads=[Rhb, Rdn], writes=[Rhn]))

                        def tr_hop(pcs=pcs):
                            b4 = self.bank()
                            psb = self.ps[b4][:].bitcast(BF16)
                            for fcl in range(4):
                                P.op("pe", lambda e, fcl=fcl, psb=psb: e.transpose(psb[:, fcl * 128:(fcl + 1) * 128], hnr[:, fcl * 128:(fcl + 1) * 128], self.ident_bf[:]),
                                     reads=[Rhn, self.Rconst], writes=[self.RP[b4]], inc=(fcl == 3))
                            for fcl in range(4):
                                P.op("act", lambda e, fcl=fcl, psb=psb, pcs=pcs: e.copy(hhT[:, fcl, pcs], psb[:, fcl * 128:(fcl + 1) * 128]), reads=[self.RP[b4]], writes=[Rhh[fcl]])
                        pend[0] = tr_hop
                    if c4 < 4:
                        dec = G["DEC"][:, c, hd:hd + 1]
                        for dc in range(4):
                            b5 = self.bank()
                            P.op("pe", lambda e, b5=b5, dc=dc, c4=c4: e.matmul(self.ps[b5][:], khat[c4][:, dc * 128:(dc + 1) * 128], vtok[c4][:], start=True, stop=True),
                                 reads=[Rkh[c4], Rvt[c4]], writes=[self.RP[b5]])
                            b6 = self.bank()
                            P.op("pe", lambda e, b6=b6, dc=dc, c4=c4: e.matmul(self.ps[b6][:, 0:1], khat[c4][:, dc * 128:(dc + 1) * 128], onec[:], start=True, stop=True),
                                 reads=[Rkh[c4], Rk], writes=[self.RP[b6]])
                            P.op("dve", lambda e, b5=b5, dc=dc, dec=dec: e.scalar_tensor_tensor(out=Cf[dc][:, 0:512], in0=Cf[dc][:, 0:512], scalar=dec, in1=self.ps[b5][:], op0=ALU.mult, op1=ALU.add),
                                 reads=[self.RP[b5], Rgt], writes=[RCf[dc]])
                            P.op("dve", lambda e, b6=b6, dc=dc, dec=dec: e.scalar_tensor_tensor(out=Cf[dc][:, 512:513], in0=Cf[dc][:, 512:513], scalar=dec, in1=self.ps[b6][:, 0:1], op0=ALU.mult, op1=ALU.add),
                                 reads=[self.RP[b6], Rgt], writes=[RCf[dc]])
                            P.op("act", lambda e, dc=dc: e.copy(Cb[dc][:, 0:513], Cf[dc][:, 0:513]), reads=[RCf[dc]], writes=[RCb[dc]])
                            for _ in range(3):
                                if hops and not getattr(hops[0], "_dve", False):
                                    hops.pop(0)()
                    while hops:
                        hops.pop(0)()
                if pend[0] is not None:
                    pend[0]()
                    pend[0] = None
            def OUTc(t):
                sl = slice(t * TW, (t + 1) * TW)
                for fcl in range(4):
                    fc = hd * 4 + fcl
                    b = self.bank()
                    for k in range(8):
                        P.op("pe", lambda e, b=b, k=k, fcl=fcl, sl=sl, wz=wz: e.matmul(self.ps[b][:], wz[:, k, fcl * 128:(fcl + 1) * 128], hn[:, k, sl], start=(k == 0), stop=(k == 7)),
                             reads=[rz, self.RN[k][t]], writes=[self.RP[b]], inc=(k == 7))
                    P.op("act", lambda e, b=b: e.activation(zs[:], self.ps[b][:], AF.Silu), reads=[self.RP[b]], writes=[Rzs])
                    tb, Rtb = (t1, Rt1) if fcl % 2 == 0 else (hbv, Rhb)
                    P.dma("pool", [(tb[:], dca.ap()[fc, :, t, :])], reads=[Rd[fc]], writes=[Rtb])
                    P.op("dve", lambda e, tb=tb, fc=fc: e.tensor_scalar(out=tb[:], in0=tb[:], scalar1=csk[:, fc:fc + 1], scalar2=None, op0=ALU.mult),
                         reads=[Rk], writes=[Rtb])
                    P.op("dve", lambda e, tb=tb, fcl=fcl, fc=fc: e.scalar_tensor_tensor(out=tb[:], in0=hhT[:, fcl, :], scalar=cng[:, fc:fc + 1], in1=tb[:], op0=ALU.mult, op1=ALU.add),
                         reads=[Rhh[fcl], Rk], writes=[Rtb])
                    P.op("pool", lambda e, tb=tb, fcl=fcl: e.tensor_tensor(out=yv[fcl][:], in0=tb[:], in1=zs[:], op=ALU.mult), reads=[Rtb, Rzs], writes=[Ryv[fcl]])
                for mo in range(8):
                    b = self.bank()
                    for k in range(4):
                        P.op("pe", lambda e, b=b, k=k, mo=mo, wo=wo: e.matmul(self.ps[b][:], wo[:, k, mo * 128:(mo + 1) * 128], yv[k][:], start=(k == 0), stop=(k == 3)),
                             reads=[ro, Ryv[k]], writes=[self.RP[b]], inc=(k == 3))
                    P.op("dve", lambda e, b=b, mo=mo, sl=sl: e.tensor_tensor(out=hT[:, mo, sl], in0=self.ps[b][:], in1=hT[:, mo, sl], op=ALU.add),
                         reads=[self.RP[b], self.RH[mo][t]], writes=[self.RH[mo][t]])
            S1c(0)
            for t in range(NT):
                CHc(t)
                if t + 1 < NT:
                    S1c(t + 1)
                OUTc(t)
            self.ring.release(sz)
            self.ring.release(so)
        self.rot = list(range(8))
        self.barrier()
        self.off = base

    def dout(self):
        if "yT" not in self.dram:
            self.dram["yT"] = self.nc.dram_tensor("yT", [1024, TOK], F32, kind="ExternalOutput").ap()
        return self.dram["yT"]

    def final(self):
        self.rmsnorm(8, out_final=self.dout())

    def store_h(self):
        yT = self.dout()
        ov = yT.rearrange("(c p) t -> p c t", p=128)
        for t in range(NT):
            sl = slice(t * TW, (t + 1) * TW)
            self.P.dma("sp", [(ov[:, :, sl], self.hT[:, :, sl])], reads=[self.RH[c][t] for c in range(8)], is_output=True)

    def run_stages(self):
        for s in self.stages:
            if s[0] == "F":
                self.ffn(int(s[1:]))
            elif s[0] == "A":
                self.mixer_a(int(s[1]), int(s[2]))
            elif s[0] == "B":
                self.mixer_b(int(s[1]))
            elif s[0] == "C":
                self.mixer_c(int(s[1]))
            elif s == "N":
                self.final()
            elif s == "S":
                self.store_h()
            else:
                raise ValueError(s)

    def build(self):
        nc, P = self.nc, self.P
        with ExitStack() as st:
            self.setup(st)
            self.epsb = self.sb("epsb", [128, 1], F32)
            self.oneb = self.sb("oneb", [128, 1], F32)
            self.arena0 = self.off
            P.dry = True
            self.run_stages()
            P.dry = False
            self.off = self.arena0
            self.rp = 0
            self.ring.reset()
            self.load_consts()
            P.op("pool", lambda e: e.memset(self.epsb[:], EPS), writes=[self.Rconst])
            P.op("pool", lambda e: e.memset(self.oneb[:], 1.0), writes=[self.Rconst])
            self.load_x()
            self.run_stages()
            P.finish("sp")
            P.emit()
        return nc


def _common_inputs(inp):
    f = lambda a: np.ascontiguousarray(np.asarray(a, dtype=np.float32))
    d = {}
    d["ident"] = np.eye(128, dtype=np.float32)
    g = np.concatenate([inp["mix_norm_g"], inp["ffn_norm_g"], inp["final_norm_g"][None]], axis=0)
    d["gains"] = f(g.reshape(9, 8, 128).transpose(2, 0, 1))
    d["cmask"] = np.triu(np.ones((128, 128), np.float32))
    for j in range(2):
        wi = inp["a_w_in"][j]
        d["a_wu_%d" % j] = f(wi[:, :2048].reshape(8, 128, 4, 512).transpose(2, 1, 0, 3))
        d["a_wv_%d" % j] = f(wi[:, 2048:].reshape(8, 128, 4, 512).transpose(2, 1, 0, 3))
        d["a_wo_%d" % j] = f(inp["a_w_out"][j].reshape(4, 4, 128, 1024).transpose(0, 2, 1, 3))
        d["a_wsT_%d" % j] = f(inp["a_ws"][j].transpose(2, 0, 1))
        d["a_bs_%d" % j] = f(inp["a_bs"][j].reshape(1, 2048))
        d["a_lng_%d" % j] = f(inp["a_ln_g"][j].reshape(16, 128).T)
        d["a_lnb_%d" % j] = f(inp["a_ln_b"][j].reshape(16, 128).T)
    bw = inp["b_w_in"][0]
    d["b_win"] = f(bw.reshape(8, 128, 4, 8, 128).transpose(3, 1, 0, 2, 4).reshape(8, 128, 8, 512))
    d["b_wout"] = f(inp["b_w_out"][0].reshape(2, 4, 128, 1024).transpose(0, 2, 1, 3))
    d["b_lbl"] = f(inp["hgrn_lb_logits"].reshape(4, 8, 128).transpose(2, 0, 1))
    d["b_ng"] = f(inp["b_norm_g"][0].reshape(8, 128).T)
    ii = np.arange(128)
    bm = ((ii[:, None] <= ii[None, :]) & ((ii[:, None] // 64) == (ii[None, :] // 64))).astype(np.float32)
    d["bmask4"] = f(np.tile(bm, (1, 4)))
    cwi = inp["c_w_in"][0]
    d["c_wxm"] = f(cwi[:, :2048].reshape(8, 128, 4, 512).transpose(2, 1, 0, 3))
    d["c_wz"] = f(cwi[:, 2048:].reshape(8, 128, 4, 512).transpose(2, 1, 0, 3))
    d["c_wo"] = f(inp["c_w_out"][0].reshape(4, 4, 128, 1024).transpose(0, 2, 1, 3))
    d["c_cw"] = f(inp["c_conv_w"][0].reshape(4, 16, 128).transpose(2, 1, 0))
    d["c_cb"] = f(inp["c_conv_b"][0].reshape(16, 128).T)
    d["c_ng"] = f(inp["c_norm_g"][0].reshape(16, 128).T)
    d["c_skip"] = f(inp["c_skip"][0].reshape(16, 128).T)
    d["c_bg"] = f(inp["c_b_gate"][0].reshape(1, 8))
    d["c_wg"] = f(inp["c_w_gate"][0].reshape(48, 128, 8).transpose(1, 0, 2))
    e4 = np.zeros((128, 4, 4), np.float32)
    for j in range(4):
        e4[:, j, j] = 1.0
    d["c_E4"] = e4
    for nm, key in [("q", "c_wq"), ("k", "c_wk"), ("v", "c_wv")]:
        wb = inp[key][0]
        bd = np.zeros((16, 32, 4, 32, 4), np.float32)
        wr = wb.reshape(16, 32, 4, 4)
        for n_ in range(32):
            bd[:, n_, :, n_, :] = wr[:, n_]
        bd = bd.reshape(16, 128, 128)
        d["c_bd%s" % nm] = f(bd.transpose(1, 0, 2))
        d["c_bd%sT" % nm] = f(bd.transpose(2, 0, 1))
    for l in range(4):
        w1 = inp["ffn_w1"][l]
        d["w1_%d" % l] = f(w1.reshape(8, 128, 8, 512).transpose(2, 1, 0, 3))
        w2 = inp["ffn_w2"][l]
        d["w2_%d" % l] = f(w2.reshape(8, 4, 128, 1024).transpose(0, 2, 1, 3))
    return d


def _run(stages, inp, hin):
    bld = Builder(stages)
    nc = bld.build()
    com = _common_inputs(inp)
    used = set(bld.dram.keys()) - set(getattr(bld, 'dbgnames', [])) - {'yT', 'b_ccin', 'b_ccout', 'c_ccin', 'c_ccout', 'c_ccin2', 'c_ccout2', 'c_dxm', 'c_dca', 'a_dgv', 'b_dsg', 'b_dit'} - {'c_ccin2_%d' % i for i in range(8)} - {'c_ccout2_%d' % i for i in range(8)}
    in_maps = []
    for c in range(8):
        b, sg = divmod(c, 4)
        m = {k: v for k, v in com.items() if k in used}
        m["xT"] = np.ascontiguousarray(hin[b, sg * TOK:(sg + 1) * TOK, :].T)
        if "sel" in used:
            selv = np.zeros((128, 4), np.float32)
            selv[:, sg] = 1.0
            m["sel"] = selv
        if "selp" in used:
            selv = np.zeros((128, 4), np.float32)
            if sg > 0:
                selv[:, sg - 1] = 1.0
            m["selp"] = selv
        in_maps.append(m)
    import os
    if os.environ.get("KTRACE"):
        res = run_bass_kernel_spmd(nc, in_maps, core_ids=list(range(8)), trace=True)
        print("EXEC_NS", res.exec_time_ns)
    else:
        res = run_bass_kernel_spmd(nc, in_maps, core_ids=list(range(8)))
    _run.dbg = [{n: res.results[c][n] for n in getattr(bld, "dbgnames", [])} for c in range(8)]
    out = np.empty((2, 8192, 1024), np.float32)
    for c in range(8):
        b, sg = divmod(c, 4)
        out[b, sg * TOK:(sg + 1) * TOK, :] = res.results[c]["yT"].T
    return out


def kernel(**inputs):
    inp = {k: np.asarray(v) for k, v in inputs.items()}
    return _run(["A00", "F0", "B1", "F1", "C2", "F2", "A31", "F3", "N"], inp, inp["x"])
```
